# Optimizing a Trainium2 kernel written in Bass

```python
import jax, jax.numpy as jnp
from jax import lax
import numpy as np

D_MODEL = 2048
BATCH = 2
SEQ = 4096
DEPTH = 2

CONV_CH = 1024
CONV_GROUPS = 8
CONV_K = 31
MLA_HEADS = 8
QK_NOPE = 128
QK_ROPE = 64
V_HEAD = 128
Q_RANK = 512
KV_RANK = 512
ROPE_BASE = 10000.0
Q_BLOCK = 128
MLA_WIDTH = MLA_HEADS * V_HEAD
HGRN_HEADS = 16
HGRN_EXPAND = 128
HGRN_FDIM = HGRN_HEADS * HGRN_EXPAND
HGRN_HEAD_V = D_MODEL // HGRN_HEADS
HGRN_CHUNK = 64
EPS = 1e-6

N_EVEN = (DEPTH + 1) // 2
N_ODD = DEPTH // 2
EVEN_MIX = CONV_CH + MLA_WIDTH
EVEN_IN = 3 * CONV_CH + Q_RANK + KV_RANK + QK_ROPE + MLA_WIDTH
ODD_IN = 2 * HGRN_FDIM + 2 * D_MODEL

kernel_name = "hybrid_conv_mla_hgrn2_gated"


def _split(p, sizes):
    out, off = [], 0
    for s in sizes:
        out.append(p[..., off:off + s])
        off += s
    return out


def rmsnorm(x, g):
    xf = x.astype(jnp.float32)
    y = xf * lax.rsqrt(jnp.mean(xf * xf, axis=-1, keepdims=True) + EPS)
    return (y * g.astype(jnp.float32)).astype(x.dtype)


def rope_tables(seq):
    inv_freq = 1.0 / (ROPE_BASE ** (jnp.arange(0, QK_ROPE, 2, dtype=jnp.float32) / QK_ROPE))
    ang = jnp.arange(seq, dtype=jnp.float32)[:, None] * inv_freq[None, :]
    return jnp.cos(ang), jnp.sin(ang)


def apply_rope(x, cos, sin):
    x1, x2 = jnp.split(x.astype(jnp.float32), 2, axis=-1)
    c = cos[None, :, None, :]
    s = sin[None, :, None, :]
    return jnp.concatenate([x1 * c - x2 * s, x1 * s + x2 * c], axis=-1).astype(x.dtype)


def conformer_conv(v, g_glu, z, w_dw, b_dw, ln_g, ln_b):
    u = v * jax.nn.sigmoid(g_glu)
    u = jnp.pad(u, ((0, 0), (CONV_K - 1, 0), (0, 0)))
    y = lax.conv_general_dilated(u, w_dw[:, None, :].astype(u.dtype), window_strides=(1,),
                                 padding='VALID', dimension_numbers=('NWC', 'WIO', 'NWC'),
                                 feature_group_count=CONV_CH) + b_dw
    B, S, C = y.shape
    yf = y.astype(jnp.float32).reshape(B, S, CONV_GROUPS, C // CONV_GROUPS)
    mu = jnp.mean(yf, axis=-1, keepdims=True)
    var = jnp.mean(jnp.square(yf - mu), axis=-1, keepdims=True)
    yf = ((yf - mu) * lax.rsqrt(var + EPS)).reshape(B, S, C)
    y = (yf * ln_g.astype(jnp.float32) + ln_b.astype(jnp.float32)).astype(v.dtype)
    return jax.nn.silu(y) * jax.nn.silu(z)


def mla(c_q, c_kv, k_pe, q_norm_g, w_uq, kv_norm_g, w_ukv, cos, sin):
    B, S, _ = c_q.shape
    q = jnp.einsum('bsr,rhd->bshd', rmsnorm(c_q, q_norm_g), w_uq)
    q_nope, q_pe = q[..., :QK_NOPE], apply_rope(q[..., QK_NOPE:], cos, sin)
    kv = jnp.einsum('bsr,rhd->bshd', rmsnorm(c_kv, kv_norm_g), w_ukv)
    k_nope, v = kv[..., :QK_NOPE], kv[..., QK_NOPE:]
    k_pe = apply_rope(k_pe[:, :, None, :], cos, sin)[:, :, 0, :]
    scale = (QK_NOPE + QK_ROPE) ** -0.5
    n_blk = S // Q_BLOCK
    qn_b = q_nope.reshape(B, n_blk, Q_BLOCK, MLA_HEADS, QK_NOPE).transpose(1, 0, 2, 3, 4)
    qp_b = q_pe.reshape(B, n_blk, Q_BLOCK, MLA_HEADS, QK_ROPE).transpose(1, 0, 2, 3, 4)
    kpos = jnp.arange(S)

    def block(args):
        qn, qp, i = args
        s = (jnp.einsum('bqhd,bkhd->bhqk', qn, k_nope) +
             jnp.einsum('bqhr,bkr->bhqk', qp, k_pe)).astype(jnp.float32) * scale
        qpos = i * Q_BLOCK + jnp.arange(Q_BLOCK)
        mask = kpos[None, :] <= qpos[:, None]
        s = jnp.where(mask[None, None], s, -jnp.inf)
        p = jax.nn.softmax(s, axis=-1).astype(v.dtype)
        return jnp.einsum('bhqk,bkhd->bqhd', p, v)

    o = lax.map(block, (qn_b, qp_b, jnp.arange(n_blk)))
    return o.transpose(1, 0, 2, 3, 4).reshape(B, S, MLA_WIDTH)


def hgrn2(q, f_raw, i_val, lb, norm_g):
    B, S, _ = q.shape
    H, K, V, C = HGRN_HEADS, HGRN_EXPAND, HGRN_HEAD_V, HGRN_CHUNK
    nC = S // C
    logf = jnp.logaddexp(jnp.log(lb), jnp.log1p(-lb) + jax.nn.log_sigmoid(f_raw.astype(jnp.float32)))
    k = -jnp.expm1(logf)

    def chunks(t, d):
        return t.astype(jnp.float32).reshape(B, nC, C, H, d).transpose(1, 0, 3, 2, 4)

    qc, kc, gc, vc = chunks(q, K), chunks(k, K), chunks(logf, K), chunks(i_val, V)
    tri = jnp.tril(jnp.ones((C, C), dtype=bool))

    def step(state, inp):
        qq, kk, gg, vv = inp
        b = jnp.cumsum(gg, axis=2)
        diff = b[:, :, :, None, :] - b[:, :, None, :, :]
        decay = jnp.exp(jnp.where(tri[None, None, :, :, None], diff, -jnp.inf))
        A = jnp.einsum('bhtk,bhsk,bhtsk->bhts', qq, kk, decay)
        o = (jnp.einsum('bhts,bhsv->bhtv', A, vv) +
             jnp.einsum('bhtk,bhkv->bhtv', qq * jnp.exp(b), state))
        b_last = b[:, :, -1:, :]
        new_state = (jnp.exp(b_last)[:, :, 0, :, None] * state +
                     jnp.einsum('bhsk,bhsv->bhkv', kk * jnp.exp(b_last - b), vv))
        return new_state, o

    state0 = jnp.zeros((B, H, K, V), jnp.float32)
    _, o = lax.scan(step, state0, (qc, kc, gc, vc))
    o = o.transpose(1, 0, 3, 2, 4).reshape(B, S, H, V).astype(q.dtype)
    return rmsnorm(o, norm_g).reshape(B, S, H * V)


def setup_inputs(seed: int = 0) -> dict:
    key = jax.random.key(seed)
    ks = jax.random.split(key, 18)

    def nrm(k, shape, scale):
        return jax.random.normal(k, shape, jnp.float32) * scale

    def gain(k, shape):
        return 1.0 + 0.02 * jax.random.normal(k, shape, jnp.float32)

    return {
        "x": nrm(ks[0], (BATCH, SEQ, D_MODEL), 1.0),
        "ev_norm_g": gain(ks[1], (N_EVEN, D_MODEL)),
        "ev_w_in": nrm(ks[2], (N_EVEN, D_MODEL, EVEN_IN), D_MODEL ** -0.5),
        "conv_w": nrm(ks[3], (N_EVEN, CONV_K, CONV_CH), CONV_K ** -0.5),
        "conv_b": nrm(ks[4], (N_EVEN, CONV_CH), 0.02),
        "conv_ln_g": gain(ks[5], (N_EVEN, CONV_CH)),
        "conv_ln_b": nrm(ks[6], (N_EVEN, CONV_CH), 0.02),
        "mla_q_norm_g": gain(ks[7], (N_EVEN, Q_RANK)),
        "mla_w_uq": nrm(ks[8], (N_EVEN, Q_RANK, MLA_HEADS, QK_NOPE + QK_ROPE), Q_RANK ** -0.5),
        "mla_kv_norm_g": gain(ks[9], (N_EVEN, KV_RANK)),
        "mla_w_ukv": nrm(ks[10], (N_EVEN, KV_RANK, MLA_HEADS, QK_NOPE + V_HEAD), KV_RANK ** -0.5),
        "ev_w_out": nrm(ks[11], (N_EVEN, EVEN_MIX, D_MODEL), EVEN_MIX ** -0.5),
        "od_norm_g": gain(ks[12], (N_ODD, D_MODEL)),
        "od_w_in": nrm(ks[13], (N_ODD, D_MODEL, ODD_IN), D_MODEL ** -0.5),
        "hgrn_lb_logits": nrm(ks[14], (DEPTH, HGRN_FDIM), 0.5),
        "hgrn_norm_g": gain(ks[15], (N_ODD, HGRN_HEAD_V)),
        "od_w_out": nrm(ks[16], (N_ODD, D_MODEL, D_MODEL), D_MODEL ** -0.5),
        "final_norm_g": gain(ks[17], (D_MODEL,)),
    }


def reference(x, ev_norm_g, ev_w_in, conv_w, conv_b, conv_ln_g, conv_ln_b, mla_q_norm_g, mla_w_uq,
              mla_kv_norm_g, mla_w_ukv, ev_w_out, od_norm_g, od_w_in, hgrn_lb_logits, hgrn_norm_g,
              od_w_out, final_norm_g):
    S = x.shape[1]
    cos, sin = rope_tables(S)
    lb_all = jnp.cumsum(jax.nn.softmax(hgrn_lb_logits.astype(jnp.float32), axis=0), axis=0)
    lb_all = lb_all - lb_all[0:1]
    for l in range(DEPTH):
        if l % 2 == 0:
            j = l // 2
            h = rmsnorm(x, ev_norm_g[j])
            p = h @ ev_w_in[j]
            a_v, a_g, a_z, c_q, c_kv, k_pe, b_z = _split(
                p, [CONV_CH, CONV_CH, CONV_CH, Q_RANK, KV_RANK, QK_ROPE, MLA_WIDTH])
            a_out = conformer_conv(a_v, a_g, a_z, conv_w[j], conv_b[j], conv_ln_g[j], conv_ln_b[j])
            b_out = mla(c_q, c_kv, k_pe, mla_q_norm_g[j], mla_w_uq[j], mla_kv_norm_g[j],
                        mla_w_ukv[j], cos, sin) * jax.nn.silu(b_z)
            x = x + jnp.concatenate([a_out, b_out], axis=-1) @ ev_w_out[j]
        else:
            j = l // 2
            h = rmsnorm(x, od_norm_g[j])
            p = h @ od_w_in[j]
            q, f_raw, i_val, g = _split(p, [HGRN_FDIM, HGRN_FDIM, D_MODEL, D_MODEL])
            o = hgrn2(q, f_raw, i_val, lb_all[l], hgrn_norm_g[j]) * jax.nn.silu(g)
            x = x + o @ od_w_out[j]
    return rmsnorm(x, final_norm_g)
```

```python
import numpy as np
import ml_dtypes
from contextlib import ExitStack
import concourse.bass as bass
import concourse.mybir as mybir
from concourse.bass_utils import run_bass_kernel_spmd

F32 = mybir.dt.float32
BF16 = mybir.dt.bfloat16
AF = mybir.ActivationFunctionType
ALU = mybir.AluOpType
ENGS = ("pe", "act", "dve", "pool", "sp")
T = 1024
HALO = 32
EPS = 1e-6
NEG = -30000.0


class Op:
    __slots__ = ("idx", "eng", "fn", "deps", "dma", "need_inc", "sem", "val", "extra_waits", "cc")

    def __init__(self, idx, eng, fn, deps, dma):
        self.idx = idx
        self.eng = eng
        self.fn = fn
        self.deps = deps
        self.dma = dma
        self.need_inc = False
        self.sem = None
        self.val = 0
        self.extra_waits = []
        self.cc = False


class Prog:
    def __init__(self, nc, n_dma_sems=8):
        self.nc = nc
        self.ops = []
        self.last_w = {}
        self.readers = {}
        self.barrier_deps = []
        self.last_on_eng = {}
        self.dmas_since_barrier = []
        self.n_dma_sems = n_dma_sems

    muted = False

    def op(self, eng, fn, reads=(), writes=(), dma=False):
        if self.muted:
            return -1
        idx = len(self.ops)
        deps = set(self.barrier_deps)
        for k in reads:
            w = self.last_w.get(k)
            if w is not None:
                deps.add(w)
        for k in writes:
            w = self.last_w.get(k)
            if w is not None:
                deps.add(w)
            for r in self.readers.get(k, ()):
                deps.add(r)
        o = Op(idx, eng, fn, deps, dma)
        self.ops.append(o)
        for k in writes:
            self.last_w[k] = idx
            self.readers[k] = []
        for k in reads:
            if k not in writes:
                self.readers.setdefault(k, []).append(idx)
        self.last_on_eng[eng] = idx
        if dma:
            self.dmas_since_barrier.append(idx)
        return idx

    def barrier(self):
        deps = set(self.last_on_eng.values()) | set(self.dmas_since_barrier)
        deps = {d for d in deps if not self.ops[d].cc}
        self.barrier_deps = sorted(deps)
        self.dmas_since_barrier = []
        keep = {k: w for k, w in self.last_w.items() if self.ops[w].cc}
        self.last_w = keep
        self.readers = {k: [] for k in keep}

    def dma(self, q, out, in_, reads=(), writes=(), **kw):
        return self.op(q, lambda e: e.dma_start(out=out, in_=in_, **kw), reads, writes, dma=True)

    def collective(self, fn, reads=(), writes=()):
        i = self.op("pool", fn, reads, writes, dma=True)
        self.ops[i].cc = True
        return i

    def emit(self, stack):
        nc = self.nc
        ops = self.ops
        for o in ops:
            for d in o.deps:
                od = ops[d]
                if od.dma:
                    od.need_inc = True
                elif od.eng == "pe" and o.eng == "pe" and not o.dma:
                    continue
                else:
                    od.need_inc = True
        esem = {e: stack.enter_context(nc.semaphore("s_" + e)) for e in ENGS}
        dsem = {}
        for q in ("sp", "act", "pool"):
            dsem[q] = [stack.enter_context(nc.semaphore("d_%s%d" % (q, i))) for i in range(self.n_dma_sems)]
        ecount = {e: 0 for e in ENGS}
        dcount = {q: [0] * self.n_dma_sems for q in dsem}
        drr = {q: 0 for q in dsem}
        for o in ops:
            if o.cc:
                o.sem = stack.enter_context(nc.semaphore("cc%d" % o.idx))
                o.val = 1
            elif o.dma:
                q = o.eng
                i = drr[q]
                drr[q] = (i + 1) % self.n_dma_sems
                prev = dcount[q][i]
                if prev > 0:
                    o.extra_waits.append((dsem[q][i], prev))
                dcount[q][i] = prev + 16
                o.sem = dsem[q][i]
                o.val = prev + 16
            elif o.need_inc:
                ecount[o.eng] += 1
                o.sem = esem[o.eng]
                o.val = ecount[o.eng]
        engobj = {"pe": "tensor", "act": "scalar", "dve": "vector", "pool": "gpsimd", "sp": "sync"}
        block = stack.enter_context(nc.Block())

        def make(engname):
            def body(eng):
                known = {}
                for o in ops:
                    if o.eng != engname:
                        continue
                    waits = list(o.extra_waits)
                    for d in sorted(o.deps):
                        od = ops[d]
                        if od.sem is None:
                            continue
                        if (not od.dma) and od.eng == "pe" and engname == "pe" and not o.dma:
                            continue
                        waits.append((od.sem, od.val))
                    best = {}
                    for s, v in waits:
                        key = id(s)
                        if key not in best or best[key][1] < v:
                            best[key] = (s, v)
                    for key, (s, v) in best.items():
                        if known.get(key, 0) >= v:
                            continue
                        eng.wait_ge(s, v)
                        known[key] = v
                    ins = o.fn(eng)
                    if o.sem is not None and ins is not None:
                        ins.then_inc(o.sem, 1 if (o.cc or not o.dma) else 16)
            return body

        for engname in ENGS:
            getattr(block, engobj[engname])(make(engname))


PRM = {}
_o = 0
for _n, _w in (("ev_g", 16), ("conv_b", 8), ("ln_g", 8), ("ln_b", 8), ("q_g", 4), ("kv_g", 4),
               ("od_g", 16), ("fin_g", 16), ("hg", 1), ("l0", 16), ("l1", 16), ("conv_w", 8 * 31),
               ("sel", 4), ("vis", 3)):
    PRM[_n] = (_o, _w)
    _o += _w
NPRM = _o

N_IN0 = 42
KVR = 1024 + 64 + 1024
CH = 64
NCH = T // CH
SW = 136


def build(mode, l1=True):
    import os
    CUT = os.environ.get("MK_CUT", "")
    nc = bass.Bass("TRN2", target_bir_lowering=False)
    L0 = mode in ("A", "B", "F")
    P1 = (mode in ("B", "F") and l1) or mode == "P"
    P2 = mode in ("C", "F")
    NH = int(os.environ.get("MK_HEADS", "16"))

    def decl(name, shape, dt, role):
        if role == "in":
            return nc.dram_tensor(name, shape, dt, kind="ExternalInput").ap()
        if role == "out":
            return nc.dram_tensor(name, shape, dt, kind="ExternalOutput").ap()
        return nc.dram_tensor(name, shape, dt).ap()

    prm_d = decl("prm", [128, NPRM], F32, "in")
    cbf_d = decl("cbf", [128, 3, 128], BF16, "in")
    if L0:
        xT = decl("xT", [128, 16, HALO + T], F32, "in")
        rope_d = decl("rope", [64, 2, T], F32, "in")
        w_in0 = decl("w_in0", [6 if mode == "A" else N_IN0, 128, 2048], F32, "in")
        w_ukvk = decl("w_ukvk", [8, 128, 4 * 128], F32, "in")
        w_ukvv = decl("w_ukvv", [2, 128, 4 * 512], F32, "in")
        if mode != "F":
            kvb_a = decl("kvb", [KVR, T], BF16, "out" if mode == "A" else "int")
    if mode in ("B", "F"):
        w_uq = decl("w_uq", [8, 128, 4 * 256], F32, "in")
        w_out0 = decl("w_out0", [16, 128, 2048], F32, "in")
        if mode == "B":
            kvg_a = decl("kvg", [4 * KVR, T], BF16, "in")
    if mode != "A":
        x1d = decl("x1d", [128, 16, T], F32, {"B": "out", "C": "in", "F": "int", "P": "in"}[mode])
    if P1 or P2:
        w_in1 = decl("w_in1", [64, 128, 2048], F32, "in")
    if P1:
        if mode != "F":
            stb_a = decl("stb", [2048, SW], F32, "out")
    if P2:
        if mode == "C":
            stg_a = decl("stg", [4 * 2048, SW], F32, "in")
        w_out1 = decl("w_out1", [16, 128, 2048], F32, "in")
        x2d = decl("x2d", [128, 16, T], F32, "int")
        outT = decl("outT", [128, 16, T], F32, "out")

    RG = [[0, 1, 2, 3], [4, 5, 6, 7]]
    if mode == "F":
        kb = [decl("kb%d" % i, [256, T], BF16, "int") for i in range(4)]
        kg = [decl("kg%d" % i, [1024, T], BF16, "int") for i in range(4)]
        kpb = decl("kpb", [64, T], BF16, "int")
        kpgd = decl("kpgd", [256, T], BF16, "int")
        vb = [decl("vb%d" % i, [256, T], BF16, "int") for i in range(4)]
        vgd = [decl("vgd%d" % i, [1024, T], BF16, "int") for i in range(4)]
        sbp = [decl("sbp%d" % i, [512, SW], F32, "int") for i in range(4)]
        sgp = [decl("sgp%d" % i, [2048, SW], F32, "int") for i in range(4)]

    st = ExitStack()
    with st:
        def AG(in_ap, out_ap, rkey, wkey):
            p.collective(lambda e: e.collective_compute("AllGather", ALU.bypass, replica_groups=RG,
                                                        ins=[in_ap.opt()], outs=[out_ap.opt()]),
                         reads=[rkey], writes=[wkey])

        nsfx = [""]

        def sb(name, shape, dt=F32, stack=None):
            return (stack or st).enter_context(nc.sbuf_tensor("sb_" + name + nsfx[0], shape, dt))

        p = Prog(nc)
        ps = [st.enter_context(nc.psum_tensor("ps%d" % i, [128, 512], F32)) for i in range(7)]
        psb = st.enter_context(nc.psum_tensor("psb", [128, 8, 128], BF16))
        PK = [("ps", i) for i in range(7)]
        psv = ps[6][:, :].bitcast(BF16)

        def MM(out, lhsT, rhs, start, stop, reads, writes):
            p.op("pe", lambda e: e.matmul(out, lhsT=lhsT, rhs=rhs, start=start, stop=stop), reads, writes)

        def TR(out, in_, reads, writes):
            p.op("pe", lambda e: e.transpose(out, in_, ident), reads + ["cbf"], writes)

        def ACT(out, in_, func, reads, writes, scale=1.0, bias=None):
            if bias is None:
                p.op("act", lambda e: e.activation(out=out, in_=in_, func=func, scale=scale), reads, writes)
            else:
                p.op("act", lambda e: e.activation(out=out, in_=in_, func=func, scale=scale, bias=bias), reads, writes)

        def TT(eng, out, in0, in1, op, reads, writes):
            p.op(eng, lambda e: e.tensor_tensor(out=out, in0=in0, in1=in1, op=op), reads, writes)

        def TS(eng, out, in0, s1, s2, op0, op1, reads, writes):
            if s2 is None:
                p.op(eng, lambda e: e.tensor_scalar(out=out, in0=in0, scalar1=s1, scalar2=None, op0=op0), reads, writes)
            else:
                p.op(eng, lambda e: e.tensor_scalar(out=out, in0=in0, scalar1=s1, scalar2=s2, op0=op0, op1=op1), reads, writes)

        def STT(out, in0, scalar, in1, op0, op1, reads, writes):
            p.op("dve", lambda e: e.scalar_tensor_tensor(out=out, in0=in0, scalar=scalar, in1=in1, op0=op0, op1=op1),
                 reads, writes)

        def CP(eng, out, in_, reads, writes):
            if eng == "act":
                p.op("act", lambda e: e.activation(out=out, in_=in_, func=AF.Identity), reads, writes)
            else:
                p.op(eng, lambda e: e.tensor_copy(out=out, in_=in_), reads, writes)

        def RECIP(out, in_, reads, writes):
            p.op("dve", lambda e: e.reciprocal(out=out, in_=in_), reads, writes)

        def MEMSET(eng, ap, val, writes):
            p.op(eng, lambda e: e.memset(ap, val), (), writes)

        prm = sb("prm", [128, NPRM])
        cbf = sb("cbf", [128, 3, 128], BF16)
        epsc = sb("epsc", [128, 1])
        lbc = sb("lbc", [128, 16])
        omlb = sb("omlb", [128, 16])
        p.dma("sp", prm[:], prm_d, writes=["prm"])
        p.dma("sp", cbf[:], cbf_d, writes=["cbf"])
        MEMSET("dve", epsc[:], EPS, ["epsc"])
        ident = cbf[:, 0, :]
        ones = cbf[:, 1, :]
        triu = cbf[:, 2, :]

        def pc(name, j=0, n=1):
            o, w = PRM[name]
            return prm[:, o + j:o + j + n]

        TT("dve", lbc[:], pc("l1", 0, 16), pc("l0", 0, 16), ALU.subtract, ["prm"], ["lbc"])
        ACT(lbc[:], lbc[:], AF.Sigmoid, ["lbc"], ["lbc"])
        TS("dve", omlb[:], lbc[:], -1.0, 1.0, ALU.mult, ALU.add, ["lbc"], ["omlb"])

        wst = [sb("wst%d" % i, [128, 2048]) for i in range(2)]
        wbf = [sb("wbf%d" % i, [128, 2048], BF16) for i in range(2)]
        wctr = [0]

        def wload(src, n=2048):
            i = wctr[0] % 2
            wctr[0] += 1
            p.dma("sp", wst[i][:, 0:n], src, writes=[("wst", i)])
            CP("pool", wbf[i][:, 0:n], wst[i][:, 0:n], [("wst", i)], [("wbf", i)])
            return wbf[i], ("wbf", i)

        mix = sb("mix", [128, 16, T], BF16)
        rstd = sb("rstd", [128, T])
        sqb = sb("sqb", [128, 512], BF16)
        xs = [sb("xs%d" % i, [128, T]) for i in range(2)]

        def rms_stream(src_ap, gname, ncol, col0, sink):
            nh = [(a, min(512, ncol - a)) for a in range(0, ncol, 512)]
            for (a, n) in nh:
                for c in range(16):
                    xb = xs[c % 2]
                    p.dma("sp", xb[:, 0:n], src_ap[:, c, col0 + a:col0 + a + n], writes=[("xs", c % 2)])
                    ACT(sqb[:, 0:n], xb[:, 0:n], AF.Square, [("xs", c % 2)], ["sqb"])
                    MM(ps[0][:, 0:n], ones, sqb[:, 0:n], c == 0, c == 15, ["sqb", "cbf"], [PK[0]])
                ACT(rstd[:, a:a + n], ps[0][:, 0:n], AF.Sqrt, [PK[0], "epsc"], ["rstd"], scale=1.0 / 2048, bias=epsc[:])
                RECIP(rstd[:, a:a + n], rstd[:, a:a + n], ["rstd"], ["rstd"])
                for c in range(16):
                    xb = xs[c % 2]
                    p.dma("sp", xb[:, 0:n], src_ap[:, c, col0 + a:col0 + a + n], writes=[("xs", c % 2)])
                    sink(c, a, n, xb, ("xs", c % 2), gname)

        def lin_tile(wb, wkey, kc, m, rhs_fn, rkeys, a, n, pst, pkey):
            for c in range(kc):
                MM(pst[0:m, 0:n], wb[:, c * 128:c * 128 + m], rhs_fn(c, a, n), c == 0, c == kc - 1,
                   [wkey] + rkeys, [pkey])

        def out_proj(w_d, src_ap, col0, dst_ap):
            for m in range(16):
                wb, wk = wload(w_d[m])
                xb = xs[m % 2]
                p.dma("sp", xb[:, :], src_ap[:, m, col0:col0 + T], writes=[("xs", m % 2)])
                for half in range(2):
                    pst, pk = ps[1 + half], PK[1 + half]
                    for c in range(16):
                        MM(pst[:, :], wb[:, c * 128:(c + 1) * 128], mix[:, c, half * 512:(half + 1) * 512],
                           c == 0, c == 15, [wk] + [("mix", c)], [pk])
                    TT("dve", xb[:, half * 512:(half + 1) * 512], xb[:, half * 512:(half + 1) * 512], pst[:, :],
                       ALU.add, [("xs", m % 2), pk], [("xs", m % 2)])
                p.dma("sp", dst_ap[:, m, :], xb[:, :], reads=[("xs", m % 2)], writes=["x_out"])

        def layer0():
            with ExitStack() as s0:
                rope = sb("rope", [64, 2, T], F32, s0)
                p.dma("sp", rope[:], rope_d, writes=["rope"])
                kn = sb("kn", [128, 8, T], BF16, s0)
                vt = sb("vt", [128, 8, T], BF16, s0)
                kpe = sb("kpe", [64, T], BF16, s0)
                cq = sb("cq", [128, 4, T], F32, s0)
                with ExitStack() as s1:
                    hT = sb("hT", [128, 16, T], BF16, s1)
                    hh = sb("hh", [128, 16, HALO], BF16, s1)
                    rsth = sb("rsth", [128, HALO], F32, s1)

                    def sink_h(c, a, n, xb, xk, gname):
                        STT(hT[:, c, a:a + n], xb[:, 0:n], pc(gname, c), rstd[:, a:a + n], ALU.mult, ALU.mult,
                            [xk, "rstd", "prm"], ["hT"])
                    rms_stream(xT, "ev_g", T, HALO, sink_h)
                    if mode != "A":
                        for c in range(16):
                            xb = xs[c % 2]
                            p.dma("sp", xb[:, 0:HALO], xT[:, c, 0:HALO], writes=[("xs", c % 2)])
                            ACT(sqb[:, 0:HALO], xb[:, 0:HALO], AF.Square, [("xs", c % 2)], ["sqb"])
                            MM(ps[0][:, 0:HALO], ones, sqb[:, 0:HALO], c == 0, c == 15, ["sqb", "cbf"], [PK[0]])
                        ACT(rsth[:], ps[0][:, 0:HALO], AF.Sqrt, [PK[0], "epsc"], ["rsth"], scale=1.0 / 2048,
                            bias=epsc[:])
                        RECIP(rsth[:], rsth[:], ["rsth"], ["rsth"])
                        for c in range(16):
                            xb = xs[c % 2]
                            p.dma("sp", xb[:, 0:HALO], xT[:, c, 0:HALO], writes=[("xs", c % 2)])
                            STT(hh[:, c, :], xb[:, 0:HALO], pc("ev_g", c), rsth[:], ALU.mult, ALU.mult,
                                [("xs", c % 2), "rsth", "prm"], ["hh"])

                    def h_rhs(c, a, n):
                        return hT[:, c, a:a + n]

                    def in0_tile(ti, m, evac):
                        wb, wk = wload(w_in0[ti])
                        for half in range(2):
                            pst, pk = ps[1 + half], PK[1 + half]
                            lin_tile(wb, wk, 16, m, h_rhs, ["hT"], half * 512, 512, pst, pk)
                            evac(half, pst, pk)
                        return wb, wk

                    with ExitStack() as s2:
                        ckv = sb("ckv", [128, 4, T], F32, s2)
                        ckn = sb("ckn", [128, 4, T], BF16, s2)
                        kp32 = sb("kp32", [64, 2, T], F32, s2)
                        for j in range(4):
                            in0_tile(j, 128, lambda half, pst, pk, j=j: CP(
                                "act", ckv[:, j, half * 512:(half + 1) * 512], pst[:, :], [pk], ["ckv"]))
                        for j in range(2):
                            in0_tile(4 + j, 64, lambda half, pst, pk, j=j: CP(
                                "act", kp32[:, j, half * 512:(half + 1) * 512], pst[0:64, :], [pk], ["kp32"]))
                        for half in range(2):
                            sl = slice(half * 512, (half + 1) * 512)
                            for j in range(4):
                                ACT(sqb[:, :], ckv[:, j, sl], AF.Square, ["ckv"], ["sqb"])
                                MM(ps[0][:, :], ones, sqb[:, :], j == 0, j == 3, ["sqb", "cbf"], [PK[0]])
                            ACT(rstd[:, sl], ps[0][:, :], AF.Sqrt, [PK[0], "epsc"], ["rstd"], scale=1.0 / 512,
                                bias=epsc[:])
                            RECIP(rstd[:, sl], rstd[:, sl], ["rstd"], ["rstd"])
                            for j in range(4):
                                STT(ckn[:, j, sl], ckv[:, j, sl], pc("kv_g", j), rstd[:, sl], ALU.mult, ALU.mult,
                                    ["ckv", "rstd", "prm"], ["ckn"])
                        TT("dve", kp32[:, 0, :], kp32[:, 0, :], rope[:, 0, :], ALU.mult, ["kp32", "rope"], ["kp32"])
                        TT("dve", kp32[:, 1, :], kp32[:, 1, :], rope[:, 1, :], ALU.mult, ["kp32", "rope"], ["kp32"])
                        TT("dve", kpe[:, :], kp32[:, 0, :], kp32[:, 1, :], ALU.add, ["kp32"], ["kpe"])
                        if mode == "F":
                            p.dma("sp", kpb, kpe[:, :], reads=["kpe"], writes=["kpb"])
                            AG(kpb, kpgd, "kpb", "kpgd")
                        else:
                            p.dma("sp", kvb_a[1024:1088, :], kpe[:, :], reads=["kpe"], writes=["kvb"])
                        for h in range(8):
                            wb, wk = wload(w_ukvk[h], 512)
                            for half in range(2):
                                pst, pk = ps[1 + half], PK[1 + half]
                                for c in range(4):
                                    MM(pst[:, :], wb[:, c * 128:(c + 1) * 128], ckn[:, c, half * 512:(half + 1) * 512],
                                       c == 0, c == 3, [wk, "ckn"], [pk])
                                CP("act", kn[:, h, half * 512:(half + 1) * 512], pst[:, :], [pk], [("kn", h)])
                            if mode == "F":
                                p.dma("sp", kb[h // 2][(h % 2) * 128:(h % 2 + 1) * 128, :], kn[:, h, :],
                                      reads=[("kn", h)], writes=[("kb", h // 2)])
                                if h % 2 == 1:
                                    AG(kb[h // 2], kg[h // 2], ("kb", h // 2), ("kg", h // 2))
                            else:
                                p.dma("sp", kvb_a[h * 128:(h + 1) * 128, :], kn[:, h, :], reads=[("kn", h)],
                                      writes=["kvb"])
                        for vh in range(2):
                            wb, wk = wload(w_ukvv[vh])
                            for tt in range(8):
                                pst, pk = ps[1 + tt % 2], PK[1 + tt % 2]
                                for c in range(4):
                                    MM(pst[:, :], ckn[:, c, tt * 128:(tt + 1) * 128], wb[:, c * 512:(c + 1) * 512],
                                       c == 0, c == 3, [wk, "ckn"], [pk])
                                CP("act", vt[:, tt, vh * 512:(vh + 1) * 512], pst[:, :], [pk], ["vt"])
                        if mode == "F":
                            for i in range(4):
                                p.dma("sp", vb[i].rearrange("(t p) n -> p t n", p=128), vt[:, 2 * i:2 * i + 2, :],
                                      reads=["vt"], writes=[("vb", i)])
                                AG(vb[i], vgd[i], ("vb", i), ("vgd", i))
                        else:
                            p.dma("sp", kvb_a[1088:1088 + 1024, :].rearrange("(t p) n -> p t n", p=128), vt[:, :, :],
                                  reads=["vt"], writes=["kvb"])
                    p.barrier()
                    if mode == "A":
                        return
                    with ExitStack() as s2:
                        u = sb("u", [128, HALO + T], BF16, s2)
                        zs = sb("zs", [128, T], BF16, s2)
                        t32 = sb("t32", [128, T], F32, s2)
                        dg = sb("dg", [128, 31, 128], BF16, s2)
                        y32 = sb("y32", [128, 512], F32, s2)
                        ybf = sb("ybf", [128, 512], BF16, s2)
                        d32 = sb("d32", [128, 512], F32, s2)
                        r32 = sb("r32", [128, 512], F32, s2)
                        ones128 = sb("ones128", [128, 128], BF16, s2)
                        MEMSET("dve", ones128[:], 1.0 / 128, ["ones128"])
                        for j in range(8):
                            wb, wk = wload(w_in0[6 + 3 * j])
                            for half in range(2):
                                pst, pk = ps[1 + half], PK[1 + half]
                                lin_tile(wb, wk, 16, 128, h_rhs, ["hT"], half * 512, 512, pst, pk)
                                CP("act", t32[:, half * 512:(half + 1) * 512], pst[:, :], [pk], ["t32"])
                            for c in range(16):
                                MM(ps[3][:, 0:HALO], wb[:, c * 128:(c + 1) * 128], hh[:, c, :], c == 0, c == 15,
                                   [wk, "hh"], [PK[3]])
                            wb2, wk2 = wload(w_in0[7 + 3 * j])
                            for c in range(16):
                                MM(ps[4][:, 0:HALO], wb2[:, c * 128:(c + 1) * 128], hh[:, c, :], c == 0, c == 15,
                                   [wk2, "hh"], [PK[4]])
                            ACT(d32[:, 0:HALO], ps[4][:, 0:HALO], AF.Sigmoid, [PK[4]], ["d32"])
                            TT("dve", u[:, 0:HALO], d32[:, 0:HALO], ps[3][:, 0:HALO], ALU.mult, ["d32", PK[3]], ["u"])
                            for half in range(2):
                                pst, pk = ps[1 + half], PK[1 + half]
                                lin_tile(wb2, wk2, 16, 128, h_rhs, ["hT"], half * 512, 512, pst, pk)
                                ACT(d32[:, :], pst[:, :], AF.Sigmoid, [pk], ["d32"])
                                TT("dve", u[:, HALO + half * 512:HALO + (half + 1) * 512], d32[:, :],
                                   t32[:, half * 512:(half + 1) * 512], ALU.mult, ["d32", "t32"], ["u"])
                            wb3, wk3 = wload(w_in0[8 + 3 * j])
                            for half in range(2):
                                pst, pk = ps[1 + half], PK[1 + half]
                                lin_tile(wb3, wk3, 16, 128, h_rhs, ["hT"], half * 512, 512, pst, pk)
                                ACT(zs[:, half * 512:(half + 1) * 512], pst[:, :], AF.Silu, [pk], ["zs"])
                            o_w = PRM["conv_w"][0] + j * 31
                            for k in range(31):
                                TS("pool", dg[:, k, :], ident, prm[:, o_w + k:o_w + k + 1], None, ALU.mult, None,
                                   ["cbf", "prm"], ["dg"])
                            for half in range(2):
                                for k in range(31):
                                    a = HALO + half * 512 - 30 + k
                                    MM(ps[5][:, :], dg[:, k, :], u[:, a:a + 512], k == 0, k == 30, ["dg", "u"], [PK[5]])
                                ACT(y32[:], ps[5][:, :], AF.Identity, [PK[5], "prm"], ["y32"], bias=pc("conv_b", j))
                                CP("dve", ybf[:], y32[:], ["y32"], ["ybf"])
                                MM(ps[6][:, :], ones128[:], ybf[:], True, True, ["ones128", "ybf"], [PK[6]])
                                TT("dve", d32[:], y32[:], ps[6][:, :], ALU.subtract, ["y32", PK[6]], ["d32"])
                                ACT(ybf[:], d32[:], AF.Square, ["d32"], ["ybf"])
                                MM(ps[6][:, :], ones128[:], ybf[:], True, True, ["ones128", "ybf"], [PK[6]])
                                ACT(r32[:], ps[6][:, :], AF.Sqrt, [PK[6], "epsc"], ["r32"], bias=epsc[:])
                                RECIP(r32[:], r32[:], ["r32"], ["r32"])
                                TT("dve", d32[:], d32[:], r32[:], ALU.mult, ["d32", "r32"], ["d32"])
                                ACT(d32[:], d32[:], AF.Silu, ["d32", "prm"], ["d32"], scale=pc("ln_g", j),
                                    bias=pc("ln_b", j))
                                TT("dve", mix[:, j, half * 512:(half + 1) * 512], d32[:],
                                   zs[:, half * 512:(half + 1) * 512], ALU.mult, ["d32", "zs"], [("mix", j)])
                    p.barrier()
                    if CUT == "B":
                        p.muted = True
                    for j in range(8):
                        in0_tile(30 + j, 128, lambda half, pst, pk, j=j: ACT(
                            mix[:, 8 + j, half * 512:(half + 1) * 512], pst[:, :], AF.Silu, [pk], [("mix", 8 + j)]))
                    for j in range(4):
                        in0_tile(38 + j, 128, lambda half, pst, pk, j=j: CP(
                            "act", cq[:, j, half * 512:(half + 1) * 512], pst[:, :], [pk], ["cq"]))
                p.barrier()
                if CUT == "B2":
                    p.muted = True
                qn = sb("qn", [128, 8, T], BF16, s0)
                qpe = sb("qpe", [64, 8, T], BF16, s0)
                with ExitStack() as s2:
                    cqn = sb("cqn", [128, 4, T], BF16, s2)
                    qa = sb("qa", [64, 512], F32, s2)
                    qb = sb("qb", [64, 512], F32, s2)
                    for half in range(2):
                        sl = slice(half * 512, (half + 1) * 512)
                        for j in range(4):
                            ACT(sqb[:, :], cq[:, j, sl], AF.Square, ["cq"], ["sqb"])
                            MM(ps[0][:, :], ones, sqb[:, :], j == 0, j == 3, ["sqb", "cbf"], [PK[0]])
                        ACT(rstd[:, sl], ps[0][:, :], AF.Sqrt, [PK[0], "epsc"], ["rstd"], scale=1.0 / 512,
                            bias=epsc[:])
                        RECIP(rstd[:, sl], rstd[:, sl], ["rstd"], ["rstd"])
                        for j in range(4):
                            STT(cqn[:, j, sl], cq[:, j, sl], pc("q_g", j), rstd[:, sl], ALU.mult, ALU.mult,
                                ["cq", "rstd", "prm"], ["cqn"])
                    scale = 192.0 ** -0.5
                    for h in range(8):
                        wb, wk = wload(w_uq[h], 1024)
                        for half in range(2):
                            sl = slice(half * 512, (half + 1) * 512)
                            for c in range(4):
                                MM(ps[1][:, :], wb[:, c * 256:c * 256 + 128], cqn[:, c, sl], c == 0, c == 3,
                                   [wk, "cqn"], [PK[1]])
                            ACT(qn[:, h, sl], ps[1][:, :], AF.Identity, [PK[1]], [("qn", h)], scale=scale)
                            for c in range(4):
                                MM(ps[2][0:64, :], wb[:, c * 256 + 128:c * 256 + 192], cqn[:, c, sl], c == 0, c == 3,
                                   [wk, "cqn"], [PK[2]])
                            for c in range(4):
                                MM(ps[3][0:64, :], wb[:, c * 256 + 192:c * 256 + 256], cqn[:, c, sl], c == 0, c == 3,
                                   [wk, "cqn"], [PK[3]])
                            TT("dve", qa[:], ps[2][0:64, :], rope[:, 0, sl], ALU.mult, [PK[2], "rope"], ["qa"])
                            TT("dve", qb[:], ps[3][0:64, :], rope[:, 1, sl], ALU.mult, [PK[3], "rope"], ["qb"])
                            TT("dve", qa[:], qa[:], qb[:], ALU.add, ["qa", "qb"], ["qa"])
                            ACT(qpe[:, h, sl], qa[:], AF.Identity, ["qa"], [("qpe", h)], scale=scale)
                p.barrier()
                if CUT == "C":
                    p.muted = True
                with ExitStack() as s2:
                    kng = [sb("kng%d" % i, [128, 3 * T], BF16, s2) for i in range(2)]
                    vg = [sb("vg%d" % i, [128, 24, 128], BF16, s2) for i in range(2)]
                    kpg = sb("kpg", [64, 3 * T], BF16, s2)
                    onesv = sb("onesv", [128, 3, 128], BF16, s2)
                    pT = [sb("pT%d" % i, [128, 512], BF16, s2) for i in range(3)]
                    rl = sb("rl", [128, 512], F32, s2)
                    o32 = sb("o32", [128, 512], F32, s2)
                    for r in range(3):
                        if mode == "F":
                            p.dma("sp", kpg[:, r * T:(r + 1) * T], kpgd[r * 64:(r + 1) * 64, :],
                                  reads=["kpgd"], writes=["kpg"])
                        else:
                            p.dma("sp", kpg[:, r * T:(r + 1) * T], kvg_a[r * KVR + 1024:r * KVR + 1088, :],
                                  reads=["kvg"], writes=["kpg"])
                        TS("dve", onesv[:, r, :], ones, pc("vis", r), None, ALU.mult, None, ["cbf", "prm"], ["onesv"])
                    step = [0]
                    for h in range(8):
                        gi = h % 2
                        for r in range(3):
                            if mode == "F":
                                o_ = r * 256 + (h % 2) * 128
                                p.dma("sp", kng[gi][:, r * T:(r + 1) * T], kg[h // 2][o_:o_ + 128, :],
                                      reads=[("kg", h // 2)], writes=[("kng", gi)])
                                for i in range(4):
                                    p.dma("sp", vg[gi][:, r * 8 + 2 * i:r * 8 + 2 * i + 2, :],
                                          vgd[i][r * 256:(r + 1) * 256, h * 128:(h + 1) * 128].rearrange(
                                              "(t p) n -> p t n", p=128),
                                          reads=[("vgd", i)], writes=[("vg", gi)])
                            else:
                                p.dma("sp", kng[gi][:, r * T:(r + 1) * T],
                                      kvg_a[r * KVR + h * 128:r * KVR + (h + 1) * 128, :],
                                      reads=["kvg"], writes=[("kng", gi)])
                                p.dma("sp", vg[gi][:, r * 8:(r + 1) * 8, :],
                                      kvg_a[r * KVR + 1088:r * KVR + 1088 + 1024, h * 128:(h + 1) * 128].rearrange(
                                          "(t p) n -> p t n", p=128),
                                      reads=["kvg"], writes=[("vg", gi)])
                            TS("pool", vg[gi][:, r * 8:(r + 1) * 8, :], vg[gi][:, r * 8:(r + 1) * 8, :], pc("vis", r),
                               None, ALU.mult, None, [("vg", gi), "prm"], [("vg", gi)])
                        for qh in range(2):
                            q0 = qh * 512
                            tiles = []
                            nown = 4 if qh == 0 else 8
                            for kt in range(nown):
                                lo = max(0, kt * 128 - q0)
                                diag = (kt * 128 >= q0)
                                tiles.append(("own", kt, lo, diag))
                            for g in range(24):
                                tiles.append(("g", g, 0, False))
                            for ti, (kind, kt, lo, diag) in enumerate(tiles):
                                n = 512 - lo
                                si = step[0] % 3
                                step[0] += 1
                                pss, pks = ps[2 + si], PK[2 + si]
                                if kind == "own":
                                    kT = kn[:, h, kt * 128:(kt + 1) * 128]
                                    kP = kpe[:, kt * 128:(kt + 1) * 128]
                                    vv = vt[:, kt, h * 128:(h + 1) * 128]
                                    ov = ones
                                    rk = [("kn", h), "kpe", "vt"]
                                else:
                                    kT = kng[gi][:, kt * 128:(kt + 1) * 128]
                                    kP = kpg[:, kt * 128:(kt + 1) * 128]
                                    vv = vg[gi][:, kt, :]
                                    ov = onesv[:, kt // 8, :]
                                    rk = [("kng", gi), "kpg", ("vg", gi), "onesv"]
                                MM(pss[:, 0:n], kT, qn[:, h, q0 + lo:q0 + 512], True, False, rk + [("qn", h)], [pks])
                                MM(pss[:, 0:n], kP, qpe[:, h, q0 + lo:q0 + 512], False, True, rk + [("qpe", h)], [pks])
                                ACT(pT[si][:, 0:n], pss[:, 0:n], AF.Exp, [pks], [("pT", si)])
                                if diag:
                                    TT("dve", pT[si][:, 0:128], pT[si][:, 0:128], triu, ALU.mult,
                                       [("pT", si), "cbf"], [("pT", si)])
                                first = (ti == 0)
                                last = (ti == len(tiles) - 1)
                                MM(ps[0][:, lo:512], vv, pT[si][:, 0:n], first, last, rk + [("pT", si)], [PK[0]])
                                MM(ps[1][:, lo:512], ov, pT[si][:, 0:n], first, last, rk + ["cbf", ("pT", si)], [PK[1]])
                            CP("act", rl[:], ps[1][:, :], [PK[1]], ["rl"])
                            RECIP(rl[:], rl[:], ["rl"], ["rl"])
                            TT("dve", o32[:], ps[0][:, :], rl[:], ALU.mult, [PK[0], "rl"], ["o32"])
                            TT("dve", mix[:, 8 + h, q0:q0 + 512], o32[:], mix[:, 8 + h, q0:q0 + 512], ALU.mult,
                               ["o32", ("mix", 8 + h)], [("mix", 8 + h)])
            p.barrier()
            if CUT == "D":
                p.muted = True
            out_proj(w_out0, xT, HALO, x1d)
            p.muted = False
            p.barrier()

        def layer1(pass2):
            nsfx[0] = "_p2" if pass2 else "_p1"
            with ExitStack() as s0:
                h1 = sb("h1", [128, 16, T], BF16, s0)
                rmask = sb("rmask", [128, T], F32, s0)
                MEMSET("dve", rmask[:], 1.0, ["rmask"])
                MEMSET("dve", rmask[:].rearrange("p (c t) -> p c t", t=CH)[:, :, 0:1], 0.0, ["rmask"])

                def sink_h(c, a, n, xb, xk, gname):
                    STT(h1[:, c, a:a + n], xb[:, 0:n], pc(gname, c), rstd[:, a:a + n], ALU.mult, ALU.mult,
                        [xk, "rstd", "prm"], ["h1"])
                rms_stream(x1d, "od_g", T, 0, sink_h)

                def h_rhs(c, a, n):
                    return h1[:, c, a:a + n]

                sin_ = None
                if pass2:
                    sin_ = sb("sin", [128, 16, 128], F32, s0)
                    with ExitStack() as s1:
                        G = sb("G", [128, 3, 16, SW], F32, s1)
                        t2 = sb("t2", [128, 128], F32, s1)
                        t3 = sb("t3", [128, 128], F32, s1)
                        for r in range(3):
                            if mode == "F":
                                for i in range(4):
                                    p.dma("sp", G[:, r, 4 * i:4 * i + 4, :],
                                          sgp[i][r * 512:(r + 1) * 512, :].rearrange("(h p) w -> p h w", p=128),
                                          reads=[("sgp", i)], writes=["G"])
                            else:
                                p.dma("sp", G[:, r, :, :],
                                      stg_a[r * 2048:(r + 1) * 2048, :].rearrange("(h p) w -> p h w", p=128),
                                      reads=["stg"], writes=["G"])
                        for h in range(16):
                            S0, S1, S2 = G[:, 0, h, 0:128], G[:, 1, h, 0:128], G[:, 2, h, 0:128]
                            D1, D2 = G[:, 1, h, 128:129], G[:, 2, h, 128:129]
                            STT(t2[:], S0, D1, S1, ALU.mult, ALU.add, ["G"], ["t2"])
                            STT(t3[:], t2[:], D2, S2, ALU.mult, ALU.add, ["G", "t2"], ["t3"])
                            TS("dve", sin_[:, h, :], S0, pc("sel", 1), None, ALU.mult, None, ["G", "prm"], ["sin"])
                            STT(sin_[:, h, :], t2[:], pc("sel", 2), sin_[:, h, :], ALU.mult, ALU.add,
                                ["t2", "prm", "sin"], ["sin"])
                            STT(sin_[:, h, :], t3[:], pc("sel", 3), sin_[:, h, :], ALU.mult, ALU.add,
                                ["t3", "prm", "sin"], ["sin"])
                    p.barrier()

                sg = sb("sg", [128, T], F32, s0)
                ff = sb("ff", [128, T], F32, s0)
                lg = sb("lg", [128, T], F32, s0)
                bb = sb("bb", [128, T], F32, s0)
                khT = sb("khT", [128, T], BF16, s0)
                vT = sb("vT", [128, T], BF16, s0)
                ebc = sb("ebc", [128, NCH], F32, s0)
                ktok = [sb("ktok%d" % i, [CH, 128], BF16, s0) for i in range(2)]
                vtok = [sb("vtok%d" % i, [CH, 128], BF16, s0) for i in range(2)]
                S = sb("S", [128, 128], F32, s0)
                sst = sb("sst", [128, SW], F32, s0)
                if pass2:
                    q32 = sb("q32", [128, T], F32, s0)
                    o32 = sb("o32l", [128, T], F32, s0)
                    gs = sb("gs", [128, T], BF16, s0)
                    qtT = sb("qtT", [128, T], BF16, s0)
                    ktT = sb("ktT", [128, T], BF16, s0)
                    qhT = sb("qhT", [128, T], BF16, s0)
                    negr = sb("negr", [128, NCH], F32, s0)
                    Sbf = sb("Sbf", [128, 128], BF16, s0)
                    Am = [sb("Am%d" % i, [CH, CH], BF16, s0) for i in range(2)]
                    rs1 = sb("rs1", [128, 512], F32, s0)

                def tile_in(ti, evac):
                    wb, wk = wload(w_in1[ti])
                    for half in range(2):
                        pst, pk = ps[1 + half], PK[1 + half]
                        lin_tile(wb, wk, 16, 128, h_rhs, ["h1"], half * 512, 512, pst, pk)
                        evac(half, pst, pk)

                for h in range(NH):
                    tile_in(4 * h + 1, lambda half, pst, pk: ACT(sg[:, half * 512:(half + 1) * 512], pst[:, :],
                                                                AF.Sigmoid, [pk], ["sg"]))
                    TS("dve", ff[:], sg[:], omlb[:, h:h + 1], lbc[:, h:h + 1], ALU.mult, ALU.add,
                       ["sg", "omlb", "lbc"], ["ff"])
                    if CUT == "L1a":
                        p.muted = True
                    ACT(lg[:], ff[:], AF.Ln, ["ff"], ["lg"])
                    TS("dve", sg[:], ff[:], -1.0, 1.0, ALU.mult, ALU.add, ["ff"], ["sg"])
                    p.op("dve", lambda e: e.tensor_tensor_scan(out=bb[:], data0=rmask[:], data1=lg[:], initial=0.0,
                                                               op0=ALU.mult, op1=ALU.add),
                         ["rmask", "lg"], ["bb"])
                    bb3 = bb[:].rearrange("p (c t) -> p c t", t=CH)
                    if CUT == "L1b":
                        p.muted = True
                    ACT(ebc[:], bb3[:, :, CH - 1], AF.Exp, ["bb"], ["ebc"])
                    for c in range(NCH):
                        cs = slice(c * CH, (c + 1) * CH)
                        ACT(ff[:, cs], bb[:, cs], AF.Exp, ["bb"], ["ff"], scale=-1.0,
                            bias=bb[:, c * CH + CH - 1:c * CH + CH])
                    TT("dve", khT[:], sg[:], ff[:], ALU.mult, ["sg", "ff"], ["khT"])
                    tile_in(4 * h + 2, lambda half, pst, pk: CP("act", vT[:, half * 512:(half + 1) * 512], pst[:, :],
                                                               [pk], ["vT"]))
                    if pass2:
                        tile_in(4 * h + 0, lambda half, pst, pk: CP("act", q32[:, half * 512:(half + 1) * 512],
                                                                   pst[:, :], [pk], ["q32"]))
                        tile_in(4 * h + 3, lambda half, pst, pk: ACT(gs[:, half * 512:(half + 1) * 512], pst[:, :],
                                                                    AF.Silu, [pk], ["gs"]))
                        TS("dve", negr[:], bb3[:, :, CH // 2 - 1], -1.0, None, ALU.mult, None, ["bb"], ["negr"])
                        for c in range(NCH):
                            cs = slice(c * CH, (c + 1) * CH)
                            ACT(ff[:, cs], bb[:, cs], AF.Exp, ["bb", "negr"], ["ff"], bias=negr[:, c:c + 1])
                        TT("dve", qtT[:], q32[:], ff[:], ALU.mult, ["q32", "ff"], ["qtT"])
                        for c in range(NCH):
                            cs = slice(c * CH, (c + 1) * CH)
                            ACT(ff[:, cs], bb[:, cs], AF.Exp, ["bb"], ["ff"], scale=-1.0,
                                bias=bb[:, c * CH + CH // 2 - 1:c * CH + CH // 2])
                        TT("dve", ktT[:], sg[:], ff[:], ALU.mult, ["sg", "ff"], ["ktT"])
                        ACT(ff[:], bb[:], AF.Exp, ["bb"], ["ff"])
                        TT("dve", qhT[:], q32[:], ff[:], ALU.mult, ["q32", "ff"], ["qhT"])
                        CP("dve", S[:], sin_[:, h, :], ["sin"], ["S"])
                        CP("act", Sbf[:], sin_[:, h, :], ["sin"], ["Sbf"])
                    else:
                        MEMSET("dve", S[:], 0.0, ["S"])
                    if CUT == "L1c":
                        p.muted = True
                    for c in range(NCH):
                        cs = slice(c * CH, (c + 1) * CH)
                        i2 = c % 2
                        TR(psb[0:CH, 0, :], khT[:, cs], ["khT"], ["psb"])
                        CP("act", ktok[i2][:], psb[0:CH, 0, :], ["psb"], [("ktok", i2)])
                        TR(psv[0:CH, 0:128], vT[:, cs], ["vT"], [PK[6]])
                        CP("dve", vtok[i2][:], psv[0:CH, 0:128], [PK[6]], [("vtok", i2)])
                        if pass2:
                            MM(ps[3][0:CH, 0:CH], ktT[:, cs], qtT[:, cs], True, True, ["ktT", "qtT"], [PK[3]])
                            TT("dve", Am[i2][:], ps[3][0:CH, 0:CH], triu[0:CH, 0:CH], ALU.mult, [PK[3], "cbf"], [("Am", i2)])
                            MM(ps[4][:, 0:CH], vtok[i2][:], Am[i2][:], True, False, [("vtok", i2), ("Am", i2)], [PK[4]])
                            MM(ps[4][:, 0:CH], Sbf[:], qhT[:, cs], False, True, ["Sbf", "qhT"], [PK[4]])
                            CP("act", o32[:, cs], ps[4][:, 0:CH], [PK[4]], ["o32l"])
                        MM(ps[5][:, 0:128], ktok[i2][:], vtok[i2][:], True, True, [("ktok", i2), ("vtok", i2)], [PK[5]])
                        STT(S[:], S[:], ebc[:, c:c + 1], ps[5][:, 0:128], ALU.mult, ALU.add, ["S", "ebc", PK[5]], ["S"])
                        if pass2:
                            CP("act", Sbf[:], S[:], ["S"], ["Sbf"])
                    if CUT == "L1d":
                        p.muted = True
                    if not pass2:
                        CP("dve", sst[:, 0:128], S[:], ["S"], ["sst"])
                        if os.environ.get("MK_NOTAIL"):
                            MEMSET("dve", sst[:, 128:129], 0.0, ["sst"])
                        else:
                            TS("dve", sst[:, 128:129], ebc[:, 0:1], 1.0, None, ALU.mult, None, ["ebc", "sst"], ["sst"])
                            for c in range(1, NCH):
                                TT("dve", sst[:, 128:129], sst[:, 128:129], ebc[:, c:c + 1], ALU.mult,
                                   ["ebc", "sst"], ["sst"])
                        if mode == "F":
                            p.dma("sp", sbp[h // 4][(h % 4) * 128:(h % 4 + 1) * 128, 0:129], sst[:, 0:129],
                                  reads=["sst"], writes=[("sbp", h // 4)])
                            if h % 4 == 3:
                                AG(sbp[h // 4], sgp[h // 4], ("sbp", h // 4), ("sgp", h // 4))
                        else:
                            p.dma("sp", stb_a[h * 128:(h + 1) * 128, 0:129], sst[:, 0:129], reads=["sst"],
                                  writes=["stb"])
                    else:
                        for half in range(2):
                            sl = slice(half * 512, (half + 1) * 512)
                            ACT(sqb[:, :], o32[:, sl], AF.Square, ["o32l"], ["sqb"])
                            MM(ps[0][:, :], ones, sqb[:, :], True, True, ["sqb", "cbf"], [PK[0]])
                            ACT(rs1[:], ps[0][:, :], AF.Sqrt, [PK[0], "epsc"], ["rs1"], scale=1.0 / 128, bias=epsc[:])
                            RECIP(rs1[:], rs1[:], ["rs1"], ["rs1"])
                            STT(rs1[:], o32[:, sl], pc("hg"), rs1[:], ALU.mult, ALU.mult, ["o32l", "prm", "rs1"], ["rs1"])
                            TT("dve", mix[:, h, sl], rs1[:], gs[:, sl], ALU.mult, ["rs1", "gs"], [("mix", h)])
            p.barrier()

        def finale():
            out_proj(w_out1, x1d, 0, x2d)
            p.barrier()

            def sink_o(c, a, n, xb, xk, gname):
                STT(xb[:, 0:n], xb[:, 0:n], pc(gname, c), rstd[:, a:a + n], ALU.mult, ALU.mult,
                    [xk, "rstd", "prm"], [xk])
                p.dma("sp", outT[:, c, a:a + n], xb[:, 0:n], reads=[xk], writes=["outT"])
            rms_stream(x2d, "fin_g", T, 0, sink_o)

        if L0:
            layer0()
        if P1:
            layer1(False)
        if P2:
            layer1(True)
            finale()
        p.muted = False
        p.barrier()
        p.op("sp", lambda e: None, reads=(), writes=["done"])
        p.emit(st)
    return nc


def _relay(tile):
    K, n = tile.shape
    return np.ascontiguousarray(tile.reshape(K // 128, 128, n).transpose(1, 0, 2).reshape(128, (K // 128) * n))


def _cols(v):
    n = v.shape[0] // 128
    return np.ascontiguousarray(v.reshape(n, 128).T)


def prepare(inputs):
    f = lambda k: np.asarray(inputs[k], dtype=np.float32)
    x = f("x")
    W = f("ev_w_in")[0]
    a_v, a_g, a_z = W[:, 0:1024], W[:, 1024:2048], W[:, 2048:3072]
    c_q, c_kv, k_pe, b_z = W[:, 3072:3584], W[:, 3584:4096], W[:, 4096:4160], W[:, 4160:5184]
    zpad = np.zeros((2048, 64), np.float32)
    sw = np.concatenate([np.arange(32, 64), np.arange(0, 32)])
    tiles = [c_kv[:, j * 128:(j + 1) * 128] for j in range(4)]
    tiles += [np.concatenate([k_pe, zpad], 1), np.concatenate([k_pe[:, sw], zpad], 1)]
    for j in range(8):
        sl = slice(j * 128, (j + 1) * 128)
        tiles += [a_v[:, sl], a_g[:, sl], a_z[:, sl]]
    tiles += [b_z[:, j * 128:(j + 1) * 128] for j in range(8)]
    tiles += [c_q[:, j * 128:(j + 1) * 128] for j in range(4)]
    w_in0 = np.stack([_relay(t) for t in tiles])
    uq = f("mla_w_uq")[0]
    w_uq = np.stack([_relay(np.concatenate([uq[:, h, 0:128], uq[:, h, 128:192], uq[:, h, 128:192][:, sw]], 1))
                     for h in range(8)])
    ukv = f("mla_w_ukv")[0]
    w_ukvk = np.stack([_relay(ukv[:, h, 0:128]) for h in range(8)])
    vfull = ukv[:, :, 128:256].reshape(512, 1024)
    w_ukvv = np.stack([_relay(vfull[:, 0:512]), _relay(vfull[:, 512:1024])])
    wo0 = f("ev_w_out")[0]
    w_out0 = np.stack([_relay(wo0[:, m * 128:(m + 1) * 128]) for m in range(16)])
    W1 = f("od_w_in")[0]
    t1 = []
    for h in range(16):
        for part in range(4):
            t1.append(W1[:, part * 2048 + h * 128: part * 2048 + (h + 1) * 128])
    w_in1 = np.stack([_relay(t) for t in t1])
    wo1 = f("od_w_out")[0]
    w_out1 = np.stack([_relay(wo1[:, m * 128:(m + 1) * 128]) for m in range(16)])
    prm = np.zeros((128, NPRM), np.float32)

    def put(name, arr):
        o, w = PRM[name]
        prm[:, o:o + w] = arr
    put("ev_g", _cols(f("ev_norm_g")[0]))
    put("conv_b", _cols(f("conv_b")[0]))
    put("ln_g", _cols(f("conv_ln_g")[0]))
    put("ln_b", _cols(f("conv_ln_b")[0]))
    put("q_g", _cols(f("mla_q_norm_g")[0]))
    put("kv_g", _cols(f("mla_kv_norm_g")[0]))
    put("od_g", _cols(f("od_norm_g")[0]))
    put("fin_g", _cols(f("final_norm_g")))
    put("hg", f("hgrn_norm_g")[0].reshape(128, 1))
    lg = f("hgrn_lb_logits")
    put("l0", _cols(lg[0]))
    put("l1", _cols(lg[1]))
    cw = f("conv_w")[0]
    put("conv_w", cw.reshape(31, 8, 128).transpose(2, 1, 0).reshape(128, 8 * 31))
    cbf = np.zeros((128, 3, 128), np.float32)
    cbf[:, 0, :] = np.eye(128)
    cbf[:, 1, :] = 1.0
    cbf[:, 2, :] = np.triu(np.ones((128, 128)))
    cbf = cbf.astype(ml_dtypes.bfloat16)
    inv_freq = (1.0 / (np.float32(10000.0) ** (np.arange(0, 64, 2, dtype=np.float32) / np.float32(64)))).astype(np.float32)
    shared = dict(cbf=cbf, w_in0=w_in0, w_uq=w_uq, w_ukvk=w_ukvk, w_ukvv=w_ukvv, w_out0=w_out0, w_in1=w_in1,
                  w_out1=w_out1)
    per = []
    for c in range(8):
        b, j = c // 4, c % 4
        s0 = j * T
        xs_ = np.zeros((HALO + T, 2048), np.float32)
        if j > 0:
            xs_[:, :] = x[b, s0 - HALO:s0 + T, :]
        else:
            xs_[HALO:, :] = x[b, 0:T, :]
        xT = np.ascontiguousarray(xs_.reshape(HALO + T, 16, 128).transpose(2, 1, 0))
        pos = np.arange(s0, s0 + T, dtype=np.float32)
        ang = pos[:, None] * inv_freq[None, :]
        cs, sn = np.cos(ang).astype(np.float32).T, np.sin(ang).astype(np.float32).T
        rope = np.stack([np.concatenate([cs, cs], 0), np.concatenate([-sn, sn], 0)], 1).astype(np.float32)
        pr = prm.copy()
        o, w = PRM["sel"]
        pr[:, o + j] = 1.0
        o, w = PRM["vis"]
        for r in range(3):
            pr[:, o + r] = 1.0 if r < j else 0.0
        per.append(dict(xT=xT, prm=pr, rope=np.ascontiguousarray(rope)))
    return shared, per


_NC = {}


def _get(mode, l1=True):
    k = (mode, l1)
    if k not in _NC:
        _NC[k] = build(mode, l1)
    return _NC[k]


def _launch(nc, maps):
    return run_bass_kernel_spmd(nc, maps, core_ids=list(range(8))).results


def _assemble(res, name):
    out = np.zeros((2, 4096, 2048), np.float32)
    for c in range(8):
        b, j = c // 4, c % 4
        o = np.asarray(res[c][name])
        out[b, j * T:(j + 1) * T, :] = o.transpose(2, 1, 0).reshape(T, 2048)
    return out


def run_unfused(inputs, upto="C"):
    shared, per = prepare(inputs)
    mA = [dict(prm=per[c]["prm"], cbf=shared["cbf"], xT=per[c]["xT"], rope=per[c]["rope"],
               w_in0=shared["w_in0"][0:6], w_ukvk=shared["w_ukvk"], w_ukvv=shared["w_ukvv"]) for c in range(8)]
    rA = _launch(_get("A"), mA)
    kvg = [np.concatenate([np.asarray(rA[(c // 4) * 4 + r]["kvb"]) for r in range(4)], 0) for c in range(8)]
    mB = [dict(prm=per[c]["prm"], cbf=shared["cbf"], xT=per[c]["xT"], rope=per[c]["rope"], kvg=kvg[c],
               w_in0=shared["w_in0"], w_ukvk=shared["w_ukvk"], w_ukvv=shared["w_ukvv"], w_uq=shared["w_uq"],
               w_out0=shared["w_out0"]) for c in range(8)]
    rB = _launch(_get("B", False), mB)
    if upto == "B0":
        return _assemble(rB, "x1d")
    x1 = [np.asarray(rB[c]["x1d"]) for c in range(8)]
    mP = [dict(prm=per[c]["prm"], cbf=shared["cbf"], x1d=x1[c], w_in1=shared["w_in1"]) for c in range(8)]
    rP = _launch(_get("P"), mP)
    stg = [np.concatenate([np.asarray(rP[(c // 4) * 4 + r]["stb"]) for r in range(4)], 0) for c in range(8)]
    mC = [dict(prm=per[c]["prm"], cbf=shared["cbf"], x1d=x1[c], stg=stg[c],
               w_in1=shared["w_in1"], w_out1=shared["w_out1"]) for c in range(8)]
    rC = _launch(_get("C"), mC)
    return _assemble(rC, "outT")


def run_fused(inputs):
    shared, per = prepare(inputs)
    maps = [dict(prm=per[c]["prm"], cbf=shared["cbf"], xT=per[c]["xT"], rope=per[c]["rope"], **{
        k: shared[k] for k in ("w_in0", "w_ukvk", "w_ukvv", "w_uq", "w_out0", "w_in1", "w_out1")}) for c in range(8)]
    return _assemble(_launch(_get("F"), maps), "outT")


FUSED = True


def kernel(**inputs):
    if FUSED:
        return run_fused(inputs)
    return run_unfused(inputs)
```

```python
import numpy as np
import ml_dtypes
from contextlib import ExitStack
import concourse.bass as bass
import concourse.mybir as mybir
from concourse.bass_utils import run_bass_kernel_spmd

F32 = mybir.dt.float32
BF16 = mybir.dt.bfloat16
AF = mybir.ActivationFunctionType
ALU = mybir.AluOpType
ENGS = ("pe", "act", "dve", "pool", "sp")
T = 1024
HALO = 32
EPS = 1e-6
NEG = -30000.0


class Op:
    __slots__ = ("idx", "eng", "fn", "deps", "dma", "need_inc", "sem", "val", "extra_waits", "cc")

    def __init__(self, idx, eng, fn, deps, dma):
        self.idx = idx
        self.eng = eng
        self.fn = fn
        self.deps = deps
        self.dma = dma
        self.need_inc = False
        self.sem = None
        self.val = 0
        self.extra_waits = []
        self.cc = False


class Prog:
    def __init__(self, nc, n_dma_sems=8):
        self.nc = nc
        self.ops = []
        self.last_w = {}
        self.readers = {}
        self.barrier_deps = []
        self.last_on_eng = {}
        self.dmas_since_barrier = []
        self.n_dma_sems = n_dma_sems

    muted = False

    def op(self, eng, fn, reads=(), writes=(), dma=False):
        if self.muted:
            return -1
        idx = len(self.ops)
        deps = set(self.barrier_deps)
        for k in reads:
            w = self.last_w.get(k)
            if w is not None:
                deps.add(w)
        for k in writes:
            w = self.last_w.get(k)
            if w is not None:
                deps.add(w)
            for r in self.readers.get(k, ()):
                deps.add(r)
        o = Op(idx, eng, fn, deps, dma)
        self.ops.append(o)
        for k in writes:
            self.last_w[k] = idx
            self.readers[k] = []
        for k in reads:
            if k not in writes:
                self.readers.setdefault(k, []).append(idx)
        self.last_on_eng[eng] = idx
        if dma:
            self.dmas_since_barrier.append(idx)
        return idx

    def barrier(self):
        deps = set(self.last_on_eng.values()) | set(self.dmas_since_barrier)
        deps = {d for d in deps if not self.ops[d].cc}
        self.barrier_deps = sorted(deps)
        self.dmas_since_barrier = []
        keep = {k: w for k, w in self.last_w.items() if self.ops[w].cc}
        self.last_w = keep
        self.readers = {k: [] for k in keep}

    def dma(self, q, out, in_, reads=(), writes=(), **kw):
        return self.op(q, lambda e: e.dma_start(out=out, in_=in_, **kw), reads, writes, dma=True)

    def collective(self, fn, reads=(), writes=()):
        i = self.op("pool", fn, reads, writes, dma=True)
        self.ops[i].cc = True
        return i

    def emit(self, stack):
        nc = self.nc
        ops = self.ops
        for o in ops:
            for d in o.deps:
                od = ops[d]
                if od.dma:
                    od.need_inc = True
                elif od.eng == "pe" and o.eng == "pe" and not o.dma:
                    continue
                else:
                    od.need_inc = True
        esem = {e: stack.enter_context(nc.semaphore("s_" + e)) for e in ENGS}
        dsem = {}
        for q in ("sp", "act", "pool"):
            dsem[q] = [stack.enter_context(nc.semaphore("d_%s%d" % (q, i))) for i in range(self.n_dma_sems)]
        ecount = {e: 0 for e in ENGS}
        dcount = {q: [0] * self.n_dma_sems for q in dsem}
        drr = {q: 0 for q in dsem}
        for o in ops:
            if o.cc:
                o.sem = stack.enter_context(nc.semaphore("cc%d" % o.idx))
                o.val = 1
            elif o.dma:
                q = o.eng
                i = drr[q]
                drr[q] = (i + 1) % self.n_dma_sems
                prev = dcount[q][i]
                if prev > 0:
                    o.extra_waits.append((dsem[q][i], prev))
                dcount[q][i] = prev + 16
                o.sem = dsem[q][i]
                o.val = prev + 16
            elif o.need_inc:
                ecount[o.eng] += 1
                o.sem = esem[o.eng]
                o.val = ecount[o.eng]
        engobj = {"pe": "tensor", "act": "scalar", "dve": "vector", "pool": "gpsimd", "sp": "sync"}
        block = stack.enter_context(nc.Block())

        def make(engname):
            def body(eng):
                known = {}
                for o in ops:
                    if o.eng != engname:
                        continue
                    waits = list(o.extra_waits)
                    for d in sorted(o.deps):
                        od = ops[d]
                        if od.sem is None:
                            continue
                        if (not od.dma) and od.eng == "pe" and engname == "pe" and not o.dma:
                            continue
                        waits.append((od.sem, od.val))
                    best = {}
                    for s, v in waits:
                        key = id(s)
                        if key not in best or best[key][1] < v:
                            best[key] = (s, v)
                    for key, (s, v) in best.items():
                        if known.get(key, 0) >= v:
                            continue
                        eng.wait_ge(s, v)
                        known[key] = v
                    ins = o.fn(eng)
                    if o.sem is not None and ins is not None:
                        ins.then_inc(o.sem, 1 if (o.cc or not o.dma) else 16)
            return body

        for engname in ENGS:
            getattr(block, engobj[engname])(make(engname))


PRM = {}
_o = 0
for _n, _w in (("ev_g", 16), ("conv_b", 8), ("ln_g", 8), ("ln_b", 8), ("q_g", 4), ("kv_g", 4),
               ("od_g", 16), ("fin_g", 16), ("hg", 1), ("l0", 16), ("l1", 16), ("conv_w", 8 * 31),
               ("sel", 4), ("vis", 3)):
    PRM[_n] = (_o, _w)
    _o += _w
NPRM = _o

N_IN0 = 42
KVR = 1024 + 64 + 1024
CH = 64
NCH = T // CH
SW = 136


def build(mode, l1=True):
    import os
    CUT = os.environ.get("MK_CUT", "")
    nc = bass.Bass("TRN2", target_bir_lowering=False)
    L0 = mode in ("A", "B", "F")
    P1 = (mode in ("B", "F") and l1) or mode == "P"
    P2 = mode in ("C", "F")
    NH = int(os.environ.get("MK_HEADS", "16"))

    def decl(name, shape, dt, role):
        if role == "in":
            return nc.dram_tensor(name, shape, dt, kind="ExternalInput").ap()
        if role == "out":
            return nc.dram_tensor(name, shape, dt, kind="ExternalOutput").ap()
        return nc.dram_tensor(name, shape, dt).ap()

    prm_d = decl("prm", [128, NPRM], F32, "in")
    cbf_d = decl("cbf", [128, 3, 128], BF16, "in")
    if L0:
        xT = decl("xT", [128, 16, HALO + T], F32, "in")
        rope_d = decl("rope", [64, 2, T], F32, "in")
        w_in0 = decl("w_in0", [6 if mode == "A" else N_IN0, 128, 2048], F32, "in")
        w_ukvk = decl("w_ukvk", [8, 128, 4 * 128], F32, "in")
        w_ukvv = decl("w_ukvv", [2, 128, 4 * 512], F32, "in")
        if mode != "F":
            kvb_a = decl("kvb", [KVR, T], BF16, "out" if mode == "A" else "int")
    if mode in ("B", "F"):
        w_uq = decl("w_uq", [8, 128, 4 * 256], F32, "in")
        w_out0 = decl("w_out0", [16, 128, 2048], F32, "in")
        if mode == "B":
            kvg_a = decl("kvg", [4 * KVR, T], BF16, "in")
    if mode != "A":
        x1d = decl("x1d", [128, 16, T], F32, {"B": "out", "C": "in", "F": "int", "P": "in"}[mode])
    if P1 or P2:
        w_in1 = decl("w_in1", [64, 128, 2048], F32, "in")
    if P1:
        if mode != "F":
            stb_a = decl("stb", [2048, SW], F32, "out")
    if P2:
        if mode == "C":
            stg_a = decl("stg", [4 * 2048, SW], F32, "in")
        w_out1 = decl("w_out1", [16, 128, 2048], F32, "in")
        x2d = decl("x2d", [128, 16, T], F32, "int")
        outT = decl("outT", [128, 16, T], F32, "out")

    RG = [[0, 1, 2, 3], [4, 5, 6, 7]]
    if mode == "F":
        kb = [decl("kb%d" % i, [256, T], BF16, "int") for i in range(4)]
        kg = [decl("kg%d" % i, [1024, T], BF16, "int") for i in range(4)]
        kpb = decl("kpb", [64, T], BF16, "int")
        kpgd = decl("kpgd", [256, T], BF16, "int")
        vb = [decl("vb%d" % i, [256, T], BF16, "int") for i in range(4)]
        vgd = [decl("vgd%d" % i, [1024, T], BF16, "int") for i in range(4)]
        sbp = [decl("sbp%d" % i, [512, SW], F32, "int") for i in range(4)]
        sgp = [decl("sgp%d" % i, [2048, SW], F32, "int") for i in range(4)]

    st = ExitStack()
    with st:
        def AG(in_ap, out_ap, rkey, wkey):
            p.collective(lambda e: e.collective_compute("AllGather", ALU.bypass, replica_groups=RG,
                                                        ins=[in_ap.opt()], outs=[out_ap.opt()]),
                         reads=[rkey], writes=[wkey])

        nsfx = [""]

        def sb(name, shape, dt=F32, stack=None):
            return (stack or st).enter_context(nc.sbuf_tensor("sb_" + name + nsfx[0], shape, dt))

        p = Prog(nc)
        ps = [st.enter_context(nc.psum_tensor("ps%d" % i, [128, 512], F32)) for i in range(7)]
        psb = st.enter_context(nc.psum_tensor("psb", [128, 8, 128], BF16))
        PK = [("ps", i) for i in range(7)]
        psv = ps[6][:, :].bitcast(BF16)

        def MM(out, lhsT, rhs, start, stop, reads, writes):
            p.op("pe", lambda e: e.matmul(out, lhsT=lhsT, rhs=rhs, start=start, stop=stop), reads, writes)

        def TR(out, in_, reads, writes):
            p.op("pe", lambda e: e.transpose(out, in_, ident), reads + ["cbf"], writes)

        def ACT(out, in_, func, reads, writes, scale=1.0, bias=None):
            if bias is None:
                p.op("act", lambda e: e.activation(out=out, in_=in_, func=func, scale=scale), reads, writes)
            else:
                p.op("act", lambda e: e.activation(out=out, in_=in_, func=func, scale=scale, bias=bias), reads, writes)

        def TT(eng, out, in0, in1, op, reads, writes):
            p.op(eng, lambda e: e.tensor_tensor(out=out, in0=in0, in1=in1, op=op), reads, writes)

        def TS(eng, out, in0, s1, s2, op0, op1, reads, writes):
            if s2 is None:
                p.op(eng, lambda e: e.tensor_scalar(out=out, in0=in0, scalar1=s1, scalar2=None, op0=op0), reads, writes)
            else:
                p.op(eng, lambda e: e.tensor_scalar(out=out, in0=in0, scalar1=s1, scalar2=s2, op0=op0, op1=op1), reads, writes)

        def STT(out, in0, scalar, in1, op0, op1, reads, writes):
            p.op("dve", lambda e: e.scalar_tensor_tensor(out=out, in0=in0, scalar=scalar, in1=in1, op0=op0, op1=op1),
                 reads, writes)

        def CP(eng, out, in_, reads, writes):
            if eng == "act":
                p.op("act", lambda e: e.activation(out=out, in_=in_, func=AF.Identity), reads, writes)
            else:
                p.op(eng, lambda e: e.tensor_copy(out=out, in_=in_), reads, writes)

        def RECIP(out, in_, reads, writes):
            p.op("dve", lambda e: e.reciprocal(out=out, in_=in_), reads, writes)

        def MEMSET(eng, ap, val, writes):
            p.op(eng, lambda e: e.memset(ap, val), (), writes)

        prm = sb("prm", [128, NPRM])
        cbf = sb("cbf", [128, 3, 128], BF16)
        epsc = sb("epsc", [128, 1])
        lbc = sb("lbc", [128, 16])
        omlb = sb("omlb", [128, 16])
        p.dma("sp", prm[:], prm_d, writes=["prm"])
        p.dma("sp", cbf[:], cbf_d, writes=["cbf"])
        MEMSET("dve", epsc[:], EPS, ["epsc"])
        ident = cbf[:, 0, :]
        ones = cbf[:, 1, :]
        triu = cbf[:, 2, :]

        def pc(name, j=0, n=1):
            o, w = PRM[name]
            return prm[:, o + j:o + j + n]

        TT("dve", lbc[:], pc("l1", 0, 16), pc("l0", 0, 16), ALU.subtract, ["prm"], ["lbc"])
        ACT(lbc[:], lbc[:], AF.Sigmoid, ["lbc"], ["lbc"])
        TS("dve", omlb[:], lbc[:], -1.0, 1.0, ALU.mult, ALU.add, ["lbc"], ["omlb"])

        wst = [sb("wst%d" % i, [128, 2048]) for i in range(2)]
        wbf = [sb("wbf%d" % i, [128, 2048], BF16) for i in range(2)]
        wctr = [0]

        def wload(src, n=2048):
            i = wctr[0] % 2
            wctr[0] += 1
            p.dma("sp", wst[i][:, 0:n], src, writes=[("wst", i)])
            CP("pool", wbf[i][:, 0:n], wst[i][:, 0:n], [("wst", i)], [("wbf", i)])
            return wbf[i], ("wbf", i)

        mix = sb("mix", [128, 16, T], BF16)
        rstd = sb("rstd", [128, T])
        sqb = sb("sqb", [128, 512], BF16)
        xs = [sb("xs%d" % i, [128, T]) for i in range(2)]

        def rms_stream(src_ap, gname, ncol, col0, sink):
            nh = [(a, min(512, ncol - a)) for a in range(0, ncol, 512)]
            for (a, n) in nh:
                for c in range(16):
                    xb = xs[c % 2]
                    p.dma("sp", xb[:, 0:n], src_ap[:, c, col0 + a:col0 + a + n], writes=[("xs", c % 2)])
                    ACT(sqb[:, 0:n], xb[:, 0:n], AF.Square, [("xs", c % 2)], ["sqb"])
                    MM(ps[0][:, 0:n], ones, sqb[:, 0:n], c == 0, c == 15, ["sqb", "cbf"], [PK[0]])
                ACT(rstd[:, a:a + n], ps[0][:, 0:n], AF.Sqrt, [PK[0], "epsc"], ["rstd"], scale=1.0 / 2048, bias=epsc[:])
                RECIP(rstd[:, a:a + n], rstd[:, a:a + n], ["rstd"], ["rstd"])
                for c in range(16):
                    xb = xs[c % 2]
                    p.dma("sp", xb[:, 0:n], src_ap[:, c, col0 + a:col0 + a + n], writes=[("xs", c % 2)])
                    sink(c, a, n, xb, ("xs", c % 2), gname)

        def lin_tile(wb, wkey, kc, m, rhs_fn, rkeys, a, n, pst, pkey):
            for c in range(kc):
                MM(pst[0:m, 0:n], wb[:, c * 128:c * 128 + m], rhs_fn(c, a, n), c == 0, c == kc - 1,
                   [wkey] + rkeys, [pkey])

        def out_proj(w_d, src_ap, col0, dst_ap):
            for m in range(16):
                wb, wk = wload(w_d[m])
                xb = xs[m % 2]
                p.dma("sp", xb[:, :], src_ap[:, m, col0:col0 + T], writes=[("xs", m % 2)])
                for half in range(2):
                    pst, pk = ps[1 + half], PK[1 + half]
                    for c in range(16):
                        MM(pst[:, :], wb[:, c * 128:(c + 1) * 128], mix[:, c, half * 512:(half + 1) * 512],
                           c == 0, c == 15, [wk] + [("mix", c)], [pk])
                    TT("dve", xb[:, half * 512:(half + 1) * 512], xb[:, half * 512:(half + 1) * 512], pst[:, :],
                       ALU.add, [("xs", m % 2), pk], [("xs", m % 2)])
                p.dma("sp", dst_ap[:, m, :], xb[:, :], reads=[("xs", m % 2)], writes=["x_out"])

        def layer0():
            with ExitStack() as s0:
                rope = sb("rope", [64, 2, T], F32, s0)
                p.dma("sp", rope[:], rope_d, writes=["rope"])
                kn = sb("kn", [128, 8, T], BF16, s0)
                vt = sb("vt", [128, 8, T], BF16, s0)
                kpe = sb("kpe", [64, T], BF16, s0)
                cq = sb("cq", [128, 4, T], F32, s0)
                with ExitStack() as s1:
                    hT = sb("hT", [128, 16, T], BF16, s1)
                    hh = sb("hh", [128, 16, HALO], BF16, s1)
                    rsth = sb("rsth", [128, HALO], F32, s1)

                    def sink_h(c, a, n, xb, xk, gname):
                        STT(hT[:, c, a:a + n], xb[:, 0:n], pc(gname, c), rstd[:, a:a + n], ALU.mult, ALU.mult,
                            [xk, "rstd", "prm"], ["hT"])
                    rms_stream(xT, "ev_g", T, HALO, sink_h)
                    if mode != "A":
                        for c in range(16):
                            xb = xs[c % 2]
                            p.dma("sp", xb[:, 0:HALO], xT[:, c, 0:HALO], writes=[("xs", c % 2)])
                            ACT(sqb[:, 0:HALO], xb[:, 0:HALO], AF.Square, [("xs", c % 2)], ["sqb"])
                            MM(ps[0][:, 0:HALO], ones, sqb[:, 0:HALO], c == 0, c == 15, ["sqb", "cbf"], [PK[0]])
                        ACT(rsth[:], ps[0][:, 0:HALO], AF.Sqrt, [PK[0], "epsc"], ["rsth"], scale=1.0 / 2048,
                            bias=epsc[:])
                        RECIP(rsth[:], rsth[:], ["rsth"], ["rsth"])
                        for c in range(16):
                            xb = xs[c % 2]
                            p.dma("sp", xb[:, 0:HALO], xT[:, c, 0:HALO], writes=[("xs", c % 2)])
                            STT(hh[:, c, :], xb[:, 0:HALO], pc("ev_g", c), rsth[:], ALU.mult, ALU.mult,
                                [("xs", c % 2), "rsth", "prm"], ["hh"])

                    def h_rhs(c, a, n):
                        return hT[:, c, a:a + n]

                    def in0_tile(ti, m, evac):
                        wb, wk = wload(w_in0[ti])
                        for half in range(2):
                            pst, pk = ps[1 + half], PK[1 + half]
                            lin_tile(wb, wk, 16, m, h_rhs, ["hT"], half * 512, 512, pst, pk)
                            evac(half, pst, pk)
                        return wb, wk

                    with ExitStack() as s2:
                        ckv = sb("ckv", [128, 4, T], F32, s2)
                        ckn = sb("ckn", [128, 4, T], BF16, s2)
                        kp32 = sb("kp32", [64, 2, T], F32, s2)
                        for j in range(4):
                            in0_tile(j, 128, lambda half, pst, pk, j=j: CP(
                                "act", ckv[:, j, half * 512:(half + 1) * 512], pst[:, :], [pk], ["ckv"]))
                        for j in range(2):
                            in0_tile(4 + j, 64, lambda half, pst, pk, j=j: CP(
                                "act", kp32[:, j, half * 512:(half + 1) * 512], pst[0:64, :], [pk], ["kp32"]))
                        for half in range(2):
                            sl = slice(half * 512, (half + 1) * 512)
                            for j in range(4):
                                ACT(sqb[:, :], ckv[:, j, sl], AF.Square, ["ckv"], ["sqb"])
                                MM(ps[0][:, :], ones, sqb[:, :], j == 0, j == 3, ["sqb", "cbf"], [PK[0]])
                            ACT(rstd[:, sl], ps[0][:, :], AF.Sqrt, [PK[0], "epsc"], ["rstd"], scale=1.0 / 512,
                                bias=epsc[:])
                            RECIP(rstd[:, sl], rstd[:, sl], ["rstd"], ["rstd"])
                            for j in range(4):
                                STT(ckn[:, j, sl], ckv[:, j, sl], pc("kv_g", j), rstd[:, sl], ALU.mult, ALU.mult,
                                    ["ckv", "rstd", "prm"], ["ckn"])
                        TT("dve", kp32[:, 0, :], kp32[:, 0, :], rope[:, 0, :], ALU.mult, ["kp32", "rope"], ["kp32"])
                        TT("dve", kp32[:, 1, :], kp32[:, 1, :], rope[:, 1, :], ALU.mult, ["kp32", "rope"], ["kp32"])
                        TT("dve", kpe[:, :], kp32[:, 0, :], kp32[:, 1, :], ALU.add, ["kp32"], ["kpe"])
                        if mode == "F":
                            p.dma("sp", kpb, kpe[:, :], reads=["kpe"], writes=["kpb"])
                            AG(kpb, kpgd, "kpb", "kpgd")
                        else:
                            p.dma("sp", kvb_a[1024:1088, :], kpe[:, :], reads=["kpe"], writes=["kvb"])
                        for h in range(8):
                            wb, wk = wload(w_ukvk[h], 512)
                            for half in range(2):
                                pst, pk = ps[1 + half], PK[1 + half]
                                for c in range(4):
                                    MM(pst[:, :], wb[:, c * 128:(c + 1) * 128], ckn[:, c, half * 512:(half + 1) * 512],
                                       c == 0, c == 3, [wk, "ckn"], [pk])
                                CP("act", kn[:, h, half * 512:(half + 1) * 512], pst[:, :], [pk], [("kn", h)])
                            if mode == "F":
                                p.dma("sp", kb[h // 2][(h % 2) * 128:(h % 2 + 1) * 128, :], kn[:, h, :],
                                      reads=[("kn", h)], writes=[("kb", h // 2)])
                                if h % 2 == 1:
                                    AG(kb[h // 2], kg[h // 2], ("kb", h // 2), ("kg", h // 2))
                            else:
                                p.dma("sp", kvb_a[h * 128:(h + 1) * 128, :], kn[:, h, :], reads=[("kn", h)],
                                      writes=["kvb"])
                        for vh in range(2):
                            wb, wk = wload(w_ukvv[vh])
                            for tt in range(8):
                                pst, pk = ps[1 + tt % 2], PK[1 + tt % 2]
                                for c in range(4):
                                    MM(pst[:, :], ckn[:, c, tt * 128:(tt + 1) * 128], wb[:, c * 512:(c + 1) * 512],
                                       c == 0, c == 3, [wk, "ckn"], [pk])
                                CP("act", vt[:, tt, vh * 512:(vh + 1) * 512], pst[:, :], [pk], ["vt"])
                        if mode == "F":
                            for i in range(4):
                                p.dma("sp", vb[i].rearrange("(t p) n -> p t n", p=128), vt[:, 2 * i:2 * i + 2, :],
                                      reads=["vt"], writes=[("vb", i)])
                                AG(vb[i], vgd[i], ("vb", i), ("vgd", i))
                        else:
                            p.dma("sp", kvb_a[1088:1088 + 1024, :].rearrange("(t p) n -> p t n", p=128), vt[:, :, :],
                                  reads=["vt"], writes=["kvb"])
                    p.barrier()
                    if mode == "A":
                        return
                    with ExitStack() as s2:
                        u = sb("u", [128, HALO + T], BF16, s2)
                        zs = sb("zs", [128, T], BF16, s2)
                        t32 = sb("t32", [128, T], F32, s2)
                        dg = sb("dg", [128, 31, 128], BF16, s2)
                        y32 = sb("y32", [128, 512], F32, s2)
                        ybf = sb("ybf", [128, 512], BF16, s2)
                        d32 = sb("d32", [128, 512], F32, s2)
                        r32 = sb("r32", [128, 512], F32, s2)
                        ones128 = sb("ones128", [128, 128], BF16, s2)
                        MEMSET("dve", ones128[:], 1.0 / 128, ["ones128"])
                        for j in range(8):
                            wb, wk = wload(w_in0[6 + 3 * j])
                            for half in range(2):
                                pst, pk = ps[1 + half], PK[1 + half]
                                lin_tile(wb, wk, 16, 128, h_rhs, ["hT"], half * 512, 512, pst, pk)
                                CP("act", t32[:, half * 512:(half + 1) * 512], pst[:, :], [pk], ["t32"])
                            for c in range(16):
                                MM(ps[3][:, 0:HALO], wb[:, c * 128:(c + 1) * 128], hh[:, c, :], c == 0, c == 15,
                                   [wk, "hh"], [PK[3]])
                            wb2, wk2 = wload(w_in0[7 + 3 * j])
                            for c in range(16):
                                MM(ps[4][:, 0:HALO], wb2[:, c * 128:(c + 1) * 128], hh[:, c, :], c == 0, c == 15,
                                   [wk2, "hh"], [PK[4]])
                            ACT(d32[:, 0:HALO], ps[4][:, 0:HALO], AF.Sigmoid, [PK[4]], ["d32"])
                            TT("dve", u[:, 0:HALO], d32[:, 0:HALO], ps[3][:, 0:HALO], ALU.mult, ["d32", PK[3]], ["u"])
                            for half in range(2):
                                pst, pk = ps[1 + half], PK[1 + half]
                                lin_tile(wb2, wk2, 16, 128, h_rhs, ["hT"], half * 512, 512, pst, pk)
                                ACT(d32[:, :], pst[:, :], AF.Sigmoid, [pk], ["d32"])
                                TT("dve", u[:, HALO + half * 512:HALO + (half + 1) * 512], d32[:, :],
                                   t32[:, half * 512:(half + 1) * 512], ALU.mult, ["d32", "t32"], ["u"])
                            wb3, wk3 = wload(w_in0[8 + 3 * j])
                            for half in range(2):
                                pst, pk = ps[1 + half], PK[1 + half]
                                lin_tile(wb3, wk3, 16, 128, h_rhs, ["hT"], half * 512, 512, pst, pk)
                                ACT(zs[:, half * 512:(half + 1) * 512], pst[:, :], AF.Silu, [pk], ["zs"])
                            o_w = PRM["conv_w"][0] + j * 31
                            for k in range(31):
                                TS("pool", dg[:, k, :], ident, prm[:, o_w + k:o_w + k + 1], None, ALU.mult, None,
                                   ["cbf", "prm"], ["dg"])
                            for half in range(2):
                                for k in range(31):
                                    a = HALO + half * 512 - 30 + k
                                    MM(ps[5][:, :], dg[:, k, :], u[:, a:a + 512], k == 0, k == 30, ["dg", "u"], [PK[5]])
                                ACT(y32[:], ps[5][:, :], AF.Identity, [PK[5], "prm"], ["y32"], bias=pc("conv_b", j))
                                CP("dve", ybf[:], y32[:], ["y32"], ["ybf"])
                                MM(ps[6][:, :], ones128[:], ybf[:], True, True, ["ones128", "ybf"], [PK[6]])
                                TT("dve", d32[:], y32[:], ps[6][:, :], ALU.subtract, ["y32", PK[6]], ["d32"])
                                ACT(ybf[:], d32[:], AF.Square, ["d32"], ["ybf"])
                                MM(ps[6][:, :], ones128[:], ybf[:], True, True, ["ones128", "ybf"], [PK[6]])
                                ACT(r32[:], ps[6][:, :], AF.Sqrt, [PK[6], "epsc"], ["r32"], bias=epsc[:])
                                RECIP(r32[:], r32[:], ["r32"], ["r32"])
                                TT("dve", d32[:], d32[:], r32[:], ALU.mult, ["d32", "r32"], ["d32"])
                                ACT(d32[:], d32[:], AF.Silu, ["d32", "prm"], ["d32"], scale=pc("ln_g", j),
                                    bias=pc("ln_b", j))
                                TT("dve", mix[:, j, half * 512:(half + 1) * 512], d32[:],
                                   zs[:, half * 512:(half + 1) * 512], ALU.mult, ["d32", "zs"], [("mix", j)])
                    p.barrier()
                    if CUT == "B":
                        p.muted = True
                    for j in range(8):
                        in0_tile(30 + j, 128, lambda half, pst, pk, j=j: ACT(
                            mix[:, 8 + j, half * 512:(half + 1) * 512], pst[:, :], AF.Silu, [pk], [("mix", 8 + j)]))
                    for j in range(4):
                        in0_tile(38 + j, 128, lambda half, pst, pk, j=j: CP(
                            "act", cq[:, j, half * 512:(half + 1) * 512], pst[:, :], [pk], ["cq"]))
                p.barrier()
                if CUT == "B2":
                    p.muted = True
                qn = sb("qn", [128, 8, T], BF16, s0)
                qpe = sb("qpe", [64, 8, T], BF16, s0)
                with ExitStack() as s2:
                    cqn = sb("cqn", [128, 4, T], BF16, s2)
                    qa = sb("qa", [64, 512], F32, s2)
                    qb = sb("qb", [64, 512], F32, s2)
                    for half in range(2):
                        sl = slice(half * 512, (half + 1) * 512)
                        for j in range(4):
                            ACT(sqb[:, :], cq[:, j, sl], AF.Square, ["cq"], ["sqb"])
                            MM(ps[0][:, :], ones, sqb[:, :], j == 0, j == 3, ["sqb", "cbf"], [PK[0]])
                        ACT(rstd[:, sl], ps[0][:, :], AF.Sqrt, [PK[0], "epsc"], ["rstd"], scale=1.0 / 512,
                            bias=epsc[:])
                        RECIP(rstd[:, sl], rstd[:, sl], ["rstd"], ["rstd"])
                        for j in range(4):
                            STT(cqn[:, j, sl], cq[:, j, sl], pc("q_g", j), rstd[:, sl], ALU.mult, ALU.mult,
                                ["cq", "rstd", "prm"], ["cqn"])
                    scale = 192.0 ** -0.5
                    for h in range(8):
                        wb, wk = wload(w_uq[h], 1024)
                        for half in range(2):
                            sl = slice(half * 512, (half + 1) * 512)
                            for c in range(4):
                                MM(ps[1][:, :], wb[:, c * 256:c * 256 + 128], cqn[:, c, sl], c == 0, c == 3,
                                   [wk, "cqn"], [PK[1]])
                            ACT(qn[:, h, sl], ps[1][:, :], AF.Identity, [PK[1]], [("qn", h)], scale=scale)
                            for c in range(4):
                                MM(ps[2][0:64, :], wb[:, c * 256 + 128:c * 256 + 192], cqn[:, c, sl], c == 0, c == 3,
                                   [wk, "cqn"], [PK[2]])
                            for c in range(4):
                                MM(ps[3][0:64, :], wb[:, c * 256 + 192:c * 256 + 256], cqn[:, c, sl], c == 0, c == 3,
                                   [wk, "cqn"], [PK[3]])
                            TT("dve", qa[:], ps[2][0:64, :], rope[:, 0, sl], ALU.mult, [PK[2], "rope"], ["qa"])
                            TT("dve", qb[:], ps[3][0:64, :], rope[:, 1, sl], ALU.mult, [PK[3], "rope"], ["qb"])
                            TT("dve", qa[:], qa[:], qb[:], ALU.add, ["qa", "qb"], ["qa"])
                            ACT(qpe[:, h, sl], qa[:], AF.Identity, ["qa"], [("qpe", h)], scale=scale)
                p.barrier()
                if CUT == "C":
                    p.muted = True
                with ExitStack() as s2:
                    kng = [sb("kng%d" % i, [128, 3 * T], BF16, s2) for i in range(2)]
                    vg = [sb("vg%d" % i, [128, 24, 128], BF16, s2) for i in range(2)]
                    kpg = sb("kpg", [64, 3 * T], BF16, s2)
                    onesv = sb("onesv", [128, 3, 128], BF16, s2)
                    pT = [sb("pT%d" % i, [128, 512], BF16, s2) for i in range(3)]
                    rl = sb("rl", [128, 512], F32, s2)
                    o32 = sb("o32", [128, 512], F32, s2)
                    for r in range(3):
                        if mode == "F":
                            p.dma("sp", kpg[:, r * T:(r + 1) * T], kpgd[r * 64:(r + 1) * 64, :],
                                  reads=["kpgd"], writes=["kpg"])
                        else:
                            p.dma("sp", kpg[:, r * T:(r + 1) * T], kvg_a[r * KVR + 1024:r * KVR + 1088, :],
                                  reads=["kvg"], writes=["kpg"])
                        TS("dve", onesv[:, r, :], ones, pc("vis", r), None, ALU.mult, None, ["cbf", "prm"], ["onesv"])
                    step = [0]
                    for h in range(8):
                        gi = h % 2
                        for r in range(3):
                            if mode == "F":
                                o_ = r * 256 + (h % 2) * 128
                                p.dma("sp", kng[gi][:, r * T:(r + 1) * T], kg[h // 2][o_:o_ + 128, :],
                                      reads=[("kg", h // 2)], writes=[("kng", gi)])
                                for i in range(4):
                                    p.dma("sp", vg[gi][:, r * 8 + 2 * i:r * 8 + 2 * i + 2, :],
                                          vgd[i][r * 256:(r + 1) * 256, h * 128:(h + 1) * 128].rearrange(
                                              "(t p) n -> p t n", p=128),
                                          reads=[("vgd", i)], writes=[("vg", gi)])
                            else:
                                p.dma("sp", kng[gi][:, r * T:(r + 1) * T],
                                      kvg_a[r * KVR + h * 128:r * KVR + (h + 1) * 128, :],
                                      reads=["kvg"], writes=[("kng", gi)])
                                p.dma("sp", vg[gi][:, r * 8:(r + 1) * 8, :],
                                      kvg_a[r * KVR + 1088:r * KVR + 1088 + 1024, h * 128:(h + 1) * 128].rearrange(
                                          "(t p) n -> p t n", p=128),
                                      reads=["kvg"], writes=[("vg", gi)])
                            TS("pool", vg[gi][:, r * 8:(r + 1) * 8, :], vg[gi][:, r * 8:(r + 1) * 8, :], pc("vis", r),
                               None, ALU.mult, None, [("vg", gi), "prm"], [("vg", gi)])
                        for qh in range(2):
                            q0 = qh * 512
                            tiles = []
                            nown = 4 if qh == 0 else 8
                            for kt in range(nown):
                                lo = max(0, kt * 128 - q0)
                                diag = (kt * 128 >= q0)
                                tiles.append(("own", kt, lo, diag))
                            for g in range(24):
                                tiles.append(("g", g, 0, False))
                            info = {}

                            def qk(ti):
                                kind, kt, lo, diag = tiles[ti]
                                n = 512 - lo
                                si = step[0] % 3
                                step[0] += 1
                                pss, pks = ps[2 + si], PK[2 + si]
                                if kind == "own":
                                    kT = kn[:, h, kt * 128:(kt + 1) * 128]
                                    kP = kpe[:, kt * 128:(kt + 1) * 128]
                                    vv = vt[:, kt, h * 128:(h + 1) * 128]
                                    ov = ones
                                    rk = [("kn", h), "kpe", "vt"]
                                else:
                                    kT = kng[gi][:, kt * 128:(kt + 1) * 128]
                                    kP = kpg[:, kt * 128:(kt + 1) * 128]
                                    vv = vg[gi][:, kt, :]
                                    ov = onesv[:, kt // 8, :]
                                    rk = [("kng", gi), "kpg", ("vg", gi), "onesv"]
                                MM(pss[:, 0:n], kT, qn[:, h, q0 + lo:q0 + 512], True, False, rk + [("qn", h)], [pks])
                                MM(pss[:, 0:n], kP, qpe[:, h, q0 + lo:q0 + 512], False, True, rk + [("qpe", h)], [pks])
                                info[ti] = (si, n, lo, diag, pss, pks, vv, ov, rk)

                            def rest(ti):
                                si, n, lo, diag, pss, pks, vv, ov, rk = info[ti]
                                ACT(pT[si][:, 0:n], pss[:, 0:n], AF.Exp, [pks], [("pT", si)])
                                if diag:
                                    TT("dve", pT[si][:, 0:128], pT[si][:, 0:128], triu, ALU.mult,
                                       [("pT", si), "cbf"], [("pT", si)])
                                first = (ti == 0)
                                last = (ti == len(tiles) - 1)
                                MM(ps[0][:, lo:512], vv, pT[si][:, 0:n], first, last, rk + [("pT", si)], [PK[0]])
                                MM(ps[1][:, lo:512], ov, pT[si][:, 0:n], first, last, rk + ["cbf", ("pT", si)], [PK[1]])

                            qk(0)
                            for ti in range(len(tiles)):
                                if ti + 1 < len(tiles):
                                    qk(ti + 1)
                                rest(ti)
                            CP("act", rl[:], ps[1][:, :], [PK[1]], ["rl"])
                            RECIP(rl[:], rl[:], ["rl"], ["rl"])
                            TT("dve", o32[:], ps[0][:, :], rl[:], ALU.mult, [PK[0], "rl"], ["o32"])
                            TT("dve", mix[:, 8 + h, q0:q0 + 512], o32[:], mix[:, 8 + h, q0:q0 + 512], ALU.mult,
                               ["o32", ("mix", 8 + h)], [("mix", 8 + h)])
            p.barrier()
            if CUT == "D":
                p.muted = True
            out_proj(w_out0, xT, HALO, x1d)
            p.muted = False
            p.barrier()

        def layer1(pass2):
            nsfx[0] = "_p2" if pass2 else "_p1"
            with ExitStack() as s0:
                h1 = sb("h1", [128, 16, T], BF16, s0)
                rmask = sb("rmask", [128, T], F32, s0)
                MEMSET("dve", rmask[:], 1.0, ["rmask"])
                MEMSET("dve", rmask[:].rearrange("p (c t) -> p c t", t=CH)[:, :, 0:1], 0.0, ["rmask"])

                def sink_h(c, a, n, xb, xk, gname):
                    STT(h1[:, c, a:a + n], xb[:, 0:n], pc(gname, c), rstd[:, a:a + n], ALU.mult, ALU.mult,
                        [xk, "rstd", "prm"], ["h1"])
                rms_stream(x1d, "od_g", T, 0, sink_h)

                def h_rhs(c, a, n):
                    return h1[:, c, a:a + n]

                sin_ = None
                if pass2:
                    sin_ = sb("sin", [128, 16, 128], F32, s0)
                    with ExitStack() as s1:
                        G = sb("G", [128, 3, 16, SW], F32, s1)
                        t2 = sb("t2", [128, 128], F32, s1)
                        t3 = sb("t3", [128, 128], F32, s1)
                        for r in range(3):
                            if mode == "F":
                                for i in range(4):
                                    p.dma("sp", G[:, r, 4 * i:4 * i + 4, :],
                                          sgp[i][r * 512:(r + 1) * 512, :].rearrange("(h p) w -> p h w", p=128),
                                          reads=[("sgp", i)], writes=["G"])
                            else:
                                p.dma("sp", G[:, r, :, :],
                                      stg_a[r * 2048:(r + 1) * 2048, :].rearrange("(h p) w -> p h w", p=128),
                                      reads=["stg"], writes=["G"])
                        for h in range(16):
                            S0, S1, S2 = G[:, 0, h, 0:128], G[:, 1, h, 0:128], G[:, 2, h, 0:128]
                            D1, D2 = G[:, 1, h, 128:129], G[:, 2, h, 128:129]
                            STT(t2[:], S0, D1, S1, ALU.mult, ALU.add, ["G"], ["t2"])
                            STT(t3[:], t2[:], D2, S2, ALU.mult, ALU.add, ["G", "t2"], ["t3"])
                            TS("dve", sin_[:, h, :], S0, pc("sel", 1), None, ALU.mult, None, ["G", "prm"], ["sin"])
                            STT(sin_[:, h, :], t2[:], pc("sel", 2), sin_[:, h, :], ALU.mult, ALU.add,
                                ["t2", "prm", "sin"], ["sin"])
                            STT(sin_[:, h, :], t3[:], pc("sel", 3), sin_[:, h, :], ALU.mult, ALU.add,
                                ["t3", "prm", "sin"], ["sin"])
                    p.barrier()

                sg = sb("sg", [128, T], F32, s0)
                ff = sb("ff", [128, T], F32, s0)
                lg = sb("lg", [128, T], F32, s0)
                bb = sb("bb", [128, T], F32, s0)
                khT = sb("khT", [128, T], BF16, s0)
                vT = sb("vT", [128, T], BF16, s0)
                ebc = sb("ebc", [128, NCH], F32, s0)
                kt_all = sb("kt_all", [CH, NCH, 128], BF16, s0)
                vt_all = sb("vt_all", [CH, NCH, 128], BF16, s0)
                psv3 = psv.rearrange("p (j k) -> p j k", k=128)
                ps3b = ps[3][:, :].bitcast(BF16).rearrange("p (j k) -> p j k", k=128)
                ps5b = ps[5][:, :].bitcast(BF16).rearrange("p (j k) -> p j k", k=128)
                banks_k = [(psb, "psb"), (psv3, PK[6])]
                banks_v = [(ps3b, PK[3]), (ps5b, PK[5])]
                S = sb("S", [128, 128], F32, s0)
                sst = sb("sst", [128, SW], F32, s0)
                if pass2:
                    q32 = sb("q32", [128, T], F32, s0)
                    o32 = sb("o32l", [128, T], F32, s0)
                    gs = sb("gs", [128, T], BF16, s0)
                    qtT = sb("qtT", [128, T], BF16, s0)
                    ktT = sb("ktT", [128, T], BF16, s0)
                    qhT = sb("qhT", [128, T], BF16, s0)
                    negr = sb("negr", [128, NCH], F32, s0)
                    Sbf = [sb("Sbf%d" % i, [128, 128], BF16, s0) for i in range(2)]
                    Am_all = sb("Am_all", [CH, NCH, CH], BF16, s0)
                    rs1 = sb("rs1", [128, 512], F32, s0)

                def tile_in(ti, evac):
                    wb, wk = wload(w_in1[ti])
                    for half in range(2):
                        pst, pk = ps[1 + half], PK[1 + half]
                        lin_tile(wb, wk, 16, 128, h_rhs, ["h1"], half * 512, 512, pst, pk)
                        evac(half, pst, pk)

                for h in range(NH):
                    tile_in(4 * h + 1, lambda half, pst, pk: ACT(sg[:, half * 512:(half + 1) * 512], pst[:, :],
                                                                AF.Sigmoid, [pk], ["sg"]))
                    TS("dve", ff[:], sg[:], omlb[:, h:h + 1], lbc[:, h:h + 1], ALU.mult, ALU.add,
                       ["sg", "omlb", "lbc"], ["ff"])
                    if CUT == "L1a":
                        p.muted = True
                    ACT(lg[:], ff[:], AF.Ln, ["ff"], ["lg"])
                    TS("dve", sg[:], ff[:], -1.0, 1.0, ALU.mult, ALU.add, ["ff"], ["sg"])
                    p.op("dve", lambda e: e.tensor_tensor_scan(out=bb[:], data0=rmask[:], data1=lg[:], initial=0.0,
                                                               op0=ALU.mult, op1=ALU.add),
                         ["rmask", "lg"], ["bb"])
                    bb3 = bb[:].rearrange("p (c t) -> p c t", t=CH)
                    if CUT == "L1b":
                        p.muted = True
                    ACT(ebc[:], bb3[:, :, CH - 1], AF.Exp, ["bb"], ["ebc"])
                    for c in range(NCH):
                        cs = slice(c * CH, (c + 1) * CH)
                        ACT(ff[:, cs], bb[:, cs], AF.Exp, ["bb"], ["ff"], scale=-1.0,
                            bias=bb[:, c * CH + CH - 1:c * CH + CH])
                    TT("dve", khT[:], sg[:], ff[:], ALU.mult, ["sg", "ff"], ["khT"])
                    tile_in(4 * h + 2, lambda half, pst, pk: CP("act", vT[:, half * 512:(half + 1) * 512], pst[:, :],
                                                               [pk], ["vT"]))
                    if pass2:
                        tile_in(4 * h + 0, lambda half, pst, pk: CP("act", q32[:, half * 512:(half + 1) * 512],
                                                                   pst[:, :], [pk], ["q32"]))
                        tile_in(4 * h + 3, lambda half, pst, pk: ACT(gs[:, half * 512:(half + 1) * 512], pst[:, :],
                                                                    AF.Silu, [pk], ["gs"]))
                        TS("dve", negr[:], bb3[:, :, CH // 2 - 1], -1.0, None, ALU.mult, None, ["bb"], ["negr"])
                        for c in range(NCH):
                            cs = slice(c * CH, (c + 1) * CH)
                            ACT(ff[:, cs], bb[:, cs], AF.Exp, ["bb", "negr"], ["ff"], bias=negr[:, c:c + 1])
                        TT("dve", qtT[:], q32[:], ff[:], ALU.mult, ["q32", "ff"], ["qtT"])
                        for c in range(NCH):
                            cs = slice(c * CH, (c + 1) * CH)
                            ACT(ff[:, cs], bb[:, cs], AF.Exp, ["bb"], ["ff"], scale=-1.0,
                                bias=bb[:, c * CH + CH // 2 - 1:c * CH + CH // 2])
                        TT("dve", ktT[:], sg[:], ff[:], ALU.mult, ["sg", "ff"], ["ktT"])
                        ACT(ff[:], bb[:], AF.Exp, ["bb"], ["ff"])
                        TT("dve", qhT[:], q32[:], ff[:], ALU.mult, ["q32", "ff"], ["qhT"])
                        CP("dve", S[:], sin_[:, h, :], ["sin"], ["S"])
                        CP("act", Sbf[1][:], sin_[:, h, :], ["sin"], [("Sbf", 1)])
                    else:
                        MEMSET("dve", S[:], 0.0, ["S"])
                    if CUT == "L1c":
                        p.muted = True
                    for g in range(2):
                        bk, keyk = banks_k[g]
                        for j in range(8):
                            c = g * 8 + j
                            TR(bk[0:CH, j, :], khT[:, c * CH:(c + 1) * CH], ["khT"], [keyk])
                        CP("act", kt_all[:, g * 8:(g + 1) * 8, :], bk[0:CH, :, :], [keyk], [("kt", g)])
                        bv, keyv = banks_v[g]
                        for j in range(8):
                            c = g * 8 + j
                            TR(bv[0:CH, j, :], vT[:, c * CH:(c + 1) * CH], ["vT"], [keyv])
                        CP("dve", vt_all[:, g * 8:(g + 1) * 8, :], bv[0:CH, :, :], [keyv], [("vt", g)])
                    if pass2:
                        for g in range(2):
                            bankA, keyA = (ps[4], PK[4]) if g == 0 else (ps[3], PK[3])
                            for j in range(8):
                                c = g * 8 + j
                                cs = slice(c * CH, (c + 1) * CH)
                                MM(bankA[0:CH, j * CH:(j + 1) * CH], ktT[:, cs], qtT[:, cs], True, True,
                                   ["ktT", "qtT"], [keyA])
                            for j in range(8):
                                c = g * 8 + j
                                TT("dve", Am_all[:, c, :], bankA[0:CH, j * CH:(j + 1) * CH], triu[0:CH, 0:CH],
                                   ALU.mult, [keyA, "cbf"], [("Am", c)])
                    for c in range(NCH):
                        cs = slice(c * CH, (c + 1) * CH)
                        g, j = divmod(c, 8)
                        dsb, dsk = ps[1 + c % 2], PK[1 + c % 2]
                        if pass2:
                            ob, ok = (ps[0], PK[0]) if g == 0 else (ps[5], PK[5])
                            MM(ob[:, j * CH:(j + 1) * CH], vt_all[:, c, :], Am_all[:, c, :], True, False,
                               [("vt", g), ("Am", c)], [ok])
                        MM(dsb[:, 0:128], kt_all[:, c, :], vt_all[:, c, :], True, True, [("kt", g), ("vt", g)], [dsk])
                        if pass2:
                            MM(ob[:, j * CH:(j + 1) * CH], Sbf[(c + 1) % 2][:], qhT[:, cs], False, True,
                               [("Sbf", (c + 1) % 2), "qhT"], [ok])
                        STT(S[:], S[:], ebc[:, c:c + 1], dsb[:, 0:128], ALU.mult, ALU.add, ["S", "ebc", dsk], ["S"])
                        if pass2 and c < NCH - 1:
                            CP("act", Sbf[c % 2][:], S[:], ["S"], [("Sbf", c % 2)])
                    if pass2:
                        CP("act", o32[:, 0:512], ps[0][:, :], [PK[0]], ["o32l"])
                        CP("act", o32[:, 512:1024], ps[5][:, :], [PK[5]], ["o32l"])
                    if CUT == "L1d":
                        p.muted = True
                    if not pass2:
                        CP("dve", sst[:, 0:128], S[:], ["S"], ["sst"])
                        if os.environ.get("MK_NOTAIL"):
                            MEMSET("dve", sst[:, 128:129], 0.0, ["sst"])
                        else:
                            TS("dve", sst[:, 128:129], ebc[:, 0:1], 1.0, None, ALU.mult, None, ["ebc", "sst"], ["sst"])
                            for c in range(1, NCH):
                                TT("dve", sst[:, 128:129], sst[:, 128:129], ebc[:, c:c + 1], ALU.mult,
                                   ["ebc", "sst"], ["sst"])
                        if mode == "F":
                            p.dma("sp", sbp[h // 4][(h % 4) * 128:(h % 4 + 1) * 128, 0:129], sst[:, 0:129],
                                  reads=["sst"], writes=[("sbp", h // 4)])
                            if h % 4 == 3:
                                AG(sbp[h // 4], sgp[h // 4], ("sbp", h // 4), ("sgp", h // 4))
                        else:
                            p.dma("sp", stb_a[h * 128:(h + 1) * 128, 0:129], sst[:, 0:129], reads=["sst"],
                                  writes=["stb"])
                    else:
                        for half in range(2):
                            sl = slice(half * 512, (half + 1) * 512)
                            ACT(sqb[:, :], o32[:, sl], AF.Square, ["o32l"], ["sqb"])
                            MM(ps[0][:, :], ones, sqb[:, :], True, True, ["sqb", "cbf"], [PK[0]])
                            ACT(rs1[:], ps[0][:, :], AF.Sqrt, [PK[0], "epsc"], ["rs1"], scale=1.0 / 128, bias=epsc[:])
                            RECIP(rs1[:], rs1[:], ["rs1"], ["rs1"])
                            STT(rs1[:], o32[:, sl], pc("hg"), rs1[:], ALU.mult, ALU.mult, ["o32l", "prm", "rs1"], ["rs1"])
                            TT("dve", mix[:, h, sl], rs1[:], gs[:, sl], ALU.mult, ["rs1", "gs"], [("mix", h)])
            p.barrier()

        def finale():
            out_proj(w_out1, x1d, 0, x2d)
            p.barrier()

            def sink_o(c, a, n, xb, xk, gname):
                STT(xb[:, 0:n], xb[:, 0:n], pc(gname, c), rstd[:, a:a + n], ALU.mult, ALU.mult,
                    [xk, "rstd", "prm"], [xk])
                p.dma("sp", outT[:, c, a:a + n], xb[:, 0:n], reads=[xk], writes=["outT"])
            rms_stream(x2d, "fin_g", T, 0, sink_o)

        if L0:
            layer0()
        if P1:
            layer1(False)
        if P2:
            layer1(True)
            finale()
        p.muted = False
        p.barrier()
        p.op("sp", lambda e: None, reads=(), writes=["done"])
        p.emit(st)
    return nc


def _relay(tile):
    K, n = tile.shape
    return np.ascontiguousarray(tile.reshape(K // 128, 128, n).transpose(1, 0, 2).reshape(128, (K // 128) * n))


def _cols(v):
    n = v.shape[0] // 128
    return np.ascontiguousarray(v.reshape(n, 128).T)


def prepare(inputs):
    f = lambda k: np.asarray(inputs[k], dtype=np.float32)
    x = f("x")
    W = f("ev_w_in")[0]
    a_v, a_g, a_z = W[:, 0:1024], W[:, 1024:2048], W[:, 2048:3072]
    c_q, c_kv, k_pe, b_z = W[:, 3072:3584], W[:, 3584:4096], W[:, 4096:4160], W[:, 4160:5184]
    zpad = np.zeros((2048, 64), np.float32)
    sw = np.concatenate([np.arange(32, 64), np.arange(0, 32)])
    tiles = [c_kv[:, j * 128:(j + 1) * 128] for j in range(4)]
    tiles += [np.concatenate([k_pe, zpad], 1), np.concatenate([k_pe[:, sw], zpad], 1)]
    for j in range(8):
        sl = slice(j * 128, (j + 1) * 128)
        tiles += [a_v[:, sl], a_g[:, sl], a_z[:, sl]]
    tiles += [b_z[:, j * 128:(j + 1) * 128] for j in range(8)]
    tiles += [c_q[:, j * 128:(j + 1) * 128] for j in range(4)]
    w_in0 = np.stack([_relay(t) for t in tiles])
    uq = f("mla_w_uq")[0]
    w_uq = np.stack([_relay(np.concatenate([uq[:, h, 0:128], uq[:, h, 128:192], uq[:, h, 128:192][:, sw]], 1))
                     for h in range(8)])
    ukv = f("mla_w_ukv")[0]
    w_ukvk = np.stack([_relay(ukv[:, h, 0:128]) for h in range(8)])
    vfull = ukv[:, :, 128:256].reshape(512, 1024)
    w_ukvv = np.stack([_relay(vfull[:, 0:512]), _relay(vfull[:, 512:1024])])
    wo0 = f("ev_w_out")[0]
    w_out0 = np.stack([_relay(wo0[:, m * 128:(m + 1) * 128]) for m in range(16)])
    W1 = f("od_w_in")[0]
    t1 = []
    for h in range(16):
        for part in range(4):
            t1.append(W1[:, part * 2048 + h * 128: part * 2048 + (h + 1) * 128])
    w_in1 = np.stack([_relay(t) for t in t1])
    wo1 = f("od_w_out")[0]
    w_out1 = np.stack([_relay(wo1[:, m * 128:(m + 1) * 128]) for m in range(16)])
    prm = np.zeros((128, NPRM), np.float32)

    def put(name, arr):
        o, w = PRM[name]
        prm[:, o:o + w] = arr
    put("ev_g", _cols(f("ev_norm_g")[0]))
    put("conv_b", _cols(f("conv_b")[0]))
    put("ln_g", _cols(f("conv_ln_g")[0]))
    put("ln_b", _cols(f("conv_ln_b")[0]))
    put("q_g", _cols(f("mla_q_norm_g")[0]))
    put("kv_g", _cols(f("mla_kv_norm_g")[0]))
    put("od_g", _cols(f("od_norm_g")[0]))
    put("fin_g", _cols(f("final_norm_g")))
    put("hg", f("hgrn_norm_g")[0].reshape(128, 1))
    lg = f("hgrn_lb_logits")
    put("l0", _cols(lg[0]))
    put("l1", _cols(lg[1]))
    cw = f("conv_w")[0]
    put("conv_w", cw.reshape(31, 8, 128).transpose(2, 1, 0).reshape(128, 8 * 31))
    cbf = np.zeros((128, 3, 128), np.float32)
    cbf[:, 0, :] = np.eye(128)
    cbf[:, 1, :] = 1.0
    cbf[:, 2, :] = np.triu(np.ones((128, 128)))
    cbf = cbf.astype(ml_dtypes.bfloat16)
    inv_freq = (1.0 / (np.float32(10000.0) ** (np.arange(0, 64, 2, dtype=np.float32) / np.float32(64)))).astype(np.float32)
    shared = dict(cbf=cbf, w_in0=w_in0, w_uq=w_uq, w_ukvk=w_ukvk, w_ukvv=w_ukvv, w_out0=w_out0, w_in1=w_in1,
                  w_out1=w_out1)
    per = []
    for c in range(8):
        b, j = c // 4, c % 4
        s0 = j * T
        xs_ = np.zeros((HALO + T, 2048), np.float32)
        if j > 0:
            xs_[:, :] = x[b, s0 - HALO:s0 + T, :]
        else:
            xs_[HALO:, :] = x[b, 0:T, :]
        xT = np.ascontiguousarray(xs_.reshape(HALO + T, 16, 128).transpose(2, 1, 0))
        pos = np.arange(s0, s0 + T, dtype=np.float32)
        ang = pos[:, None] * inv_freq[None, :]
        cs, sn = np.cos(ang).astype(np.float32).T, np.sin(ang).astype(np.float32).T
        rope = np.stack([np.concatenate([cs, cs], 0), np.concatenate([-sn, sn], 0)], 1).astype(np.float32)
        pr = prm.copy()
        o, w = PRM["sel"]
        pr[:, o + j] = 1.0
        o, w = PRM["vis"]
        for r in range(3):
            pr[:, o + r] = 1.0 if r < j else 0.0
        per.append(dict(xT=xT, prm=pr, rope=np.ascontiguousarray(rope)))
    return shared, per


_NC = {}


def _get(mode, l1=True):
    k = (mode, l1)
    if k not in _NC:
        _NC[k] = build(mode, l1)
    return _NC[k]


def _launch(nc, maps):
    return run_bass_kernel_spmd(nc, maps, core_ids=list(range(8))).results


def _assemble(res, name):
    out = np.zeros((2, 4096, 2048), np.float32)
    for c in range(8):
        b, j = c // 4, c % 4
        o = np.asarray(res[c][name])
        out[b, j * T:(j + 1) * T, :] = o.transpose(2, 1, 0).reshape(T, 2048)
    return out


def run_unfused(inputs, upto="C"):
    shared, per = prepare(inputs)
    mA = [dict(prm=per[c]["prm"], cbf=shared["cbf"], xT=per[c]["xT"], rope=per[c]["rope"],
               w_in0=shared["w_in0"][0:6], w_ukvk=shared["w_ukvk"], w_ukvv=shared["w_ukvv"]) for c in range(8)]
    rA = _launch(_get("A"), mA)
    kvg = [np.concatenate([np.asarray(rA[(c // 4) * 4 + r]["kvb"]) for r in range(4)], 0) for c in range(8)]
    mB = [dict(prm=per[c]["prm"], cbf=shared["cbf"], xT=per[c]["xT"], rope=per[c]["rope"], kvg=kvg[c],
               w_in0=shared["w_in0"], w_ukvk=shared["w_ukvk"], w_ukvv=shared["w_ukvv"], w_uq=shared["w_uq"],
               w_out0=shared["w_out0"]) for c in range(8)]
    rB = _launch(_get("B", False), mB)
    if upto == "B0":
        return _assemble(rB, "x1d")
    x1 = [np.asarray(rB[c]["x1d"]) for c in range(8)]
    mP = [dict(prm=per[c]["prm"], cbf=shared["cbf"], x1d=x1[c], w_in1=shared["w_in1"]) for c in range(8)]
    rP = _launch(_get("P"), mP)
    stg = [np.concatenate([np.asarray(rP[(c // 4) * 4 + r]["stb"]) for r in range(4)], 0) for c in range(8)]
    mC = [dict(prm=per[c]["prm"], cbf=shared["cbf"], x1d=x1[c], stg=stg[c],
               w_in1=shared["w_in1"], w_out1=shared["w_out1"]) for c in range(8)]
    rC = _launch(_get("C"), mC)
    return _assemble(rC, "outT")


def run_fused(inputs):
    shared, per = prepare(inputs)
    maps = [dict(prm=per[c]["prm"], cbf=shared["cbf"], xT=per[c]["xT"], rope=per[c]["rope"], **{
        k: shared[k] for k in ("w_in0", "w_ukvk", "w_ukvv", "w_uq", "w_out0", "w_in1", "w_out1")}) for c in range(8)]
    return _assemble(_launch(_get("F"), maps), "outT")


FUSED = True


def kernel(**inputs):
    if FUSED:
        return run_fused(inputs)
    return run_unfused(inputs)
```

```python
import numpy as np
import ml_dtypes
from contextlib import ExitStack
import concourse.bass as bass
import concourse.mybir as mybir
from concourse.bass_utils import run_bass_kernel_spmd

F32 = mybir.dt.float32
BF16 = mybir.dt.bfloat16
AF = mybir.ActivationFunctionType
ALU = mybir.AluOpType
ENGS = ("pe", "act", "dve", "pool", "sp")
T = 1024
HALO = 32
EPS = 1e-6
NEG = -30000.0


class Op:
    __slots__ = ("idx", "eng", "fn", "deps", "dma", "need_inc", "sem", "val", "extra_waits", "cc")

    def __init__(self, idx, eng, fn, deps, dma):
        self.idx = idx
        self.eng = eng
        self.fn = fn
        self.deps = deps
        self.dma = dma
        self.need_inc = False
        self.sem = None
        self.val = 0
        self.extra_waits = []
        self.cc = False


class Prog:
    def __init__(self, nc, n_dma_sems=8):
        self.nc = nc
        self.ops = []
        self.last_w = {}
        self.readers = {}
        self.barrier_deps = []
        self.last_on_eng = {}
        self.dmas_since_barrier = []
        self.n_dma_sems = n_dma_sems

    muted = False

    def op(self, eng, fn, reads=(), writes=(), dma=False):
        if self.muted:
            return -1
        idx = len(self.ops)
        deps = set(self.barrier_deps)
        for k in reads:
            w = self.last_w.get(k)
            if w is not None:
                deps.add(w)
        for k in writes:
            w = self.last_w.get(k)
            if w is not None:
                deps.add(w)
            for r in self.readers.get(k, ()):
                deps.add(r)
        o = Op(idx, eng, fn, deps, dma)
        self.ops.append(o)
        for k in writes:
            self.last_w[k] = idx
            self.readers[k] = []
        for k in reads:
            if k not in writes:
                self.readers.setdefault(k, []).append(idx)
        self.last_on_eng[eng] = idx
        if dma:
            self.dmas_since_barrier.append(idx)
        return idx

    def barrier(self):
        deps = set(self.last_on_eng.values()) | set(self.dmas_since_barrier)
        deps = {d for d in deps if not self.ops[d].cc}
        self.barrier_deps = sorted(deps)
        self.dmas_since_barrier = []
        keep = {k: w for k, w in self.last_w.items() if self.ops[w].cc}
        self.last_w = keep
        self.readers = {k: [] for k in keep}

    def dma(self, q, out, in_, reads=(), writes=(), **kw):
        return self.op(q, lambda e: e.dma_start(out=out, in_=in_, **kw), reads, writes, dma=True)

    def collective(self, fn, reads=(), writes=()):
        i = self.op("pool", fn, reads, writes, dma=True)
        self.ops[i].cc = True
        return i

    def emit(self, stack):
        nc = self.nc
        ops = self.ops
        for o in ops:
            for d in o.deps:
                od = ops[d]
                if od.dma:
                    od.need_inc = True
                elif od.eng == "pe" and o.eng == "pe" and not o.dma:
                    continue
                else:
                    od.need_inc = True
        esem = {e: stack.enter_context(nc.semaphore("s_" + e)) for e in ENGS}
        dsem = {}
        for q in ("sp", "act", "pool"):
            dsem[q] = [stack.enter_context(nc.semaphore("d_%s%d" % (q, i))) for i in range(self.n_dma_sems)]
        ecount = {e: 0 for e in ENGS}
        dcount = {q: [0] * self.n_dma_sems for q in dsem}
        drr = {q: 0 for q in dsem}
        for o in ops:
            if o.cc:
                o.sem = stack.enter_context(nc.semaphore("cc%d" % o.idx))
                o.val = 1
            elif o.dma:
                q = o.eng
                i = drr[q]
                drr[q] = (i + 1) % self.n_dma_sems
                prev = dcount[q][i]
                if prev > 0:
                    o.extra_waits.append((dsem[q][i], prev))
                dcount[q][i] = prev + 16
                o.sem = dsem[q][i]
                o.val = prev + 16
            elif o.need_inc:
                ecount[o.eng] += 1
                o.sem = esem[o.eng]
                o.val = ecount[o.eng]
        engobj = {"pe": "tensor", "act": "scalar", "dve": "vector", "pool": "gpsimd", "sp": "sync"}
        block = stack.enter_context(nc.Block())

        def make(engname):
            def body(eng):
                known = {}
                for o in ops:
                    if o.eng != engname:
                        continue
                    waits = list(o.extra_waits)
                    for d in sorted(o.deps):
                        od = ops[d]
                        if od.sem is None:
                            continue
                        if (not od.dma) and od.eng == "pe" and engname == "pe" and not o.dma:
                            continue
                        waits.append((od.sem, od.val))
                    best = {}
                    for s, v in waits:
                        key = id(s)
                        if key not in best or best[key][1] < v:
                            best[key] = (s, v)
                    for key, (s, v) in best.items():
                        if known.get(key, 0) >= v:
                            continue
                        eng.wait_ge(s, v)
                        known[key] = v
                    ins = o.fn(eng)
                    if o.sem is not None and ins is not None:
                        ins.then_inc(o.sem, 1 if (o.cc or not o.dma) else 16)
            return body

        for engname in ENGS:
            getattr(block, engobj[engname])(make(engname))


PRM = {}
_o = 0
for _n, _w in (("ev_g", 16), ("conv_b", 8), ("ln_g", 8), ("ln_b", 8), ("q_g", 4), ("kv_g", 4),
               ("od_g", 16), ("fin_g", 16), ("hg", 1), ("l0", 16), ("l1", 16), ("conv_w", 8 * 31),
               ("sel", 4), ("vis", 3)):
    PRM[_n] = (_o, _w)
    _o += _w
NPRM = _o

N_IN0 = 42
KVR = 1024 + 64 + 1024
CH = 64
NCH = T // CH
SW = 136


def build(mode, l1=True):
    import os
    CUT = os.environ.get("MK_CUT", "")
    nc = bass.Bass("TRN2", target_bir_lowering=False)
    L0 = mode in ("A", "B", "F")
    P1 = (mode in ("B", "F") and l1) or mode == "P"
    P2 = mode in ("C", "F")
    NH = int(os.environ.get("MK_HEADS", "16"))

    def decl(name, shape, dt, role):
        if role == "in":
            return nc.dram_tensor(name, shape, dt, kind="ExternalInput").ap()
        if role == "out":
            return nc.dram_tensor(name, shape, dt, kind="ExternalOutput").ap()
        return nc.dram_tensor(name, shape, dt).ap()

    prm_d = decl("prm", [128, NPRM], F32, "in")
    cbf_d = decl("cbf", [128, 3, 128], BF16, "in")
    if L0:
        xT = decl("xT", [128, 16, HALO + T], F32, "in")
        rope_d = decl("rope", [64, 2, T], F32, "in")
        w_in0 = decl("w_in0", [6 if mode == "A" else N_IN0, 128, 2048], F32, "in")
        w_ukvk = decl("w_ukvk", [8, 128, 4 * 128], F32, "in")
        w_ukvv = decl("w_ukvv", [2, 128, 4 * 512], F32, "in")
        if mode != "F":
            kvb_a = decl("kvb", [KVR, T], BF16, "out" if mode == "A" else "int")
    if mode in ("B", "F"):
        w_uq = decl("w_uq", [8, 128, 4 * 256], F32, "in")
        w_out0 = decl("w_out0", [16, 128, 2048], F32, "in")
        if mode == "B":
            kvg_a = decl("kvg", [4 * KVR, T], BF16, "in")
    if mode != "A":
        x1d = decl("x1d", [128, 16, T], F32, {"B": "out", "C": "in", "F": "int", "P": "in"}[mode])
    if P1 or P2:
        w_in1 = decl("w_in1", [64, 128, 2048], F32, "in")
    if P1:
        if mode != "F":
            stb_a = decl("stb", [2048, SW], F32, "out")
    if P2:
        if mode == "C":
            stg_a = decl("stg", [4 * 2048, SW], F32, "in")
        w_out1 = decl("w_out1", [16, 128, 2048], F32, "in")
        x2d = decl("x2d", [128, 16, T], F32, "int")
        outT = decl("outT", [128, 16, T], F32, "out")

    RG = [[0, 1, 2, 3], [4, 5, 6, 7]]
    if mode == "F":
        kb = [decl("kb%d" % i, [256, T], BF16, "int") for i in range(4)]
        kg = [decl("kg%d" % i, [1024, T], BF16, "int") for i in range(4)]
        kpb = decl("kpb", [64, T], BF16, "int")
        kpgd = decl("kpgd", [256, T], BF16, "int")
        vb = [decl("vb%d" % i, [256, T], BF16, "int") for i in range(4)]
        vgd = [decl("vgd%d" % i, [1024, T], BF16, "int") for i in range(4)]
        sbp = [decl("sbp%d" % i, [512, SW], F32, "int") for i in range(4)]
        sgp = [decl("sgp%d" % i, [2048, SW], F32, "int") for i in range(4)]

    st = ExitStack()
    with st:
        def AG(in_ap, out_ap, rkey, wkey):
            p.collective(lambda e: e.collective_compute("AllGather", ALU.bypass, replica_groups=RG,
                                                        ins=[in_ap.opt()], outs=[out_ap.opt()]),
                         reads=[rkey], writes=[wkey])

        nsfx = [""]

        def sb(name, shape, dt=F32, stack=None):
            return (stack or st).enter_context(nc.sbuf_tensor("sb_" + name + nsfx[0], shape, dt))

        p = Prog(nc)
        ps = [st.enter_context(nc.psum_tensor("ps%d" % i, [128, 512], F32)) for i in range(7)]
        psb = st.enter_context(nc.psum_tensor("psb", [128, 8, 128], BF16))
        PK = [("ps", i) for i in range(7)]
        psv = ps[6][:, :].bitcast(BF16)

        def MM(out, lhsT, rhs, start, stop, reads, writes):
            p.op("pe", lambda e: e.matmul(out, lhsT=lhsT, rhs=rhs, start=start, stop=stop), reads, writes)

        def TR(out, in_, reads, writes):
            p.op("pe", lambda e: e.transpose(out, in_, ident), reads + ["cbf"], writes)

        def ACT(out, in_, func, reads, writes, scale=1.0, bias=None):
            if bias is None:
                p.op("act", lambda e: e.activation(out=out, in_=in_, func=func, scale=scale), reads, writes)
            else:
                p.op("act", lambda e: e.activation(out=out, in_=in_, func=func, scale=scale, bias=bias), reads, writes)

        def TT(eng, out, in0, in1, op, reads, writes):
            p.op(eng, lambda e: e.tensor_tensor(out=out, in0=in0, in1=in1, op=op), reads, writes)

        def TS(eng, out, in0, s1, s2, op0, op1, reads, writes):
            if s2 is None:
                p.op(eng, lambda e: e.tensor_scalar(out=out, in0=in0, scalar1=s1, scalar2=None, op0=op0), reads, writes)
            else:
                p.op(eng, lambda e: e.tensor_scalar(out=out, in0=in0, scalar1=s1, scalar2=s2, op0=op0, op1=op1), reads, writes)

        def STT(out, in0, scalar, in1, op0, op1, reads, writes):
            p.op("dve", lambda e: e.scalar_tensor_tensor(out=out, in0=in0, scalar=scalar, in1=in1, op0=op0, op1=op1),
                 reads, writes)

        def CP(eng, out, in_, reads, writes):
            if eng == "act":
                p.op("act", lambda e: e.activation(out=out, in_=in_, func=AF.Identity), reads, writes)
            else:
                p.op(eng, lambda e: e.tensor_copy(out=out, in_=in_), reads, writes)

        def RECIP(out, in_, reads, writes):
            p.op("dve", lambda e: e.reciprocal(out=out, in_=in_), reads, writes)

        def MEMSET(eng, ap, val, writes):
            p.op(eng, lambda e: e.memset(ap, val), (), writes)

        prm = sb("prm", [128, NPRM])
        cbf = sb("cbf", [128, 3, 128], BF16)
        epsc = sb("epsc", [128, 1])
        lbc = sb("lbc", [128, 16])
        omlb = sb("omlb", [128, 16])
        p.dma("sp", prm[:], prm_d, writes=["prm"])
        p.dma("sp", cbf[:], cbf_d, writes=["cbf"])
        MEMSET("dve", epsc[:], EPS, ["epsc"])
        ident = cbf[:, 0, :]
        ones = cbf[:, 1, :]
        triu = cbf[:, 2, :]

        def pc(name, j=0, n=1):
            o, w = PRM[name]
            return prm[:, o + j:o + j + n]

        TT("dve", lbc[:], pc("l1", 0, 16), pc("l0", 0, 16), ALU.subtract, ["prm"], ["lbc"])
        ACT(lbc[:], lbc[:], AF.Sigmoid, ["lbc"], ["lbc"])
        TS("dve", omlb[:], lbc[:], -1.0, 1.0, ALU.mult, ALU.add, ["lbc"], ["omlb"])

        wst = [sb("wst%d" % i, [128, 2048]) for i in range(2)]
        wbf = [sb("wbf%d" % i, [128, 2048], BF16) for i in range(2)]
        wctr = [0]

        def wload(src, n=2048):
            i = wctr[0] % 2
            wctr[0] += 1
            p.dma("sp", wst[i][:, 0:n], src, writes=[("wst", i)])
            CP("pool", wbf[i][:, 0:n], wst[i][:, 0:n], [("wst", i)], [("wbf", i)])
            return wbf[i], ("wbf", i)

        mix = sb("mix", [128, 16, T], BF16)
        rstd = sb("rstd", [128, T])
        sqb = sb("sqb", [128, 512], BF16)
        xs = [sb("xs%d" % i, [128, T]) for i in range(2)]

        def rms_stream(src_ap, gname, ncol, col0, sink):
            nh = [(a, min(512, ncol - a)) for a in range(0, ncol, 512)]
            for (a, n) in nh:
                for c in range(16):
                    xb = xs[c % 2]
                    p.dma("sp", xb[:, 0:n], src_ap[:, c, col0 + a:col0 + a + n], writes=[("xs", c % 2)])
                    ACT(sqb[:, 0:n], xb[:, 0:n], AF.Square, [("xs", c % 2)], ["sqb"])
                    MM(ps[0][:, 0:n], ones, sqb[:, 0:n], c == 0, c == 15, ["sqb", "cbf"], [PK[0]])
                ACT(rstd[:, a:a + n], ps[0][:, 0:n], AF.Sqrt, [PK[0], "epsc"], ["rstd"], scale=1.0 / 2048, bias=epsc[:])
                RECIP(rstd[:, a:a + n], rstd[:, a:a + n], ["rstd"], ["rstd"])
                for c in range(16):
                    xb = xs[c % 2]
                    p.dma("sp", xb[:, 0:n], src_ap[:, c, col0 + a:col0 + a + n], writes=[("xs", c % 2)])
                    sink(c, a, n, xb, ("xs", c % 2), gname)

        def lin_tile(wb, wkey, kc, m, rhs_fn, rkeys, a, n, pst, pkey):
            for c in range(kc):
                MM(pst[0:m, 0:n], wb[:, c * 128:c * 128 + m], rhs_fn(c, a, n), c == 0, c == kc - 1,
                   [wkey] + rkeys, [pkey])

        def out_proj(w_d, src_ap, col0, dst_ap):
            for m in range(16):
                wb, wk = wload(w_d[m])
                xb = xs[m % 2]
                p.dma("sp", xb[:, :], src_ap[:, m, col0:col0 + T], writes=[("xs", m % 2)])
                for half in range(2):
                    pst, pk = ps[1 + half], PK[1 + half]
                    for c in range(16):
                        MM(pst[:, :], wb[:, c * 128:(c + 1) * 128], mix[:, c, half * 512:(half + 1) * 512],
                           c == 0, c == 15, [wk] + [("mix", c)], [pk])
                    TT("dve", xb[:, half * 512:(half + 1) * 512], xb[:, half * 512:(half + 1) * 512], pst[:, :],
                       ALU.add, [("xs", m % 2), pk], [("xs", m % 2)])
                p.dma("sp", dst_ap[:, m, :], xb[:, :], reads=[("xs", m % 2)], writes=["x_out"])

        def layer0():
            with ExitStack() as s0:
                rope = sb("rope", [64, 2, T], F32, s0)
                p.dma("sp", rope[:], rope_d, writes=["rope"])
                kn = sb("kn", [128, 8, T], BF16, s0)
                vt = sb("vt", [128, 8, T], BF16, s0)
                kpe = sb("kpe", [64, T], BF16, s0)
                cq = sb("cq", [128, 4, T], F32, s0)
                with ExitStack() as s1:
                    hT = sb("hT", [128, 16, T], BF16, s1)
                    hh = sb("hh", [128, 16, HALO], BF16, s1)
                    rsth = sb("rsth", [128, HALO], F32, s1)

                    def sink_h(c, a, n, xb, xk, gname):
                        STT(hT[:, c, a:a + n], xb[:, 0:n], pc(gname, c), rstd[:, a:a + n], ALU.mult, ALU.mult,
                            [xk, "rstd", "prm"], ["hT"])
                    rms_stream(xT, "ev_g", T, HALO, sink_h)
                    if mode != "A":
                        for c in range(16):
                            xb = xs[c % 2]
                            p.dma("sp", xb[:, 0:HALO], xT[:, c, 0:HALO], writes=[("xs", c % 2)])
                            ACT(sqb[:, 0:HALO], xb[:, 0:HALO], AF.Square, [("xs", c % 2)], ["sqb"])
                            MM(ps[0][:, 0:HALO], ones, sqb[:, 0:HALO], c == 0, c == 15, ["sqb", "cbf"], [PK[0]])
                        ACT(rsth[:], ps[0][:, 0:HALO], AF.Sqrt, [PK[0], "epsc"], ["rsth"], scale=1.0 / 2048,
                            bias=epsc[:])
                        RECIP(rsth[:], rsth[:], ["rsth"], ["rsth"])
                        for c in range(16):
                            xb = xs[c % 2]
                            p.dma("sp", xb[:, 0:HALO], xT[:, c, 0:HALO], writes=[("xs", c % 2)])
                            STT(hh[:, c, :], xb[:, 0:HALO], pc("ev_g", c), rsth[:], ALU.mult, ALU.mult,
                                [("xs", c % 2), "rsth", "prm"], ["hh"])

                    def h_rhs(c, a, n):
                        return hT[:, c, a:a + n]

                    def in0_tile(ti, m, evac):
                        wb, wk = wload(w_in0[ti])
                        for half in range(2):
                            pst, pk = ps[1 + half], PK[1 + half]
                            lin_tile(wb, wk, 16, m, h_rhs, ["hT"], half * 512, 512, pst, pk)
                            evac(half, pst, pk)
                        return wb, wk

                    with ExitStack() as s2:
                        ckv = sb("ckv", [128, 4, T], F32, s2)
                        ckn = sb("ckn", [128, 4, T], BF16, s2)
                        kp32 = sb("kp32", [64, 2, T], F32, s2)
                        for j in range(4):
                            in0_tile(j, 128, lambda half, pst, pk, j=j: CP(
                                "act", ckv[:, j, half * 512:(half + 1) * 512], pst[:, :], [pk], ["ckv"]))
                        for j in range(2):
                            in0_tile(4 + j, 64, lambda half, pst, pk, j=j: CP(
                                "act", kp32[:, j, half * 512:(half + 1) * 512], pst[0:64, :], [pk], ["kp32"]))
                        for half in range(2):
                            sl = slice(half * 512, (half + 1) * 512)
                            for j in range(4):
                                ACT(sqb[:, :], ckv[:, j, sl], AF.Square, ["ckv"], ["sqb"])
                                MM(ps[0][:, :], ones, sqb[:, :], j == 0, j == 3, ["sqb", "cbf"], [PK[0]])
                            ACT(rstd[:, sl], ps[0][:, :], AF.Sqrt, [PK[0], "epsc"], ["rstd"], scale=1.0 / 512,
                                bias=epsc[:])
                            RECIP(rstd[:, sl], rstd[:, sl], ["rstd"], ["rstd"])
                            for j in range(4):
                                STT(ckn[:, j, sl], ckv[:, j, sl], pc("kv_g", j), rstd[:, sl], ALU.mult, ALU.mult,
                                    ["ckv", "rstd", "prm"], ["ckn"])
                        TT("dve", kp32[:, 0, :], kp32[:, 0, :], rope[:, 0, :], ALU.mult, ["kp32", "rope"], ["kp32"])
                        TT("dve", kp32[:, 1, :], kp32[:, 1, :], rope[:, 1, :], ALU.mult, ["kp32", "rope"], ["kp32"])
                        TT("dve", kpe[:, :], kp32[:, 0, :], kp32[:, 1, :], ALU.add, ["kp32"], ["kpe"])
                        if mode == "F":
                            p.dma("sp", kpb, kpe[:, :], reads=["kpe"], writes=["kpb"])
                            AG(kpb, kpgd, "kpb", "kpgd")
                        else:
                            p.dma("sp", kvb_a[1024:1088, :], kpe[:, :], reads=["kpe"], writes=["kvb"])
                        for h in range(8):
                            wb, wk = wload(w_ukvk[h], 512)
                            for half in range(2):
                                pst, pk = ps[1 + half], PK[1 + half]
                                for c in range(4):
                                    MM(pst[:, :], wb[:, c * 128:(c + 1) * 128], ckn[:, c, half * 512:(half + 1) * 512],
                                       c == 0, c == 3, [wk, "ckn"], [pk])
                                CP("act", kn[:, h, half * 512:(half + 1) * 512], pst[:, :], [pk], [("kn", h)])
                            if mode == "F":
                                p.dma("sp", kb[h // 2][(h % 2) * 128:(h % 2 + 1) * 128, :], kn[:, h, :],
                                      reads=[("kn", h)], writes=[("kb", h // 2)])
                                if h % 2 == 1:
                                    AG(kb[h // 2], kg[h // 2], ("kb", h // 2), ("kg", h // 2))
                            else:
                                p.dma("sp", kvb_a[h * 128:(h + 1) * 128, :], kn[:, h, :], reads=[("kn", h)],
                                      writes=["kvb"])
                        for vh in range(2):
                            wb, wk = wload(w_ukvv[vh])
                            for tt in range(8):
                                pst, pk = ps[1 + tt % 2], PK[1 + tt % 2]
                                for c in range(4):
                                    MM(pst[:, :], ckn[:, c, tt * 128:(tt + 1) * 128], wb[:, c * 512:(c + 1) * 512],
                                       c == 0, c == 3, [wk, "ckn"], [pk])
                                CP("act", vt[:, tt, vh * 512:(vh + 1) * 512], pst[:, :], [pk], ["vt"])
                        if mode == "F":
                            for i in range(4):
                                p.dma("sp", vb[i].rearrange("(t p) n -> p t n", p=128), vt[:, 2 * i:2 * i + 2, :],
                                      reads=["vt"], writes=[("vb", i)])
                                AG(vb[i], vgd[i], ("vb", i), ("vgd", i))
                        else:
                            p.dma("sp", kvb_a[1088:1088 + 1024, :].rearrange("(t p) n -> p t n", p=128), vt[:, :, :],
                                  reads=["vt"], writes=["kvb"])
                    p.barrier()
                    if mode == "A":
                        return
                    with ExitStack() as s2:
                        u = sb("u", [128, HALO + T], BF16, s2)
                        zs = sb("zs", [128, T], BF16, s2)
                        t32 = sb("t32", [128, T], F32, s2)
                        dg = sb("dg", [128, 31, 128], BF16, s2)
                        y32 = sb("y32", [128, 512], F32, s2)
                        ybf = sb("ybf", [128, 512], BF16, s2)
                        d32 = sb("d32", [128, 512], F32, s2)
                        r32 = sb("r32", [128, 512], F32, s2)
                        ones128 = sb("ones128", [128, 128], BF16, s2)
                        MEMSET("dve", ones128[:], 1.0 / 128, ["ones128"])
                        for j in range(8):
                            wb, wk = wload(w_in0[6 + 3 * j])
                            for half in range(2):
                                pst, pk = ps[1 + half], PK[1 + half]
                                lin_tile(wb, wk, 16, 128, h_rhs, ["hT"], half * 512, 512, pst, pk)
                                CP("act", t32[:, half * 512:(half + 1) * 512], pst[:, :], [pk], ["t32"])
                            for c in range(16):
                                MM(ps[3][:, 0:HALO], wb[:, c * 128:(c + 1) * 128], hh[:, c, :], c == 0, c == 15,
                                   [wk, "hh"], [PK[3]])
                            wb2, wk2 = wload(w_in0[7 + 3 * j])
                            for c in range(16):
                                MM(ps[4][:, 0:HALO], wb2[:, c * 128:(c + 1) * 128], hh[:, c, :], c == 0, c == 15,
                                   [wk2, "hh"], [PK[4]])
                            ACT(d32[:, 0:HALO], ps[4][:, 0:HALO], AF.Sigmoid, [PK[4]], ["d32"])
                            TT("dve", u[:, 0:HALO], d32[:, 0:HALO], ps[3][:, 0:HALO], ALU.mult, ["d32", PK[3]], ["u"])
                            for half in range(2):
                                pst, pk = ps[1 + half], PK[1 + half]
                                lin_tile(wb2, wk2, 16, 128, h_rhs, ["hT"], half * 512, 512, pst, pk)
                                ACT(d32[:, :], pst[:, :], AF.Sigmoid, [pk], ["d32"])
                                TT("dve", u[:, HALO + half * 512:HALO + (half + 1) * 512], d32[:, :],
                                   t32[:, half * 512:(half + 1) * 512], ALU.mult, ["d32", "t32"], ["u"])
                            wb3, wk3 = wload(w_in0[8 + 3 * j])
                            for half in range(2):
                                pst, pk = ps[1 + half], PK[1 + half]
                                lin_tile(wb3, wk3, 16, 128, h_rhs, ["hT"], half * 512, 512, pst, pk)
                                ACT(zs[:, half * 512:(half + 1) * 512], pst[:, :], AF.Silu, [pk], ["zs"])
                            o_w = PRM["conv_w"][0] + j * 31
                            for k in range(31):
                                TS("pool", dg[:, k, :], ident, prm[:, o_w + k:o_w + k + 1], None, ALU.mult, None,
                                   ["cbf", "prm"], ["dg"])
                            for half in range(2):
                                for k in range(31):
                                    a = HALO + half * 512 - 30 + k
                                    MM(ps[5][:, :], dg[:, k, :], u[:, a:a + 512], k == 0, k == 30, ["dg", "u"], [PK[5]])
                                ACT(y32[:], ps[5][:, :], AF.Identity, [PK[5], "prm"], ["y32"], bias=pc("conv_b", j))
                                CP("dve", ybf[:], y32[:], ["y32"], ["ybf"])
                                MM(ps[6][:, :], ones128[:], ybf[:], True, True, ["ones128", "ybf"], [PK[6]])
                                TT("dve", d32[:], y32[:], ps[6][:, :], ALU.subtract, ["y32", PK[6]], ["d32"])
                                ACT(ybf[:], d32[:], AF.Square, ["d32"], ["ybf"])
                                MM(ps[6][:, :], ones128[:], ybf[:], True, True, ["ones128", "ybf"], [PK[6]])
                                ACT(r32[:], ps[6][:, :], AF.Sqrt, [PK[6], "epsc"], ["r32"], bias=epsc[:])
                                RECIP(r32[:], r32[:], ["r32"], ["r32"])
                                TT("dve", d32[:], d32[:], r32[:], ALU.mult, ["d32", "r32"], ["d32"])
                                ACT(d32[:], d32[:], AF.Silu, ["d32", "prm"], ["d32"], scale=pc("ln_g", j),
                                    bias=pc("ln_b", j))
                                TT("dve", mix[:, j, half * 512:(half + 1) * 512], d32[:],
                                   zs[:, half * 512:(half + 1) * 512], ALU.mult, ["d32", "zs"], [("mix", j)])
                    p.barrier()
                    if CUT == "B":
                        p.muted = True
                    for j in range(8):
                        in0_tile(30 + j, 128, lambda half, pst, pk, j=j: ACT(
                            mix[:, 8 + j, half * 512:(half + 1) * 512], pst[:, :], AF.Silu, [pk], [("mix", 8 + j)]))
                    for j in range(4):
                        in0_tile(38 + j, 128, lambda half, pst, pk, j=j: CP(
                            "act", cq[:, j, half * 512:(half + 1) * 512], pst[:, :], [pk], ["cq"]))
                p.barrier()
                if CUT == "B2":
                    p.muted = True
                qn = sb("qn", [128, 8, T], BF16, s0)
                qpe = sb("qpe", [64, 8, T], BF16, s0)
                with ExitStack() as s2:
                    cqn = sb("cqn", [128, 4, T], BF16, s2)
                    qa = sb("qa", [64, 512], F32, s2)
                    qb = sb("qb", [64, 512], F32, s2)
                    for half in range(2):
                        sl = slice(half * 512, (half + 1) * 512)
                        for j in range(4):
                            ACT(sqb[:, :], cq[:, j, sl], AF.Square, ["cq"], ["sqb"])
                            MM(ps[0][:, :], ones, sqb[:, :], j == 0, j == 3, ["sqb", "cbf"], [PK[0]])
                        ACT(rstd[:, sl], ps[0][:, :], AF.Sqrt, [PK[0], "epsc"], ["rstd"], scale=1.0 / 512,
                            bias=epsc[:])
                        RECIP(rstd[:, sl], rstd[:, sl], ["rstd"], ["rstd"])
                        for j in range(4):
                            STT(cqn[:, j, sl], cq[:, j, sl], pc("q_g", j), rstd[:, sl], ALU.mult, ALU.mult,
                                ["cq", "rstd", "prm"], ["cqn"])
                    scale = 192.0 ** -0.5
                    for h in range(8):
                        wb, wk = wload(w_uq[h], 1024)
                        for half in range(2):
                            sl = slice(half * 512, (half + 1) * 512)
                            for c in range(4):
                                MM(ps[1][:, :], wb[:, c * 256:c * 256 + 128], cqn[:, c, sl], c == 0, c == 3,
                                   [wk, "cqn"], [PK[1]])
                            ACT(qn[:, h, sl], ps[1][:, :], AF.Identity, [PK[1]], [("qn", h)], scale=scale)
                            for c in range(4):
                                MM(ps[2][0:64, :], wb[:, c * 256 + 128:c * 256 + 192], cqn[:, c, sl], c == 0, c == 3,
                                   [wk, "cqn"], [PK[2]])
                            for c in range(4):
                                MM(ps[3][0:64, :], wb[:, c * 256 + 192:c * 256 + 256], cqn[:, c, sl], c == 0, c == 3,
                                   [wk, "cqn"], [PK[3]])
                            TT("dve", qa[:], ps[2][0:64, :], rope[:, 0, sl], ALU.mult, [PK[2], "rope"], ["qa"])
                            TT("dve", qb[:], ps[3][0:64, :], rope[:, 1, sl], ALU.mult, [PK[3], "rope"], ["qb"])
                            TT("dve", qa[:], qa[:], qb[:], ALU.add, ["qa", "qb"], ["qa"])
                            ACT(qpe[:, h, sl], qa[:], AF.Identity, ["qa"], [("qpe", h)], scale=scale)
                p.barrier()
                if CUT == "C":
                    p.muted = True
                with ExitStack() as s2:
                    kng = [sb("kng%d" % i, [128, 3 * T], BF16, s2) for i in range(2)]
                    vg = [sb("vg%d" % i, [128, 24, 128], BF16, s2) for i in range(2)]
                    kpg = sb("kpg", [64, 3 * T], BF16, s2)
                    onesv = sb("onesv", [128, 3, 128], BF16, s2)
                    pT = [sb("pT%d" % i, [128, 512], BF16, s2) for i in range(3)]
                    rl = sb("rl", [128, 512], F32, s2)
                    o32 = sb("o32", [128, 512], F32, s2)
                    for r in range(3):
                        if mode == "F":
                            p.dma("sp", kpg[:, r * T:(r + 1) * T], kpgd[r * 64:(r + 1) * 64, :],
                                  reads=["kpgd"], writes=["kpg"])
                        else:
                            p.dma("sp", kpg[:, r * T:(r + 1) * T], kvg_a[r * KVR + 1024:r * KVR + 1088, :],
                                  reads=["kvg"], writes=["kpg"])
                        TS("dve", onesv[:, r, :], ones, pc("vis", r), None, ALU.mult, None, ["cbf", "prm"], ["onesv"])
                    step = [0]
                    for h in range(8):
                        gi = h % 2
                        for r in range(3):
                            if mode == "F":
                                o_ = r * 256 + (h % 2) * 128
                                p.dma("sp", kng[gi][:, r * T:(r + 1) * T], kg[h // 2][o_:o_ + 128, :],
                                      reads=[("kg", h // 2)], writes=[("kng", gi)])
                                for i in range(4):
                                    p.dma("sp", vg[gi][:, r * 8 + 2 * i:r * 8 + 2 * i + 2, :],
                                          vgd[i][r * 256:(r + 1) * 256, h * 128:(h + 1) * 128].rearrange(
                                              "(t p) n -> p t n", p=128),
                                          reads=[("vgd", i)], writes=[("vg", gi)])
                            else:
                                p.dma("sp", kng[gi][:, r * T:(r + 1) * T],
                                      kvg_a[r * KVR + h * 128:r * KVR + (h + 1) * 128, :],
                                      reads=["kvg"], writes=[("kng", gi)])
                                p.dma("sp", vg[gi][:, r * 8:(r + 1) * 8, :],
                                      kvg_a[r * KVR + 1088:r * KVR + 1088 + 1024, h * 128:(h + 1) * 128].rearrange(
                                          "(t p) n -> p t n", p=128),
                                      reads=["kvg"], writes=[("vg", gi)])
                            TS("pool", vg[gi][:, r * 8:(r + 1) * 8, :], vg[gi][:, r * 8:(r + 1) * 8, :], pc("vis", r),
                               None, ALU.mult, None, [("vg", gi), "prm"], [("vg", gi)])
                        for qh in range(2):
                            q0 = qh * 512
                            tiles = []
                            nown = 4 if qh == 0 else 8
                            for kt in range(nown):
                                lo = max(0, kt * 128 - q0)
                                diag = (kt * 128 >= q0)
                                tiles.append(("own", kt, lo, diag))
                            for g in range(24):
                                tiles.append(("g", g, 0, False))
                            info = {}

                            def qk(ti):
                                kind, kt, lo, diag = tiles[ti]
                                n = 512 - lo
                                si = step[0] % 3
                                step[0] += 1
                                pss, pks = ps[2 + si], PK[2 + si]
                                if kind == "own":
                                    kT = kn[:, h, kt * 128:(kt + 1) * 128]
                                    kP = kpe[:, kt * 128:(kt + 1) * 128]
                                    vv = vt[:, kt, h * 128:(h + 1) * 128]
                                    ov = ones
                                    rk = [("kn", h), "kpe", "vt"]
                                else:
                                    kT = kng[gi][:, kt * 128:(kt + 1) * 128]
                                    kP = kpg[:, kt * 128:(kt + 1) * 128]
                                    vv = vg[gi][:, kt, :]
                                    ov = onesv[:, kt // 8, :]
                                    rk = [("kng", gi), "kpg", ("vg", gi), "onesv"]
                                MM(pss[:, 0:n], kT, qn[:, h, q0 + lo:q0 + 512], True, False, rk + [("qn", h)], [pks])
                                MM(pss[:, 0:n], kP, qpe[:, h, q0 + lo:q0 + 512], False, True, rk + [("qpe", h)], [pks])
                                info[ti] = (si, n, lo, diag, pss, pks, vv, ov, rk)

                            def rest(ti):
                                si, n, lo, diag, pss, pks, vv, ov, rk = info[ti]
                                ACT(pT[si][:, 0:n], pss[:, 0:n], AF.Exp, [pks], [("pT", si)])
                                if diag:
                                    TT("dve", pT[si][:, 0:128], pT[si][:, 0:128], triu, ALU.mult,
                                       [("pT", si), "cbf"], [("pT", si)])
                                first = (ti == 0)
                                last = (ti == len(tiles) - 1)
                                MM(ps[0][:, lo:512], vv, pT[si][:, 0:n], first, last, rk + [("pT", si)], [PK[0]])
                                MM(ps[1][:, lo:512], ov, pT[si][:, 0:n], first, last, rk + ["cbf", ("pT", si)], [PK[1]])

                            qk(0)
                            for ti in range(len(tiles)):
                                if ti + 1 < len(tiles):
                                    qk(ti + 1)
                                rest(ti)
                            CP("act", rl[:], ps[1][:, :], [PK[1]], ["rl"])
                            RECIP(rl[:], rl[:], ["rl"], ["rl"])
                            TT("dve", o32[:], ps[0][:, :], rl[:], ALU.mult, [PK[0], "rl"], ["o32"])
                            TT("dve", mix[:, 8 + h, q0:q0 + 512], o32[:], mix[:, 8 + h, q0:q0 + 512], ALU.mult,
                               ["o32", ("mix", 8 + h)], [("mix", 8 + h)])
            p.barrier()
            if CUT == "D":
                p.muted = True
            out_proj(w_out0, xT, HALO, x1d)
            p.muted = False
            p.barrier()

        def layer1(pass2):
            nsfx[0] = "_p2" if pass2 else "_p1"
            with ExitStack() as s0:
                h1 = sb("h1", [128, 16, T], BF16, s0)
                rmask = sb("rmask", [128, T], F32, s0)
                MEMSET("dve", rmask[:], 1.0, ["rmask"])
                MEMSET("dve", rmask[:].rearrange("p (c t) -> p c t", t=CH)[:, :, 0:1], 0.0, ["rmask"])

                def sink_h(c, a, n, xb, xk, gname):
                    STT(h1[:, c, a:a + n], xb[:, 0:n], pc(gname, c), rstd[:, a:a + n], ALU.mult, ALU.mult,
                        [xk, "rstd", "prm"], ["h1"])
                rms_stream(x1d, "od_g", T, 0, sink_h)

                def h_rhs(c, a, n):
                    return h1[:, c, a:a + n]

                sin_ = None
                if pass2:
                    sin_ = sb("sin", [128, 16, 128], F32, s0)
                    with ExitStack() as s1:
                        G = sb("G", [128, 3, 16, SW], F32, s1)
                        t2 = sb("t2", [128, 128], F32, s1)
                        t3 = sb("t3", [128, 128], F32, s1)
                        for r in range(3):
                            if mode == "F":
                                for i in range(4):
                                    p.dma("sp", G[:, r, 4 * i:4 * i + 4, :],
                                          sgp[i][r * 512:(r + 1) * 512, :].rearrange("(h p) w -> p h w", p=128),
                                          reads=[("sgp", i)], writes=["G"])
                            else:
                                p.dma("sp", G[:, r, :, :],
                                      stg_a[r * 2048:(r + 1) * 2048, :].rearrange("(h p) w -> p h w", p=128),
                                      reads=["stg"], writes=["G"])
                        for h in range(16):
                            S0, S1, S2 = G[:, 0, h, 0:128], G[:, 1, h, 0:128], G[:, 2, h, 0:128]
                            D1, D2 = G[:, 1, h, 128:129], G[:, 2, h, 128:129]
                            STT(t2[:], S0, D1, S1, ALU.mult, ALU.add, ["G"], ["t2"])
                            STT(t3[:], t2[:], D2, S2, ALU.mult, ALU.add, ["G", "t2"], ["t3"])
                            TS("dve", sin_[:, h, :], S0, pc("sel", 1), None, ALU.mult, None, ["G", "prm"], ["sin"])
                            STT(sin_[:, h, :], t2[:], pc("sel", 2), sin_[:, h, :], ALU.mult, ALU.add,
                                ["t2", "prm", "sin"], ["sin"])
                            STT(sin_[:, h, :], t3[:], pc("sel", 3), sin_[:, h, :], ALU.mult, ALU.add,
                                ["t3", "prm", "sin"], ["sin"])
                    p.barrier()

                sg = sb("sg", [128, T], F32, s0)
                ff = sb("ff", [128, T], F32, s0)
                lg = sb("lg", [128, T], F32, s0)
                bb = sb("bb", [128, T], F32, s0)
                khT = sb("khT", [128, T], BF16, s0)
                vT = sb("vT", [128, T], BF16, s0)
                ebc = sb("ebc", [128, NCH], F32, s0)
                kt_all = sb("kt_all", [CH, NCH, 128], BF16, s0)
                vt_all = sb("vt_all", [CH, NCH, 128], BF16, s0)
                psv3 = psv.rearrange("p (j k) -> p j k", k=128)
                ps3b = ps[3][:, :].bitcast(BF16).rearrange("p (j k) -> p j k", k=128)
                ps5b = ps[5][:, :].bitcast(BF16).rearrange("p (j k) -> p j k", k=128)
                banks_k = [(psb, "psb"), (psv3, PK[6])]
                banks_v = [(ps3b, PK[3]), (ps5b, PK[5])]
                S = sb("S", [128, 128], F32, s0)
                sst = sb("sst", [128, SW], F32, s0)
                if pass2:
                    q32 = sb("q32", [128, T], F32, s0)
                    o32 = sb("o32l", [128, T], F32, s0)
                    gs = sb("gs", [128, T], BF16, s0)
                    qtT = sb("qtT", [128, T], BF16, s0)
                    ktT = sb("ktT", [128, T], BF16, s0)
                    qhT = sb("qhT", [128, T], BF16, s0)
                    negr = sb("negr", [128, NCH], F32, s0)
                    Sbf = [sb("Sbf%d" % i, [128, 128], BF16, s0) for i in range(2)]
                    Am_all = sb("Am_all", [CH, NCH, CH], BF16, s0)
                    rs1 = sb("rs1", [128, 512], F32, s0)

                def tile_in(ti, evac):
                    wb, wk = wload(w_in1[ti])
                    for half in range(2):
                        pst, pk = ps[1 + half], PK[1 + half]
                        lin_tile(wb, wk, 16, 128, h_rhs, ["h1"], half * 512, 512, pst, pk)
                        evac(half, pst, pk)

                for h in range(NH):
                    tile_in(4 * h + 1, lambda half, pst, pk: ACT(sg[:, half * 512:(half + 1) * 512], pst[:, :],
                                                                AF.Sigmoid, [pk], ["sg"]))
                    TS("dve", ff[:], sg[:], omlb[:, h:h + 1], lbc[:, h:h + 1], ALU.mult, ALU.add,
                       ["sg", "omlb", "lbc"], ["ff"])
                    if CUT == "L1a":
                        p.muted = True
                    ACT(lg[:], ff[:], AF.Ln, ["ff"], ["lg"])
                    TS("dve", sg[:], ff[:], -1.0, 1.0, ALU.mult, ALU.add, ["ff"], ["sg"])
                    p.op("dve", lambda e: e.tensor_tensor_scan(out=bb[:], data0=rmask[:], data1=lg[:], initial=0.0,
                                                               op0=ALU.mult, op1=ALU.add),
                         ["rmask", "lg"], ["bb"])
                    bb3 = bb[:].rearrange("p (c t) -> p c t", t=CH)
                    if CUT == "L1b":
                        p.muted = True
                    ACT(ebc[:], bb3[:, :, CH - 1], AF.Exp, ["bb"], ["ebc"])
                    lg3 = lg[:].rearrange("p (c t) -> p c t", t=CH)
                    TT("dve", lg3[:, :, 0], lg3[:, :, 0], bb3[:, :, CH - 1], ALU.subtract, ["lg", "bb"], ["lg"])
                    p.op("dve", lambda e: e.tensor_tensor_scan(out=ff[:], data0=rmask[:], data1=lg[:], initial=0.0,
                                                               op0=ALU.mult, op1=ALU.add),
                         ["rmask", "lg"], ["ff"])
                    ACT(ff[:], ff[:], AF.Exp, ["ff"], ["ff"], scale=-1.0)
                    TT("dve", khT[:], sg[:], ff[:], ALU.mult, ["sg", "ff"], ["khT"])
                    tile_in(4 * h + 2, lambda half, pst, pk: CP("act", vT[:, half * 512:(half + 1) * 512], pst[:, :],
                                                               [pk], ["vT"]))
                    if pass2:
                        tile_in(4 * h + 0, lambda half, pst, pk: CP("act", q32[:, half * 512:(half + 1) * 512],
                                                                   pst[:, :], [pk], ["q32"]))
                        tile_in(4 * h + 3, lambda half, pst, pk: ACT(gs[:, half * 512:(half + 1) * 512], pst[:, :],
                                                                    AF.Silu, [pk], ["gs"]))
                        TT("dve", negr[:], bb3[:, :, CH - 1], bb3[:, :, CH // 2 - 1], ALU.subtract, ["bb"], ["negr"])
                        TT("dve", lg3[:, :, 0], lg3[:, :, 0], negr[:], ALU.add, ["lg", "negr"], ["lg"])
                        p.op("dve", lambda e: e.tensor_tensor_scan(out=ff[:], data0=rmask[:], data1=lg[:],
                                                                   initial=0.0, op0=ALU.mult, op1=ALU.add),
                             ["rmask", "lg"], ["ff"])
                        ACT(lg[:], ff[:], AF.Exp, ["ff"], ["lg"])
                        TT("dve", qtT[:], q32[:], lg[:], ALU.mult, ["q32", "lg"], ["qtT"])
                        ACT(lg[:], ff[:], AF.Exp, ["ff"], ["lg"], scale=-1.0)
                        TT("dve", ktT[:], sg[:], lg[:], ALU.mult, ["sg", "lg"], ["ktT"])
                        ACT(ff[:], bb[:], AF.Exp, ["bb"], ["ff"])
                        TT("dve", qhT[:], q32[:], ff[:], ALU.mult, ["q32", "ff"], ["qhT"])
                        CP("dve", S[:], sin_[:, h, :], ["sin"], ["S"])
                        CP("act", Sbf[1][:], sin_[:, h, :], ["sin"], [("Sbf", 1)])
                    else:
                        MEMSET("dve", S[:], 0.0, ["S"])
                    if CUT == "L1c":
                        p.muted = True
                    for g in range(2):
                        bk, keyk = banks_k[g]
                        for j in range(8):
                            c = g * 8 + j
                            TR(bk[0:CH, j, :], khT[:, c * CH:(c + 1) * CH], ["khT"], [keyk])
                        CP("act", kt_all[:, g * 8:(g + 1) * 8, :], bk[0:CH, :, :], [keyk], [("kt", g)])
                        bv, keyv = banks_v[g]
                        for j in range(8):
                            c = g * 8 + j
                            TR(bv[0:CH, j, :], vT[:, c * CH:(c + 1) * CH], ["vT"], [keyv])
                        CP("dve", vt_all[:, g * 8:(g + 1) * 8, :], bv[0:CH, :, :], [keyv], [("vt", g)])
                    if pass2:
                        for g in range(2):
                            bankA, keyA = (ps[4], PK[4]) if g == 0 else (ps[3], PK[3])
                            for j in range(8):
                                c = g * 8 + j
                                cs = slice(c * CH, (c + 1) * CH)
                                MM(bankA[0:CH, j * CH:(j + 1) * CH], ktT[:, cs], qtT[:, cs], True, True,
                                   ["ktT", "qtT"], [keyA])
                            for j in range(8):
                                c = g * 8 + j
                                TT("dve", Am_all[:, c, :], bankA[0:CH, j * CH:(j + 1) * CH], triu[0:CH, 0:CH],
                                   ALU.mult, [keyA, "cbf"], [("Am", c)])
                    for c in range(NCH):
                        cs = slice(c * CH, (c + 1) * CH)
                        g, j = divmod(c, 8)
                        dsb, dsk = ps[1 + c % 2], PK[1 + c % 2]
                        if pass2:
                            ob, ok = (ps[0], PK[0]) if g == 0 else (ps[5], PK[5])
                            MM(ob[:, j * CH:(j + 1) * CH], vt_all[:, c, :], Am_all[:, c, :], True, False,
                               [("vt", g), ("Am", c)], [ok])
                        MM(dsb[:, 0:128], kt_all[:, c, :], vt_all[:, c, :], True, True, [("kt", g), ("vt", g)], [dsk])
                        if pass2:
                            MM(ob[:, j * CH:(j + 1) * CH], Sbf[(c + 1) % 2][:], qhT[:, cs], False, True,
                               [("Sbf", (c + 1) % 2), "qhT"], [ok])
                        STT(S[:], S[:], ebc[:, c:c + 1], dsb[:, 0:128], ALU.mult, ALU.add, ["S", "ebc", dsk], ["S"])
                        if pass2 and c < NCH - 1:
                            CP("act", Sbf[c % 2][:], S[:], ["S"], [("Sbf", c % 2)])
                    if pass2:
                        CP("act", o32[:, 0:512], ps[0][:, :], [PK[0]], ["o32l"])
                        CP("act", o32[:, 512:1024], ps[5][:, :], [PK[5]], ["o32l"])
                    if CUT == "L1d":
                        p.muted = True
                    if not pass2:
                        CP("dve", sst[:, 0:128], S[:], ["S"], ["sst"])
                        if os.environ.get("MK_NOTAIL"):
                            MEMSET("dve", sst[:, 128:129], 0.0, ["sst"])
                        else:
                            TS("dve", sst[:, 128:129], ebc[:, 0:1], 1.0, None, ALU.mult, None, ["ebc", "sst"], ["sst"])
                            for c in range(1, NCH):
                                TT("dve", sst[:, 128:129], sst[:, 128:129], ebc[:, c:c + 1], ALU.mult,
                                   ["ebc", "sst"], ["sst"])
                        if mode == "F":
                            p.dma("sp", sbp[h // 4][(h % 4) * 128:(h % 4 + 1) * 128, 0:129], sst[:, 0:129],
                                  reads=["sst"], writes=[("sbp", h // 4)])
                            if h % 4 == 3:
                                AG(sbp[h // 4], sgp[h // 4], ("sbp", h // 4), ("sgp", h // 4))
                        else:
                            p.dma("sp", stb_a[h * 128:(h + 1) * 128, 0:129], sst[:, 0:129], reads=["sst"],
                                  writes=["stb"])
                    else:
                        for half in range(2):
                            sl = slice(half * 512, (half + 1) * 512)
                            ACT(sqb[:, :], o32[:, sl], AF.Square, ["o32l"], ["sqb"])
                            MM(ps[0][:, :], ones, sqb[:, :], True, True, ["sqb", "cbf"], [PK[0]])
                            ACT(rs1[:], ps[0][:, :], AF.Sqrt, [PK[0], "epsc"], ["rs1"], scale=1.0 / 128, bias=epsc[:])
                            RECIP(rs1[:], rs1[:], ["rs1"], ["rs1"])
                            STT(rs1[:], o32[:, sl], pc("hg"), rs1[:], ALU.mult, ALU.mult, ["o32l", "prm", "rs1"], ["rs1"])
                            TT("dve", mix[:, h, sl], rs1[:], gs[:, sl], ALU.mult, ["rs1", "gs"], [("mix", h)])
            p.barrier()

        def finale():
            out_proj(w_out1, x1d, 0, x2d)
            p.barrier()

            def sink_o(c, a, n, xb, xk, gname):
                STT(xb[:, 0:n], xb[:, 0:n], pc(gname, c), rstd[:, a:a + n], ALU.mult, ALU.mult,
                    [xk, "rstd", "prm"], [xk])
                p.dma("sp", outT[:, c, a:a + n], xb[:, 0:n], reads=[xk], writes=["outT"])
            rms_stream(x2d, "fin_g", T, 0, sink_o)

        if L0:
            layer0()
        if P1:
            layer1(False)
        if P2:
            layer1(True)
            finale()
        p.muted = False
        p.barrier()
        p.op("sp", lambda e: None, reads=(), writes=["done"])
        p.emit(st)
    return nc


def _relay(tile):
    K, n = tile.shape
    return np.ascontiguousarray(tile.reshape(K // 128, 128, n).transpose(1, 0, 2).reshape(128, (K // 128) * n))


def _cols(v):
    n = v.shape[0] // 128
    return np.ascontiguousarray(v.reshape(n, 128).T)


def prepare(inputs):
    f = lambda k: np.asarray(inputs[k], dtype=np.float32)
    x = f("x")
    W = f("ev_w_in")[0]
    a_v, a_g, a_z = W[:, 0:1024], W[:, 1024:2048], W[:, 2048:3072]
    c_q, c_kv, k_pe, b_z = W[:, 3072:3584], W[:, 3584:4096], W[:, 4096:4160], W[:, 4160:5184]
    zpad = np.zeros((2048, 64), np.float32)
    sw = np.concatenate([np.arange(32, 64), np.arange(0, 32)])
    tiles = [c_kv[:, j * 128:(j + 1) * 128] for j in range(4)]
    tiles += [np.concatenate([k_pe, zpad], 1), np.concatenate([k_pe[:, sw], zpad], 1)]
    for j in range(8):
        sl = slice(j * 128, (j + 1) * 128)
        tiles += [a_v[:, sl], a_g[:, sl], a_z[:, sl]]
    tiles += [b_z[:, j * 128:(j + 1) * 128] for j in range(8)]
    tiles += [c_q[:, j * 128:(j + 1) * 128] for j in range(4)]
    w_in0 = np.stack([_relay(t) for t in tiles])
    uq = f("mla_w_uq")[0]
    w_uq = np.stack([_relay(np.concatenate([uq[:, h, 0:128], uq[:, h, 128:192], uq[:, h, 128:192][:, sw]], 1))
                     for h in range(8)])
    ukv = f("mla_w_ukv")[0]
    w_ukvk = np.stack([_relay(ukv[:, h, 0:128]) for h in range(8)])
    vfull = ukv[:, :, 128:256].reshape(512, 1024)
    w_ukvv = np.stack([_relay(vfull[:, 0:512]), _relay(vfull[:, 512:1024])])
    wo0 = f("ev_w_out")[0]
    w_out0 = np.stack([_relay(wo0[:, m * 128:(m + 1) * 128]) for m in range(16)])
    W1 = f("od_w_in")[0]
    t1 = []
    for h in range(16):
        for part in range(4):
            t1.append(W1[:, part * 2048 + h * 128: part * 2048 + (h + 1) * 128])
    w_in1 = np.stack([_relay(t) for t in t1])
    wo1 = f("od_w_out")[0]
    w_out1 = np.stack([_relay(wo1[:, m * 128:(m + 1) * 128]) for m in range(16)])
    prm = np.zeros((128, NPRM), np.float32)

    def put(name, arr):
        o, w = PRM[name]
        prm[:, o:o + w] = arr
    put("ev_g", _cols(f("ev_norm_g")[0]))
    put("conv_b", _cols(f("conv_b")[0]))
    put("ln_g", _cols(f("conv_ln_g")[0]))
    put("ln_b", _cols(f("conv_ln_b")[0]))
    put("q_g", _cols(f("mla_q_norm_g")[0]))
    put("kv_g", _cols(f("mla_kv_norm_g")[0]))
    put("od_g", _cols(f("od_norm_g")[0]))
    put("fin_g", _cols(f("final_norm_g")))
    put("hg", f("hgrn_norm_g")[0].reshape(128, 1))
    lg = f("hgrn_lb_logits")
    put("l0", _cols(lg[0]))
    put("l1", _cols(lg[1]))
    cw = f("conv_w")[0]
    put("conv_w", cw.reshape(31, 8, 128).transpose(2, 1, 0).reshape(128, 8 * 31))
    cbf = np.zeros((128, 3, 128), np.float32)
    cbf[:, 0, :] = np.eye(128)
    cbf[:, 1, :] = 1.0
    cbf[:, 2, :] = np.triu(np.ones((128, 128)))
    cbf = cbf.astype(ml_dtypes.bfloat16)
    inv_freq = (1.0 / (np.float32(10000.0) ** (np.arange(0, 64, 2, dtype=np.float32) / np.float32(64)))).astype(np.float32)
    shared = dict(cbf=cbf, w_in0=w_in0, w_uq=w_uq, w_ukvk=w_ukvk, w_ukvv=w_ukvv, w_out0=w_out0, w_in1=w_in1,
                  w_out1=w_out1)
    per = []
    for c in range(8):
        b, j = c // 4, c % 4
        s0 = j * T
        xs_ = np.zeros((HALO + T, 2048), np.float32)
        if j > 0:
            xs_[:, :] = x[b, s0 - HALO:s0 + T, :]
        else:
            xs_[HALO:, :] = x[b, 0:T, :]
        xT = np.ascontiguousarray(xs_.reshape(HALO + T, 16, 128).transpose(2, 1, 0))
        pos = np.arange(s0, s0 + T, dtype=np.float32)
        ang = pos[:, None] * inv_freq[None, :]
        cs, sn = np.cos(ang).astype(np.float32).T, np.sin(ang).astype(np.float32).T
        rope = np.stack([np.concatenate([cs, cs], 0), np.concatenate([-sn, sn], 0)], 1).astype(np.float32)
        pr = prm.copy()
        o, w = PRM["sel"]
        pr[:, o + j] = 1.0
        o, w = PRM["vis"]
        for r in range(3):
            pr[:, o + r] = 1.0 if r < j else 0.0
        per.append(dict(xT=xT, prm=pr, rope=np.ascontiguousarray(rope)))
    return shared, per


_NC = {}


def _get(mode, l1=True):
    k = (mode, l1)
    if k not in _NC:
        _NC[k] = build(mode, l1)
    return _NC[k]


def _launch(nc, maps):
    return run_bass_kernel_spmd(nc, maps, core_ids=list(range(8))).results


def _assemble(res, name):
    out = np.zeros((2, 4096, 2048), np.float32)
    for c in range(8):
        b, j = c // 4, c % 4
        o = np.asarray(res[c][name])
        out[b, j * T:(j + 1) * T, :] = o.transpose(2, 1, 0).reshape(T, 2048)
    return out


def run_unfused(inputs, upto="C"):
    shared, per = prepare(inputs)
    mA = [dict(prm=per[c]["prm"], cbf=shared["cbf"], xT=per[c]["xT"], rope=per[c]["rope"],
               w_in0=shared["w_in0"][0:6], w_ukvk=shared["w_ukvk"], w_ukvv=shared["w_ukvv"]) for c in range(8)]
    rA = _launch(_get("A"), mA)
    kvg = [np.concatenate([np.asarray(rA[(c // 4) * 4 + r]["kvb"]) for r in range(4)], 0) for c in range(8)]
    mB = [dict(prm=per[c]["prm"], cbf=shared["cbf"], xT=per[c]["xT"], rope=per[c]["rope"], kvg=kvg[c],
               w_in0=shared["w_in0"], w_ukvk=shared["w_ukvk"], w_ukvv=shared["w_ukvv"], w_uq=shared["w_uq"],
               w_out0=shared["w_out0"]) for c in range(8)]
    rB = _launch(_get("B", False), mB)
    if upto == "B0":
        return _assemble(rB, "x1d")
    x1 = [np.asarray(rB[c]["x1d"]) for c in range(8)]
    mP = [dict(prm=per[c]["prm"], cbf=shared["cbf"], x1d=x1[c], w_in1=shared["w_in1"]) for c in range(8)]
    rP = _launch(_get("P"), mP)
    stg = [np.concatenate([np.asarray(rP[(c // 4) * 4 + r]["stb"]) for r in range(4)], 0) for c in range(8)]
    mC = [dict(prm=per[c]["prm"], cbf=shared["cbf"], x1d=x1[c], stg=stg[c],
               w_in1=shared["w_in1"], w_out1=shared["w_out1"]) for c in range(8)]
    rC = _launch(_get("C"), mC)
    return _assemble(rC, "outT")


def run_fused(inputs):
    shared, per = prepare(inputs)
    maps = [dict(prm=per[c]["prm"], cbf=shared["cbf"], xT=per[c]["xT"], rope=per[c]["rope"], **{
        k: shared[k] for k in ("w_in0", "w_ukvk", "w_ukvv", "w_uq", "w_out0", "w_in1", "w_out1")}) for c in range(8)]
    return _assemble(_launch(_get("F"), maps), "outT")


FUSED = True


def kernel(**inputs):
    if FUSED:
        return run_fused(inputs)
    return run_unfused(inputs)
```

```python
import numpy as np
import ml_dtypes
from contextlib import ExitStack
import concourse.bass as bass
import concourse.mybir as mybir
from concourse.bass_utils import run_bass_kernel_spmd

F32 = mybir.dt.float32
BF16 = mybir.dt.bfloat16
AF = mybir.ActivationFunctionType
ALU = mybir.AluOpType
ENGS = ("pe", "act", "dve", "pool", "sp")
T = 1024
HALO = 32
EPS = 1e-6
NEG = -30000.0


class Op:
    __slots__ = ("idx", "eng", "fn", "deps", "dma", "need_inc", "sem", "val", "extra_waits", "cc")

    def __init__(self, idx, eng, fn, deps, dma):
        self.idx = idx
        self.eng = eng
        self.fn = fn
        self.deps = deps
        self.dma = dma
        self.need_inc = False
        self.sem = None
        self.val = 0
        self.extra_waits = []
        self.cc = False


class Prog:
    def __init__(self, nc, n_dma_sems=8):
        self.nc = nc
        self.ops = []
        self.last_w = {}
        self.readers = {}
        self.barrier_deps = []
        self.last_on_eng = {}
        self.dmas_since_barrier = []
        self.n_dma_sems = n_dma_sems

    muted = False

    def op(self, eng, fn, reads=(), writes=(), dma=False):
        if self.muted:
            return -1
        idx = len(self.ops)
        deps = set(self.barrier_deps)
        for k in reads:
            w = self.last_w.get(k)
            if w is not None:
                deps.add(w)
        for k in writes:
            w = self.last_w.get(k)
            if w is not None:
                deps.add(w)
            for r in self.readers.get(k, ()):
                deps.add(r)
        o = Op(idx, eng, fn, deps, dma)
        self.ops.append(o)
        for k in writes:
            self.last_w[k] = idx
            self.readers[k] = []
        for k in reads:
            if k not in writes:
                self.readers.setdefault(k, []).append(idx)
        self.last_on_eng[eng] = idx
        if dma:
            self.dmas_since_barrier.append(idx)
        return idx

    def barrier(self):
        deps = set(self.last_on_eng.values()) | set(self.dmas_since_barrier)
        deps = {d for d in deps if not self.ops[d].cc}
        self.barrier_deps = sorted(deps)
        self.dmas_since_barrier = []
        keep = {k: w for k, w in self.last_w.items() if self.ops[w].cc}
        self.last_w = keep
        self.readers = {k: [] for k in keep}

    def dma(self, q, out, in_, reads=(), writes=(), **kw):
        return self.op(q, lambda e: e.dma_start(out=out, in_=in_, **kw), reads, writes, dma=True)

    def collective(self, fn, reads=(), writes=()):
        i = self.op("pool", fn, reads, writes, dma=True)
        self.ops[i].cc = True
        return i

    def emit(self, stack):
        nc = self.nc
        ops = self.ops
        for o in ops:
            for d in o.deps:
                od = ops[d]
                if od.dma:
                    od.need_inc = True
                elif od.eng == "pe" and o.eng == "pe" and not o.dma:
                    continue
                else:
                    od.need_inc = True
        esem = {e: stack.enter_context(nc.semaphore("s_" + e)) for e in ENGS}
        dsem = {}
        for q in ("sp", "act", "pool"):
            dsem[q] = [stack.enter_context(nc.semaphore("d_%s%d" % (q, i))) for i in range(self.n_dma_sems)]
        ecount = {e: 0 for e in ENGS}
        dcount = {q: [0] * self.n_dma_sems for q in dsem}
        drr = {q: 0 for q in dsem}
        for o in ops:
            if o.cc:
                o.sem = stack.enter_context(nc.semaphore("cc%d" % o.idx))
                o.val = 1
            elif o.dma:
                q = o.eng
                i = drr[q]
                drr[q] = (i + 1) % self.n_dma_sems
                prev = dcount[q][i]
                if prev > 0:
                    o.extra_waits.append((dsem[q][i], prev))
                dcount[q][i] = prev + 16
                o.sem = dsem[q][i]
                o.val = prev + 16
            elif o.need_inc:
                ecount[o.eng] += 1
                o.sem = esem[o.eng]
                o.val = ecount[o.eng]
        engobj = {"pe": "tensor", "act": "scalar", "dve": "vector", "pool": "gpsimd", "sp": "sync"}
        block = stack.enter_context(nc.Block())

        def make(engname):
            def body(eng):
                known = {}
                for o in ops:
                    if o.eng != engname:
                        continue
                    waits = list(o.extra_waits)
                    for d in sorted(o.deps):
                        od = ops[d]
                        if od.sem is None:
                            continue
                        if (not od.dma) and od.eng == "pe" and engname == "pe" and not o.dma:
                            continue
                        waits.append((od.sem, od.val))
                    best = {}
                    for s, v in waits:
                        key = id(s)
                        if key not in best or best[key][1] < v:
                            best[key] = (s, v)
                    for key, (s, v) in best.items():
                        if known.get(key, 0) >= v:
                            continue
                        eng.wait_ge(s, v)
                        known[key] = v
                    ins = o.fn(eng)
                    if o.sem is not None and ins is not None:
                        ins.then_inc(o.sem, 1 if (o.cc or not o.dma) else 16)
            return body

        for engname in ENGS:
            getattr(block, engobj[engname])(make(engname))


PRM = {}
_o = 0
for _n, _w in (("ev_g", 16), ("conv_b", 8), ("ln_g", 8), ("ln_b", 8), ("q_g", 4), ("kv_g", 4),
               ("od_g", 16), ("fin_g", 16), ("hg", 1), ("l0", 16), ("l1", 16), ("conv_w", 8 * 31),
               ("sel", 4), ("vis", 3)):
    PRM[_n] = (_o, _w)
    _o += _w
NPRM = _o

N_IN0 = 42
KVR = 1024 + 64 + 1024
CH = 64
NCH = T // CH
SW = 136


def build(mode, l1=True):
    import os
    CUT = os.environ.get("MK_CUT", "")
    nc = bass.Bass("TRN2", target_bir_lowering=False)
    L0 = mode in ("A", "B", "F")
    P1 = (mode in ("B", "F") and l1) or mode == "P"
    P2 = mode in ("C", "F")
    NH = int(os.environ.get("MK_HEADS", "16"))

    def decl(name, shape, dt, role):
        if role == "in":
            return nc.dram_tensor(name, shape, dt, kind="ExternalInput").ap()
        if role == "out":
            return nc.dram_tensor(name, shape, dt, kind="ExternalOutput").ap()
        return nc.dram_tensor(name, shape, dt).ap()

    prm_d = decl("prm", [128, NPRM], F32, "in")
    cbf_d = decl("cbf", [128, 3, 128], BF16, "in")
    if L0:
        xT = decl("xT", [128, 16, HALO + T], F32, "in")
        rope_d = decl("rope", [64, 2, T], F32, "in")
        w_in0 = decl("w_in0", [6 if mode == "A" else N_IN0, 128, 2048], F32, "in")
        w_ukvk = decl("w_ukvk", [8, 128, 4 * 128], F32, "in")
        w_ukvv = decl("w_ukvv", [2, 128, 4 * 512], F32, "in")
        if mode != "F":
            kvb_a = decl("kvb", [KVR, T], BF16, "out" if mode == "A" else "int")
    if mode in ("B", "F"):
        w_uq = decl("w_uq", [8, 128, 4 * 256], F32, "in")
        w_out0 = decl("w_out0", [16, 128, 2048], F32, "in")
        if mode == "B":
            kvg_a = decl("kvg", [4 * KVR, T], BF16, "in")
    if mode != "A":
        x1d = decl("x1d", [128, 16, T], F32, {"B": "out", "C": "in", "F": "int", "P": "in"}[mode])
    if P1 or P2:
        w_in1 = decl("w_in1", [64, 128, 2048], F32, "in")
    if P1:
        if mode != "F":
            stb_a = decl("stb", [2048, SW], F32, "out")
    if P2:
        if mode == "C":
            stg_a = decl("stg", [4 * 2048, SW], F32, "in")
        w_out1 = decl("w_out1", [16, 128, 2048], F32, "in")
        x2d = decl("x2d", [128, 16, T], F32, "int")
        outT = decl("outT", [128, 16, T], F32, "out")

    RG = [[0, 1, 2, 3], [4, 5, 6, 7]]
    if mode == "F":
        kb = [decl("kb%d" % i, [256, T], BF16, "int") for i in range(4)]
        kg = [decl("kg%d" % i, [1024, T], BF16, "int") for i in range(4)]
        kpb = decl("kpb", [64, T], BF16, "int")
        kpgd = decl("kpgd", [256, T], BF16, "int")
        vb = [decl("vb%d" % i, [256, T], BF16, "int") for i in range(4)]
        vgd = [decl("vgd%d" % i, [1024, T], BF16, "int") for i in range(4)]
        sbp = [decl("sbp%d" % i, [512, SW], F32, "int") for i in range(4)]
        sgp = [decl("sgp%d" % i, [2048, SW], F32, "int") for i in range(4)]

    st = ExitStack()
    with st:
        def AG(in_ap, out_ap, rkey, wkey):
            p.collective(lambda e: e.collective_compute("AllGather", ALU.bypass, replica_groups=RG,
                                                        ins=[in_ap.opt()], outs=[out_ap.opt()]),
                         reads=[rkey], writes=[wkey])

        nsfx = [""]

        def sb(name, shape, dt=F32, stack=None):
            return (stack or st).enter_context(nc.sbuf_tensor("sb_" + name + nsfx[0], shape, dt))

        p = Prog(nc)
        ps = [st.enter_context(nc.psum_tensor("ps%d" % i, [128, 512], F32)) for i in range(7)]
        psb = st.enter_context(nc.psum_tensor("psb", [128, 8, 128], BF16))
        PK = [("ps", i) for i in range(7)]
        psv = ps[6][:, :].bitcast(BF16)

        def MM(out, lhsT, rhs, start, stop, reads, writes):
            p.op("pe", lambda e: e.matmul(out, lhsT=lhsT, rhs=rhs, start=start, stop=stop), reads, writes)

        def TR(out, in_, reads, writes):
            p.op("pe", lambda e: e.transpose(out, in_, ident), reads + ["cbf"], writes)

        def ACT(out, in_, func, reads, writes, scale=1.0, bias=None):
            if bias is None:
                p.op("act", lambda e: e.activation(out=out, in_=in_, func=func, scale=scale), reads, writes)
            else:
                p.op("act", lambda e: e.activation(out=out, in_=in_, func=func, scale=scale, bias=bias), reads, writes)

        def TT(eng, out, in0, in1, op, reads, writes):
            p.op(eng, lambda e: e.tensor_tensor(out=out, in0=in0, in1=in1, op=op), reads, writes)

        def TS(eng, out, in0, s1, s2, op0, op1, reads, writes):
            if s2 is None:
                p.op(eng, lambda e: e.tensor_scalar(out=out, in0=in0, scalar1=s1, scalar2=None, op0=op0), reads, writes)
            else:
                p.op(eng, lambda e: e.tensor_scalar(out=out, in0=in0, scalar1=s1, scalar2=s2, op0=op0, op1=op1), reads, writes)

        def STT(out, in0, scalar, in1, op0, op1, reads, writes):
            p.op("dve", lambda e: e.scalar_tensor_tensor(out=out, in0=in0, scalar=scalar, in1=in1, op0=op0, op1=op1),
                 reads, writes)

        def CP(eng, out, in_, reads, writes):
            if eng == "act":
                p.op("act", lambda e: e.activation(out=out, in_=in_, func=AF.Identity), reads, writes)
            else:
                p.op(eng, lambda e: e.tensor_copy(out=out, in_=in_), reads, writes)

        def RECIP(out, in_, reads, writes):
            p.op("dve", lambda e: e.reciprocal(out=out, in_=in_), reads, writes)

        def MEMSET(eng, ap, val, writes):
            p.op(eng, lambda e: e.memset(ap, val), (), writes)

        prm = sb("prm", [128, NPRM])
        cbf = sb("cbf", [128, 3, 128], BF16)
        epsc = sb("epsc", [128, 1])
        lbc = sb("lbc", [128, 16])
        omlb = sb("omlb", [128, 16])
        p.dma("sp", prm[:], prm_d, writes=["prm"])
        p.dma("sp", cbf[:], cbf_d, writes=["cbf"])
        MEMSET("dve", epsc[:], EPS, ["epsc"])
        ident = cbf[:, 0, :]
        ones = cbf[:, 1, :]
        triu = cbf[:, 2, :]

        def pc(name, j=0, n=1):
            o, w = PRM[name]
            return prm[:, o + j:o + j + n]

        TT("dve", lbc[:], pc("l1", 0, 16), pc("l0", 0, 16), ALU.subtract, ["prm"], ["lbc"])
        ACT(lbc[:], lbc[:], AF.Sigmoid, ["lbc"], ["lbc"])
        TS("dve", omlb[:], lbc[:], -1.0, 1.0, ALU.mult, ALU.add, ["lbc"], ["omlb"])

        wst = [sb("wst%d" % i, [128, 2048]) for i in range(2)]
        wbf = [sb("wbf%d" % i, [128, 2048], BF16) for i in range(2)]
        wctr = [0]

        def wload(src, n=2048):
            i = wctr[0] % 2
            wctr[0] += 1
            p.dma("sp", wst[i][:, 0:n], src, writes=[("wst", i)])
            CP("pool", wbf[i][:, 0:n], wst[i][:, 0:n], [("wst", i)], [("wbf", i)])
            return wbf[i], ("wbf", i)

        mix = sb("mix", [128, 16, T], BF16)
        rstd = sb("rstd", [128, T])
        sqb = sb("sqb", [128, 512], BF16)
        xs = [sb("xs%d" % i, [128, T]) for i in range(2)]

        def rms_stream(src_ap, gname, ncol, col0, sink):
            nh = [(a, min(512, ncol - a)) for a in range(0, ncol, 512)]
            for (a, n) in nh:
                for c in range(16):
                    xb = xs[c % 2]
                    p.dma("sp", xb[:, 0:n], src_ap[:, c, col0 + a:col0 + a + n], writes=[("xs", c % 2)])
                    ACT(sqb[:, 0:n], xb[:, 0:n], AF.Square, [("xs", c % 2)], ["sqb"])
                    MM(ps[0][:, 0:n], ones, sqb[:, 0:n], c == 0, c == 15, ["sqb", "cbf"], [PK[0]])
                ACT(rstd[:, a:a + n], ps[0][:, 0:n], AF.Sqrt, [PK[0], "epsc"], ["rstd"], scale=1.0 / 2048, bias=epsc[:])
                RECIP(rstd[:, a:a + n], rstd[:, a:a + n], ["rstd"], ["rstd"])
                for c in range(16):
                    xb = xs[c % 2]
                    p.dma("sp", xb[:, 0:n], src_ap[:, c, col0 + a:col0 + a + n], writes=[("xs", c % 2)])
                    sink(c, a, n, xb, ("xs", c % 2), gname)

        def lin_tile(wb, wkey, kc, m, rhs_fn, rkeys, a, n, pst, pkey):
            for c in range(kc):
                MM(pst[0:m, 0:n], wb[:, c * 128:c * 128 + m], rhs_fn(c, a, n), c == 0, c == kc - 1,
                   [wkey] + rkeys, [pkey])

        def out_proj(w_d, src_ap, col0, dst_ap):
            for m in range(16):
                wb, wk = wload(w_d[m])
                xb = xs[m % 2]
                p.dma("sp", xb[:, :], src_ap[:, m, col0:col0 + T], writes=[("xs", m % 2)])
                for half in range(2):
                    pst, pk = ps[1 + half], PK[1 + half]
                    for c in range(16):
                        MM(pst[:, :], wb[:, c * 128:(c + 1) * 128], mix[:, c, half * 512:(half + 1) * 512],
                           c == 0, c == 15, [wk] + [("mix", c)], [pk])
                    TT("dve", xb[:, half * 512:(half + 1) * 512], xb[:, half * 512:(half + 1) * 512], pst[:, :],
                       ALU.add, [("xs", m % 2), pk], [("xs", m % 2)])
                p.dma("sp", dst_ap[:, m, :], xb[:, :], reads=[("xs", m % 2)], writes=["x_out"])

        def layer0():
            with ExitStack() as s0:
                rope = sb("rope", [64, 2, T], F32, s0)
                p.dma("sp", rope[:], rope_d, writes=["rope"])
                kn = sb("kn", [128, 8, T], BF16, s0)
                vt = sb("vt", [128, 8, T], BF16, s0)
                kpe = sb("kpe", [64, T], BF16, s0)
                cq = sb("cq", [128, 4, T], F32, s0)
                with ExitStack() as s1:
                    hT = sb("hT", [128, 16, T], BF16, s1)
                    hh = sb("hh", [128, 16, HALO], BF16, s1)
                    rsth = sb("rsth", [128, HALO], F32, s1)

                    def sink_h(c, a, n, xb, xk, gname):
                        STT(hT[:, c, a:a + n], xb[:, 0:n], pc(gname, c), rstd[:, a:a + n], ALU.mult, ALU.mult,
                            [xk, "rstd", "prm"], ["hT"])
                    rms_stream(xT, "ev_g", T, HALO, sink_h)
                    if mode != "A":
                        for c in range(16):
                            xb = xs[c % 2]
                            p.dma("sp", xb[:, 0:HALO], xT[:, c, 0:HALO], writes=[("xs", c % 2)])
                            ACT(sqb[:, 0:HALO], xb[:, 0:HALO], AF.Square, [("xs", c % 2)], ["sqb"])
                            MM(ps[0][:, 0:HALO], ones, sqb[:, 0:HALO], c == 0, c == 15, ["sqb", "cbf"], [PK[0]])
                        ACT(rsth[:], ps[0][:, 0:HALO], AF.Sqrt, [PK[0], "epsc"], ["rsth"], scale=1.0 / 2048,
                            bias=epsc[:])
                        RECIP(rsth[:], rsth[:], ["rsth"], ["rsth"])
                        for c in range(16):
                            xb = xs[c % 2]
                            p.dma("sp", xb[:, 0:HALO], xT[:, c, 0:HALO], writes=[("xs", c % 2)])
                            STT(hh[:, c, :], xb[:, 0:HALO], pc("ev_g", c), rsth[:], ALU.mult, ALU.mult,
                                [("xs", c % 2), "rsth", "prm"], ["hh"])

                    def h_rhs(c, a, n):
                        return hT[:, c, a:a + n]

                    def in0_tile(ti, m, evac):
                        wb, wk = wload(w_in0[ti])
                        for half in range(2):
                            pst, pk = ps[1 + half], PK[1 + half]
                            lin_tile(wb, wk, 16, m, h_rhs, ["hT"], half * 512, 512, pst, pk)
                            evac(half, pst, pk)
                        return wb, wk

                    with ExitStack() as s2:
                        ckv = sb("ckv", [128, 4, T], F32, s2)
                        ckn = sb("ckn", [128, 4, T], BF16, s2)
                        kp32 = sb("kp32", [64, 2, T], F32, s2)
                        for j in range(4):
                            in0_tile(j, 128, lambda half, pst, pk, j=j: CP(
                                "act", ckv[:, j, half * 512:(half + 1) * 512], pst[:, :], [pk], ["ckv"]))
                        for j in range(2):
                            in0_tile(4 + j, 64, lambda half, pst, pk, j=j: CP(
                                "act", kp32[:, j, half * 512:(half + 1) * 512], pst[0:64, :], [pk], ["kp32"]))
                        for half in range(2):
                            sl = slice(half * 512, (half + 1) * 512)
                            for j in range(4):
                                ACT(sqb[:, :], ckv[:, j, sl], AF.Square, ["ckv"], ["sqb"])
                                MM(ps[0][:, :], ones, sqb[:, :], j == 0, j == 3, ["sqb", "cbf"], [PK[0]])
                            ACT(rstd[:, sl], ps[0][:, :], AF.Sqrt, [PK[0], "epsc"], ["rstd"], scale=1.0 / 512,
                                bias=epsc[:])
                            RECIP(rstd[:, sl], rstd[:, sl], ["rstd"], ["rstd"])
                            for j in range(4):
                                STT(ckn[:, j, sl], ckv[:, j, sl], pc("kv_g", j), rstd[:, sl], ALU.mult, ALU.mult,
                                    ["ckv", "rstd", "prm"], ["ckn"])
                        TT("dve", kp32[:, 0, :], kp32[:, 0, :], rope[:, 0, :], ALU.mult, ["kp32", "rope"], ["kp32"])
                        TT("dve", kp32[:, 1, :], kp32[:, 1, :], rope[:, 1, :], ALU.mult, ["kp32", "rope"], ["kp32"])
                        TT("dve", kpe[:, :], kp32[:, 0, :], kp32[:, 1, :], ALU.add, ["kp32"], ["kpe"])
                        if mode == "F":
                            p.dma("sp", kpb, kpe[:, :], reads=["kpe"], writes=["kpb"])
                            AG(kpb, kpgd, "kpb", "kpgd")
                        else:
                            p.dma("sp", kvb_a[1024:1088, :], kpe[:, :], reads=["kpe"], writes=["kvb"])
                        for h in range(8):
                            wb, wk = wload(w_ukvk[h], 512)
                            for half in range(2):
                                pst, pk = ps[1 + half], PK[1 + half]
                                for c in range(4):
                                    MM(pst[:, :], wb[:, c * 128:(c + 1) * 128], ckn[:, c, half * 512:(half + 1) * 512],
                                       c == 0, c == 3, [wk, "ckn"], [pk])
                                CP("act", kn[:, h, half * 512:(half + 1) * 512], pst[:, :], [pk], [("kn", h)])
                            if mode == "F":
                                p.dma("sp", kb[h // 2][(h % 2) * 128:(h % 2 + 1) * 128, :], kn[:, h, :],
                                      reads=[("kn", h)], writes=[("kb", h // 2)])
                                if h % 2 == 1:
                                    AG(kb[h // 2], kg[h // 2], ("kb", h // 2), ("kg", h // 2))
                            else:
                                p.dma("sp", kvb_a[h * 128:(h + 1) * 128, :], kn[:, h, :], reads=[("kn", h)],
                                      writes=["kvb"])
                        for vh in range(2):
                            wb, wk = wload(w_ukvv[vh])
                            for tt in range(8):
                                pst, pk = ps[1 + tt % 2], PK[1 + tt % 2]
                                for c in range(4):
                                    MM(pst[:, :], ckn[:, c, tt * 128:(tt + 1) * 128], wb[:, c * 512:(c + 1) * 512],
                                       c == 0, c == 3, [wk, "ckn"], [pk])
                                CP("act", vt[:, tt, vh * 512:(vh + 1) * 512], pst[:, :], [pk], ["vt"])
                        if mode == "F":
                            for i in range(4):
                                p.dma("sp", vb[i].rearrange("(t p) n -> p t n", p=128), vt[:, 2 * i:2 * i + 2, :],
                                      reads=["vt"], writes=[("vb", i)])
                                AG(vb[i], vgd[i], ("vb", i), ("vgd", i))
                        else:
                            p.dma("sp", kvb_a[1088:1088 + 1024, :].rearrange("(t p) n -> p t n", p=128), vt[:, :, :],
                                  reads=["vt"], writes=["kvb"])
                    p.barrier()
                    if mode == "A":
                        return
                    with ExitStack() as s2:
                        u = sb("u", [128, HALO + T], BF16, s2)
                        zs = sb("zs", [128, T], BF16, s2)
                        t32 = sb("t32", [128, T], F32, s2)
                        dg = sb("dg", [128, 31, 128], BF16, s2)
                        y32 = sb("y32", [128, 512], F32, s2)
                        ybf = sb("ybf", [128, 512], BF16, s2)
                        d32 = sb("d32", [128, 512], F32, s2)
                        r32 = sb("r32", [128, 512], F32, s2)
                        ones128 = sb("ones128", [128, 128], BF16, s2)
                        MEMSET("dve", ones128[:], 1.0 / 128, ["ones128"])
                        for j in range(8):
                            wb, wk = wload(w_in0[6 + 3 * j])
                            for half in range(2):
                                pst, pk = ps[1 + half], PK[1 + half]
                                lin_tile(wb, wk, 16, 128, h_rhs, ["hT"], half * 512, 512, pst, pk)
                                CP("act", t32[:, half * 512:(half + 1) * 512], pst[:, :], [pk], ["t32"])
                            for c in range(16):
                                MM(ps[3][:, 0:HALO], wb[:, c * 128:(c + 1) * 128], hh[:, c, :], c == 0, c == 15,
                                   [wk, "hh"], [PK[3]])
                            wb2, wk2 = wload(w_in0[7 + 3 * j])
                            for c in range(16):
                                MM(ps[4][:, 0:HALO], wb2[:, c * 128:(c + 1) * 128], hh[:, c, :], c == 0, c == 15,
                                   [wk2, "hh"], [PK[4]])
                            ACT(d32[:, 0:HALO], ps[4][:, 0:HALO], AF.Sigmoid, [PK[4]], ["d32"])
                            TT("dve", u[:, 0:HALO], d32[:, 0:HALO], ps[3][:, 0:HALO], ALU.mult, ["d32", PK[3]], ["u"])
                            for half in range(2):
                                pst, pk = ps[1 + half], PK[1 + half]
                                lin_tile(wb2, wk2, 16, 128, h_rhs, ["hT"], half * 512, 512, pst, pk)
                                ACT(d32[:, :], pst[:, :], AF.Sigmoid, [pk], ["d32"])
                                TT("dve", u[:, HALO + half * 512:HALO + (half + 1) * 512], d32[:, :],
                                   t32[:, half * 512:(half + 1) * 512], ALU.mult, ["d32", "t32"], ["u"])
                            wb3, wk3 = wload(w_in0[8 + 3 * j])
                            for half in range(2):
                                pst, pk = ps[1 + half], PK[1 + half]
                                lin_tile(wb3, wk3, 16, 128, h_rhs, ["hT"], half * 512, 512, pst, pk)
                                ACT(zs[:, half * 512:(half + 1) * 512], pst[:, :], AF.Silu, [pk], ["zs"])
                            o_w = PRM["conv_w"][0] + j * 31
                            for k in range(31):
                                TS("pool", dg[:, k, :], ident, prm[:, o_w + k:o_w + k + 1], None, ALU.mult, None,
                                   ["cbf", "prm"], [("dg", k)])
                            for half in range(2):
                                for k in range(31):
                                    a = HALO + half * 512 - 30 + k
                                    MM(ps[5][:, :], dg[:, k, :], u[:, a:a + 512], k == 0, k == 30, [("dg", k), "u"],
                                       [PK[5]])
                                ACT(y32[:], ps[5][:, :], AF.Identity, [PK[5], "prm"], ["y32"], bias=pc("conv_b", j))
                                CP("dve", ybf[:], y32[:], ["y32"], ["ybf"])
                                MM(ps[6][:, :], ones128[:], ybf[:], True, True, ["ones128", "ybf"], [PK[6]])
                                TT("dve", d32[:], y32[:], ps[6][:, :], ALU.subtract, ["y32", PK[6]], ["d32"])
                                ACT(ybf[:], d32[:], AF.Square, ["d32"], ["ybf"])
                                MM(ps[6][:, :], ones128[:], ybf[:], True, True, ["ones128", "ybf"], [PK[6]])
                                ACT(r32[:], ps[6][:, :], AF.Sqrt, [PK[6], "epsc"], ["r32"], bias=epsc[:])
                                RECIP(r32[:], r32[:], ["r32"], ["r32"])
                                TT("dve", d32[:], d32[:], r32[:], ALU.mult, ["d32", "r32"], ["d32"])
                                ACT(d32[:], d32[:], AF.Silu, ["d32", "prm"], ["d32"], scale=pc("ln_g", j),
                                    bias=pc("ln_b", j))
                                TT("dve", mix[:, j, half * 512:(half + 1) * 512], d32[:],
                                   zs[:, half * 512:(half + 1) * 512], ALU.mult, ["d32", "zs"], [("mix", j)])
                    p.barrier()
                    if CUT == "B":
                        p.muted = True
                    for j in range(8):
                        in0_tile(30 + j, 128, lambda half, pst, pk, j=j: ACT(
                            mix[:, 8 + j, half * 512:(half + 1) * 512], pst[:, :], AF.Silu, [pk], [("mix", 8 + j)]))
                    for j in range(4):
                        in0_tile(38 + j, 128, lambda half, pst, pk, j=j: CP(
                            "act", cq[:, j, half * 512:(half + 1) * 512], pst[:, :], [pk], ["cq"]))
                p.barrier()
                if CUT == "B2":
                    p.muted = True
                qn = sb("qn", [128, 8, T], BF16, s0)
                qpe = sb("qpe", [64, 8, T], BF16, s0)
                with ExitStack() as s2:
                    cqn = sb("cqn", [128, 4, T], BF16, s2)
                    qa = sb("qa", [64, 512], F32, s2)
                    qb = sb("qb", [64, 512], F32, s2)
                    for half in range(2):
                        sl = slice(half * 512, (half + 1) * 512)
                        for j in range(4):
                            ACT(sqb[:, :], cq[:, j, sl], AF.Square, ["cq"], ["sqb"])
                            MM(ps[0][:, :], ones, sqb[:, :], j == 0, j == 3, ["sqb", "cbf"], [PK[0]])
                        ACT(rstd[:, sl], ps[0][:, :], AF.Sqrt, [PK[0], "epsc"], ["rstd"], scale=1.0 / 512,
                            bias=epsc[:])
                        RECIP(rstd[:, sl], rstd[:, sl], ["rstd"], ["rstd"])
                        for j in range(4):
                            STT(cqn[:, j, sl], cq[:, j, sl], pc("q_g", j), rstd[:, sl], ALU.mult, ALU.mult,
                                ["cq", "rstd", "prm"], ["cqn"])
                    scale = 192.0 ** -0.5
                    for h in range(8):
                        wb, wk = wload(w_uq[h], 1024)
                        for half in range(2):
                            sl = slice(half * 512, (half + 1) * 512)
                            for c in range(4):
                                MM(ps[1][:, :], wb[:, c * 256:c * 256 + 128], cqn[:, c, sl], c == 0, c == 3,
                                   [wk, "cqn"], [PK[1]])
                            ACT(qn[:, h, sl], ps[1][:, :], AF.Identity, [PK[1]], [("qn", h)], scale=scale)
                            for c in range(4):
                                MM(ps[2][0:64, :], wb[:, c * 256 + 128:c * 256 + 192], cqn[:, c, sl], c == 0, c == 3,
                                   [wk, "cqn"], [PK[2]])
                            for c in range(4):
                                MM(ps[3][0:64, :], wb[:, c * 256 + 192:c * 256 + 256], cqn[:, c, sl], c == 0, c == 3,
                                   [wk, "cqn"], [PK[3]])
                            TT("dve", qa[:], ps[2][0:64, :], rope[:, 0, sl], ALU.mult, [PK[2], "rope"], ["qa"])
                            TT("dve", qb[:], ps[3][0:64, :], rope[:, 1, sl], ALU.mult, [PK[3], "rope"], ["qb"])
                            TT("dve", qa[:], qa[:], qb[:], ALU.add, ["qa", "qb"], ["qa"])
                            ACT(qpe[:, h, sl], qa[:], AF.Identity, ["qa"], [("qpe", h)], scale=scale)
                p.barrier()
                if CUT == "C":
                    p.muted = True
                with ExitStack() as s2:
                    kng = [sb("kng%d" % i, [128, 3 * T], BF16, s2) for i in range(2)]
                    vg = [sb("vg%d" % i, [128, 24, 128], BF16, s2) for i in range(2)]
                    kpg = sb("kpg", [64, 3 * T], BF16, s2)
                    onesv = sb("onesv", [128, 3, 128], BF16, s2)
                    pT = [sb("pT%d" % i, [128, 512], BF16, s2) for i in range(3)]
                    rl = sb("rl", [128, 512], F32, s2)
                    o32 = sb("o32", [128, 512], F32, s2)
                    for r in range(3):
                        if mode == "F":
                            p.dma("sp", kpg[:, r * T:(r + 1) * T], kpgd[r * 64:(r + 1) * 64, :],
                                  reads=["kpgd"], writes=["kpg"])
                        else:
                            p.dma("sp", kpg[:, r * T:(r + 1) * T], kvg_a[r * KVR + 1024:r * KVR + 1088, :],
                                  reads=["kvg"], writes=["kpg"])
                        TS("dve", onesv[:, r, :], ones, pc("vis", r), None, ALU.mult, None, ["cbf", "prm"], ["onesv"])
                    step = [0]
                    for h in range(8):
                        gi = h % 2
                        for r in range(3):
                            if mode == "F":
                                o_ = r * 256 + (h % 2) * 128
                                p.dma("sp", kng[gi][:, r * T:(r + 1) * T], kg[h // 2][o_:o_ + 128, :],
                                      reads=[("kg", h // 2)], writes=[("kng", gi)])
                                for i in range(4):
                                    p.dma("sp", vg[gi][:, r * 8 + 2 * i:r * 8 + 2 * i + 2, :],
                                          vgd[i][r * 256:(r + 1) * 256, h * 128:(h + 1) * 128].rearrange(
                                              "(t p) n -> p t n", p=128),
                                          reads=[("vgd", i)], writes=[("vg", gi)])
                            else:
                                p.dma("sp", kng[gi][:, r * T:(r + 1) * T],
                                      kvg_a[r * KVR + h * 128:r * KVR + (h + 1) * 128, :],
                                      reads=["kvg"], writes=[("kng", gi)])
                                p.dma("sp", vg[gi][:, r * 8:(r + 1) * 8, :],
                                      kvg_a[r * KVR + 1088:r * KVR + 1088 + 1024, h * 128:(h + 1) * 128].rearrange(
                                          "(t p) n -> p t n", p=128),
                                      reads=["kvg"], writes=[("vg", gi)])
                            TS("pool", vg[gi][:, r * 8:(r + 1) * 8, :], vg[gi][:, r * 8:(r + 1) * 8, :], pc("vis", r),
                               None, ALU.mult, None, [("vg", gi), "prm"], [("vg", gi)])
                        for qh in range(2):
                            q0 = qh * 512
                            tiles = []
                            nown = 4 if qh == 0 else 8
                            for kt in range(nown):
                                lo = max(0, kt * 128 - q0)
                                diag = (kt * 128 >= q0)
                                tiles.append(("own", kt, lo, diag))
                            for g in range(24):
                                tiles.append(("g", g, 0, False))
                            info = {}

                            def qk(ti):
                                kind, kt, lo, diag = tiles[ti]
                                n = 512 - lo
                                si = step[0] % 3
                                step[0] += 1
                                pss, pks = ps[2 + si], PK[2 + si]
                                if kind == "own":
                                    kT = kn[:, h, kt * 128:(kt + 1) * 128]
                                    kP = kpe[:, kt * 128:(kt + 1) * 128]
                                    vv = vt[:, kt, h * 128:(h + 1) * 128]
                                    ov = ones
                                    rk = [("kn", h), "kpe", "vt"]
                                else:
                                    kT = kng[gi][:, kt * 128:(kt + 1) * 128]
                                    kP = kpg[:, kt * 128:(kt + 1) * 128]
                                    vv = vg[gi][:, kt, :]
                                    ov = onesv[:, kt // 8, :]
                                    rk = [("kng", gi), "kpg", ("vg", gi), "onesv"]
                                MM(pss[:, 0:n], kT, qn[:, h, q0 + lo:q0 + 512], True, False, rk + [("qn", h)], [pks])
                                MM(pss[:, 0:n], kP, qpe[:, h, q0 + lo:q0 + 512], False, True, rk + [("qpe", h)], [pks])
                                info[ti] = (si, n, lo, diag, pss, pks, vv, ov, rk)

                            def rest(ti):
                                si, n, lo, diag, pss, pks, vv, ov, rk = info[ti]
                                ACT(pT[si][:, 0:n], pss[:, 0:n], AF.Exp, [pks], [("pT", si)])
                                if diag:
                                    TT("dve", pT[si][:, 0:128], pT[si][:, 0:128], triu, ALU.mult,
                                       [("pT", si), "cbf"], [("pT", si)])
                                first = (ti == 0)
                                last = (ti == len(tiles) - 1)
                                MM(ps[0][:, lo:512], vv, pT[si][:, 0:n], first, last, rk + [("pT", si)], [PK[0]])
                                MM(ps[1][:, lo:512], ov, pT[si][:, 0:n], first, last, rk + ["cbf", ("pT", si)], [PK[1]])

                            qk(0)
                            for ti in range(len(tiles)):
                                if ti + 1 < len(tiles):
                                    qk(ti + 1)
                                rest(ti)
                            CP("act", rl[:], ps[1][:, :], [PK[1]], ["rl"])
                            RECIP(rl[:], rl[:], ["rl"], ["rl"])
                            TT("dve", o32[:], ps[0][:, :], rl[:], ALU.mult, [PK[0], "rl"], ["o32"])
                            TT("dve", mix[:, 8 + h, q0:q0 + 512], o32[:], mix[:, 8 + h, q0:q0 + 512], ALU.mult,
                               ["o32", ("mix", 8 + h)], [("mix", 8 + h)])
            p.barrier()
            if CUT == "D":
                p.muted = True
            out_proj(w_out0, xT, HALO, x1d)
            p.muted = False
            p.barrier()

        def layer1(pass2):
            nsfx[0] = "_p2" if pass2 else "_p1"
            with ExitStack() as s0:
                h1 = sb("h1", [128, 16, T], BF16, s0)
                rmask = sb("rmask", [128, T], F32, s0)
                MEMSET("dve", rmask[:], 1.0, ["rmask"])
                MEMSET("dve", rmask[:].rearrange("p (c t) -> p c t", t=CH)[:, :, 0:1], 0.0, ["rmask"])

                def sink_h(c, a, n, xb, xk, gname):
                    STT(h1[:, c, a:a + n], xb[:, 0:n], pc(gname, c), rstd[:, a:a + n], ALU.mult, ALU.mult,
                        [xk, "rstd", "prm"], ["h1"])
                rms_stream(x1d, "od_g", T, 0, sink_h)

                def h_rhs(c, a, n):
                    return h1[:, c, a:a + n]

                sin_ = None
                if pass2:
                    sin_ = sb("sin", [128, 16, 128], F32, s0)
                    with ExitStack() as s1:
                        G = sb("G", [128, 3, 16, SW], F32, s1)
                        t2 = sb("t2", [128, 128], F32, s1)
                        t3 = sb("t3", [128, 128], F32, s1)
                        for r in range(3):
                            if mode == "F":
                                for i in range(4):
                                    p.dma("sp", G[:, r, 4 * i:4 * i + 4, :],
                                          sgp[i][r * 512:(r + 1) * 512, :].rearrange("(h p) w -> p h w", p=128),
                                          reads=[("sgp", i)], writes=["G"])
                            else:
                                p.dma("sp", G[:, r, :, :],
                                      stg_a[r * 2048:(r + 1) * 2048, :].rearrange("(h p) w -> p h w", p=128),
                                      reads=["stg"], writes=["G"])
                        for h in range(16):
                            S0, S1, S2 = G[:, 0, h, 0:128], G[:, 1, h, 0:128], G[:, 2, h, 0:128]
                            D1, D2 = G[:, 1, h, 128:129], G[:, 2, h, 128:129]
                            STT(t2[:], S0, D1, S1, ALU.mult, ALU.add, ["G"], ["t2"])
                            STT(t3[:], t2[:], D2, S2, ALU.mult, ALU.add, ["G", "t2"], ["t3"])
                            TS("dve", sin_[:, h, :], S0, pc("sel", 1), None, ALU.mult, None, ["G", "prm"], ["sin"])
                            STT(sin_[:, h, :], t2[:], pc("sel", 2), sin_[:, h, :], ALU.mult, ALU.add,
                                ["t2", "prm", "sin"], ["sin"])
                            STT(sin_[:, h, :], t3[:], pc("sel", 3), sin_[:, h, :], ALU.mult, ALU.add,
                                ["t3", "prm", "sin"], ["sin"])
                    p.barrier()

                sg = sb("sg", [128, T], F32, s0)
                ff = sb("ff", [128, T], F32, s0)
                lg = sb("lg", [128, T], F32, s0)
                bb = sb("bb", [128, T], F32, s0)
                khT = sb("khT", [128, T], BF16, s0)
                vT = sb("vT", [128, T], BF16, s0)
                ebc = sb("ebc", [128, NCH], F32, s0)
                kt_all = sb("kt_all", [CH, NCH, 128], BF16, s0)
                vt_all = sb("vt_all", [CH, NCH, 128], BF16, s0)
                psv3 = psv.rearrange("p (j k) -> p j k", k=128)
                ps3b = ps[3][:, :].bitcast(BF16).rearrange("p (j k) -> p j k", k=128)
                ps5b = ps[5][:, :].bitcast(BF16).rearrange("p (j k) -> p j k", k=128)
                banks_k = [(psb, "psb"), (psv3, PK[6])]
                banks_v = [(ps3b, PK[3]), (ps5b, PK[5])]
                S = sb("S", [128, 128], F32, s0)
                sst = sb("sst", [128, SW], F32, s0)
                MEMSET("dve", sst[:, 128:SW], 0.0, ["sst"])
                if pass2:
                    q32 = sb("q32", [128, T], F32, s0)
                    o32 = sb("o32l", [128, T], F32, s0)
                    gs = sb("gs", [128, T], BF16, s0)
                    qtT = sb("qtT", [128, T], BF16, s0)
                    ktT = sb("ktT", [128, T], BF16, s0)
                    qhT = sb("qhT", [128, T], BF16, s0)
                    negr = sb("negr", [128, NCH], F32, s0)
                    Sbf = [sb("Sbf%d" % i, [128, 128], BF16, s0) for i in range(2)]
                    Am_all = sb("Am_all", [CH, NCH, CH], BF16, s0)
                    rs1 = sb("rs1", [128, 512], F32, s0)

                def tile_in(ti, evac):
                    wb, wk = wload(w_in1[ti])
                    for half in range(2):
                        pst, pk = ps[1 + half], PK[1 + half]
                        lin_tile(wb, wk, 16, 128, h_rhs, ["h1"], half * 512, 512, pst, pk)
                        evac(half, pst, pk)

                for h in range(NH):
                    tile_in(4 * h + 1, lambda half, pst, pk: ACT(sg[:, half * 512:(half + 1) * 512], pst[:, :],
                                                                AF.Sigmoid, [pk], ["sg"]))
                    TS("dve", ff[:], sg[:], omlb[:, h:h + 1], lbc[:, h:h + 1], ALU.mult, ALU.add,
                       ["sg", "omlb", "lbc"], ["ff"])
                    if CUT == "L1a":
                        p.muted = True
                    ACT(lg[:], ff[:], AF.Ln, ["ff"], ["lg"])
                    TS("dve", sg[:], ff[:], -1.0, 1.0, ALU.mult, ALU.add, ["ff"], ["sg"])
                    p.op("dve", lambda e: e.tensor_tensor_scan(out=bb[:], data0=rmask[:], data1=lg[:], initial=0.0,
                                                               op0=ALU.mult, op1=ALU.add),
                         ["rmask", "lg"], ["bb"])
                    bb3 = bb[:].rearrange("p (c t) -> p c t", t=CH)
                    if CUT == "L1b":
                        p.muted = True
                    ACT(ebc[:], bb3[:, :, CH - 1], AF.Exp, ["bb"], ["ebc"])
                    lg3 = lg[:].rearrange("p (c t) -> p c t", t=CH)
                    TT("dve", lg3[:, :, 0], lg3[:, :, 0], bb3[:, :, CH - 1], ALU.subtract, ["lg", "bb"], ["lg"])
                    p.op("dve", lambda e: e.tensor_tensor_scan(out=ff[:], data0=rmask[:], data1=lg[:], initial=0.0,
                                                               op0=ALU.mult, op1=ALU.add),
                         ["rmask", "lg"], ["ff"])
                    ACT(ff[:], ff[:], AF.Exp, ["ff"], ["ff"], scale=-1.0)
                    TT("dve", khT[:], sg[:], ff[:], ALU.mult, ["sg", "ff"], ["khT"])
                    tile_in(4 * h + 2, lambda half, pst, pk: CP("act", vT[:, half * 512:(half + 1) * 512], pst[:, :],
                                                               [pk], ["vT"]))
                    if pass2:
                        tile_in(4 * h + 0, lambda half, pst, pk: CP("act", q32[:, half * 512:(half + 1) * 512],
                                                                   pst[:, :], [pk], ["q32"]))
                        tile_in(4 * h + 3, lambda half, pst, pk: ACT(gs[:, half * 512:(half + 1) * 512], pst[:, :],
                                                                    AF.Silu, [pk], ["gs"]))
                        TT("dve", negr[:], bb3[:, :, CH - 1], bb3[:, :, CH // 2 - 1], ALU.subtract, ["bb"], ["negr"])
                        TT("dve", lg3[:, :, 0], lg3[:, :, 0], negr[:], ALU.add, ["lg", "negr"], ["lg"])
                        p.op("dve", lambda e: e.tensor_tensor_scan(out=ff[:], data0=rmask[:], data1=lg[:],
                                                                   initial=0.0, op0=ALU.mult, op1=ALU.add),
                             ["rmask", "lg"], ["ff"])
                        ACT(lg[:], ff[:], AF.Exp, ["ff"], ["lg"])
                        TT("dve", qtT[:], q32[:], lg[:], ALU.mult, ["q32", "lg"], ["qtT"])
                        ACT(lg[:], ff[:], AF.Exp, ["ff"], ["lg"], scale=-1.0)
                        TT("dve", ktT[:], sg[:], lg[:], ALU.mult, ["sg", "lg"], ["ktT"])
                        ACT(ff[:], bb[:], AF.Exp, ["bb"], ["ff"])
                        TT("dve", qhT[:], q32[:], ff[:], ALU.mult, ["q32", "ff"], ["qhT"])
                        CP("dve", S[:], sin_[:, h, :], ["sin"], ["S"])
                        CP("act", Sbf[1][:], sin_[:, h, :], ["sin"], [("Sbf", 1)])
                    else:
                        MEMSET("dve", S[:], 0.0, ["S"])
                    if CUT == "L1c":
                        p.muted = True
                    for g in range(2):
                        bk, keyk = banks_k[g]
                        for j in range(8):
                            c = g * 8 + j
                            TR(bk[0:CH, j, :], khT[:, c * CH:(c + 1) * CH], ["khT"], [keyk])
                        CP("act", kt_all[:, g * 8:(g + 1) * 8, :], bk[0:CH, :, :], [keyk], [("kt", g)])
                        bv, keyv = banks_v[g]
                        for j in range(8):
                            c = g * 8 + j
                            TR(bv[0:CH, j, :], vT[:, c * CH:(c + 1) * CH], ["vT"], [keyv])
                        CP("dve", vt_all[:, g * 8:(g + 1) * 8, :], bv[0:CH, :, :], [keyv], [("vt", g)])
                    if pass2:
                        for g in range(2):
                            bankA, keyA = (ps[4], PK[4]) if g == 0 else (ps[3], PK[3])
                            for j in range(8):
                                c = g * 8 + j
                                cs = slice(c * CH, (c + 1) * CH)
                                MM(bankA[0:CH, j * CH:(j + 1) * CH], ktT[:, cs], qtT[:, cs], True, True,
                                   ["ktT", "qtT"], [keyA])
                            for j in range(8):
                                c = g * 8 + j
                                TT("dve", Am_all[:, c, :], bankA[0:CH, j * CH:(j + 1) * CH], triu[0:CH, 0:CH],
                                   ALU.mult, [keyA, "cbf"], [("Am", c)])
                    for c in range(NCH):
                        cs = slice(c * CH, (c + 1) * CH)
                        g, j = divmod(c, 8)
                        dsb, dsk = ps[1 + c % 2], PK[1 + c % 2]
                        if pass2:
                            ob, ok = (ps[0], PK[0]) if g == 0 else (ps[5], PK[5])
                            MM(ob[:, j * CH:(j + 1) * CH], vt_all[:, c, :], Am_all[:, c, :], True, False,
                               [("vt", g), ("Am", c)], [ok])
                        MM(dsb[:, 0:128], kt_all[:, c, :], vt_all[:, c, :], True, True, [("kt", g), ("vt", g)], [dsk])
                        if pass2:
                            MM(ob[:, j * CH:(j + 1) * CH], Sbf[(c + 1) % 2][:], qhT[:, cs], False, True,
                               [("Sbf", (c + 1) % 2), "qhT"], [ok])
                        STT(S[:], S[:], ebc[:, c:c + 1], dsb[:, 0:128], ALU.mult, ALU.add, ["S", "ebc", dsk], ["S"])
                        if pass2 and c < NCH - 1:
                            CP("act", Sbf[c % 2][:], S[:], ["S"], [("Sbf", c % 2)])
                    if pass2:
                        CP("act", o32[:, 0:512], ps[0][:, :], [PK[0]], ["o32l"])
                        CP("act", o32[:, 512:1024], ps[5][:, :], [PK[5]], ["o32l"])
                    if CUT == "L1d":
                        p.muted = True
                    if not pass2:
                        CP("dve", sst[:, 0:128], S[:], ["S"], ["sst"])
                        if os.environ.get("MK_NOTAIL"):
                            MEMSET("dve", sst[:, 128:129], 0.0, ["sst"])
                        else:
                            TS("dve", sst[:, 128:129], ebc[:, 0:1], 1.0, None, ALU.mult, None, ["ebc", "sst"], ["sst"])
                            for c in range(1, NCH):
                                TT("dve", sst[:, 128:129], sst[:, 128:129], ebc[:, c:c + 1], ALU.mult,
                                   ["ebc", "sst"], ["sst"])
                        if mode == "F":
                            p.dma("sp", sbp[h // 4][(h % 4) * 128:(h % 4 + 1) * 128, :], sst[:, :],
                                  reads=["sst"], writes=[("sbp", h // 4)])
                            if h % 4 == 3:
                                AG(sbp[h // 4], sgp[h // 4], ("sbp", h // 4), ("sgp", h // 4))
                        else:
                            p.dma("sp", stb_a[h * 128:(h + 1) * 128, :], sst[:, :], reads=["sst"],
                                  writes=["stb"])
                    else:
                        for half in range(2):
                            sl = slice(half * 512, (half + 1) * 512)
                            ACT(sqb[:, :], o32[:, sl], AF.Square, ["o32l"], ["sqb"])
                            MM(ps[0][:, :], ones, sqb[:, :], True, True, ["sqb", "cbf"], [PK[0]])
                            ACT(rs1[:], ps[0][:, :], AF.Sqrt, [PK[0], "epsc"], ["rs1"], scale=1.0 / 128, bias=epsc[:])
                            RECIP(rs1[:], rs1[:], ["rs1"], ["rs1"])
                            STT(rs1[:], o32[:, sl], pc("hg"), rs1[:], ALU.mult, ALU.mult, ["o32l", "prm", "rs1"], ["rs1"])
                            TT("dve", mix[:, h, sl], rs1[:], gs[:, sl], ALU.mult, ["rs1", "gs"], [("mix", h)])
            p.barrier()

        def finale():
            out_proj(w_out1, x1d, 0, x2d)
            p.barrier()

            def sink_o(c, a, n, xb, xk, gname):
                STT(xb[:, 0:n], xb[:, 0:n], pc(gname, c), rstd[:, a:a + n], ALU.mult, ALU.mult,
                    [xk, "rstd", "prm"], [xk])
                p.dma("sp", outT[:, c, a:a + n], xb[:, 0:n], reads=[xk], writes=["outT"])
            rms_stream(x2d, "fin_g", T, 0, sink_o)

        if L0:
            layer0()
        if P1:
            layer1(False)
        if P2:
            layer1(True)
            finale()
        p.muted = False
        p.barrier()
        p.op("sp", lambda e: None, reads=(), writes=["done"])
        p.emit(st)
    return nc


def _relay(tile):
    K, n = tile.shape
    return np.ascontiguousarray(tile.reshape(K // 128, 128, n).transpose(1, 0, 2).reshape(128, (K // 128) * n))


def _cols(v):
    n = v.shape[0] // 128
    return np.ascontiguousarray(v.reshape(n, 128).T)


def prepare(inputs):
    f = lambda k: np.asarray(inputs[k], dtype=np.float32)
    x = f("x")
    W = f("ev_w_in")[0]
    a_v, a_g, a_z = W[:, 0:1024], W[:, 1024:2048], W[:, 2048:3072]
    c_q, c_kv, k_pe, b_z = W[:, 3072:3584], W[:, 3584:4096], W[:, 4096:4160], W[:, 4160:5184]
    zpad = np.zeros((2048, 64), np.float32)
    sw = np.concatenate([np.arange(32, 64), np.arange(0, 32)])
    tiles = [c_kv[:, j * 128:(j + 1) * 128] for j in range(4)]
    tiles += [np.concatenate([k_pe, zpad], 1), np.concatenate([k_pe[:, sw], zpad], 1)]
    for j in range(8):
        sl = slice(j * 128, (j + 1) * 128)
        tiles += [a_v[:, sl], a_g[:, sl], a_z[:, sl]]
    tiles += [b_z[:, j * 128:(j + 1) * 128] for j in range(8)]
    tiles += [c_q[:, j * 128:(j + 1) * 128] for j in range(4)]
    w_in0 = np.stack([_relay(t) for t in tiles])
    uq = f("mla_w_uq")[0]
    w_uq = np.stack([_relay(np.concatenate([uq[:, h, 0:128], uq[:, h, 128:192], uq[:, h, 128:192][:, sw]], 1))
                     for h in range(8)])
    ukv = f("mla_w_ukv")[0]
    w_ukvk = np.stack([_relay(ukv[:, h, 0:128]) for h in range(8)])
    vfull = ukv[:, :, 128:256].reshape(512, 1024)
    w_ukvv = np.stack([_relay(vfull[:, 0:512]), _relay(vfull[:, 512:1024])])
    wo0 = f("ev_w_out")[0]
    w_out0 = np.stack([_relay(wo0[:, m * 128:(m + 1) * 128]) for m in range(16)])
    W1 = f("od_w_in")[0]
    t1 = []
    for h in range(16):
        for part in range(4):
            t1.append(W1[:, part * 2048 + h * 128: part * 2048 + (h + 1) * 128])
    w_in1 = np.stack([_relay(t) for t in t1])
    wo1 = f("od_w_out")[0]
    w_out1 = np.stack([_relay(wo1[:, m * 128:(m + 1) * 128]) for m in range(16)])
    prm = np.zeros((128, NPRM), np.float32)

    def put(name, arr):
        o, w = PRM[name]
        prm[:, o:o + w] = arr
    put("ev_g", _cols(f("ev_norm_g")[0]))
    put("conv_b", _cols(f("conv_b")[0]))
    put("ln_g", _cols(f("conv_ln_g")[0]))
    put("ln_b", _cols(f("conv_ln_b")[0]))
    put("q_g", _cols(f("mla_q_norm_g")[0]))
    put("kv_g", _cols(f("mla_kv_norm_g")[0]))
    put("od_g", _cols(f("od_norm_g")[0]))
    put("fin_g", _cols(f("final_norm_g")))
    put("hg", f("hgrn_norm_g")[0].reshape(128, 1))
    lg = f("hgrn_lb_logits")
    put("l0", _cols(lg[0]))
    put("l1", _cols(lg[1]))
    cw = f("conv_w")[0]
    put("conv_w", cw.reshape(31, 8, 128).transpose(2, 1, 0).reshape(128, 8 * 31))
    cbf = np.zeros((128, 3, 128), np.float32)
    cbf[:, 0, :] = np.eye(128)
    cbf[:, 1, :] = 1.0
    cbf[:, 2, :] = np.triu(np.ones((128, 128)))
    cbf = cbf.astype(ml_dtypes.bfloat16)
    inv_freq = (1.0 / (np.float32(10000.0) ** (np.arange(0, 64, 2, dtype=np.float32) / np.float32(64)))).astype(np.float32)
    shared = dict(cbf=cbf, w_in0=w_in0, w_uq=w_uq, w_ukvk=w_ukvk, w_ukvv=w_ukvv, w_out0=w_out0, w_in1=w_in1,
                  w_out1=w_out1)
    per = []
    for c in range(8):
        b, j = c // 4, c % 4
        s0 = j * T
        xs_ = np.zeros((HALO + T, 2048), np.float32)
        if j > 0:
            xs_[:, :] = x[b, s0 - HALO:s0 + T, :]
        else:
            xs_[HALO:, :] = x[b, 0:T, :]
        xT = np.ascontiguousarray(xs_.reshape(HALO + T, 16, 128).transpose(2, 1, 0))
        pos = np.arange(s0, s0 + T, dtype=np.float32)
        ang = pos[:, None] * inv_freq[None, :]
        cs, sn = np.cos(ang).astype(np.float32).T, np.sin(ang).astype(np.float32).T
        rope = np.stack([np.concatenate([cs, cs], 0), np.concatenate([-sn, sn], 0)], 1).astype(np.float32)
        pr = prm.copy()
        o, w = PRM["sel"]
        pr[:, o + j] = 1.0
        o, w = PRM["vis"]
        for r in range(3):
            pr[:, o + r] = 1.0 if r < j else 0.0
        per.append(dict(xT=xT, prm=pr, rope=np.ascontiguousarray(rope)))
    return shared, per


_NC = {}


def _get(mode, l1=True):
    k = (mode, l1)
    if k not in _NC:
        _NC[k] = build(mode, l1)
    return _NC[k]


def _launch(nc, maps):
    return run_bass_kernel_spmd(nc, maps, core_ids=list(range(8))).results


def _assemble(res, name):
    out = np.zeros((2, 4096, 2048), np.float32)
    for c in range(8):
        b, j = c // 4, c % 4
        o = np.asarray(res[c][name])
        out[b, j * T:(j + 1) * T, :] = o.transpose(2, 1, 0).reshape(T, 2048)
    return out


def run_unfused(inputs, upto="C"):
    shared, per = prepare(inputs)
    mA = [dict(prm=per[c]["prm"], cbf=shared["cbf"], xT=per[c]["xT"], rope=per[c]["rope"],
               w_in0=shared["w_in0"][0:6], w_ukvk=shared["w_ukvk"], w_ukvv=shared["w_ukvv"]) for c in range(8)]
    rA = _launch(_get("A"), mA)
    kvg = [np.concatenate([np.asarray(rA[(c // 4) * 4 + r]["kvb"]) for r in range(4)], 0) for c in range(8)]
    mB = [dict(prm=per[c]["prm"], cbf=shared["cbf"], xT=per[c]["xT"], rope=per[c]["rope"], kvg=kvg[c],
               w_in0=shared["w_in0"], w_ukvk=shared["w_ukvk"], w_ukvv=shared["w_ukvv"], w_uq=shared["w_uq"],
               w_out0=shared["w_out0"]) for c in range(8)]
    rB = _launch(_get("B", False), mB)
    if upto == "B0":
        return _assemble(rB, "x1d")
    x1 = [np.asarray(rB[c]["x1d"]) for c in range(8)]
    mP = [dict(prm=per[c]["prm"], cbf=shared["cbf"], x1d=x1[c], w_in1=shared["w_in1"]) for c in range(8)]
    rP = _launch(_get("P"), mP)
    stg = [np.concatenate([np.asarray(rP[(c // 4) * 4 + r]["stb"]) for r in range(4)], 0) for c in range(8)]
    mC = [dict(prm=per[c]["prm"], cbf=shared["cbf"], x1d=x1[c], stg=stg[c],
               w_in1=shared["w_in1"], w_out1=shared["w_out1"]) for c in range(8)]
    rC = _launch(_get("C"), mC)
    return _assemble(rC, "outT")


def run_fused(inputs):
    shared, per = prepare(inputs)
    maps = [dict(prm=per[c]["prm"], cbf=shared["cbf"], xT=per[c]["xT"], rope=per[c]["rope"], **{
        k: shared[k] for k in ("w_in0", "w_ukvk", "w_ukvv", "w_uq", "w_out0", "w_in1", "w_out1")}) for c in range(8)]
    return _assemble(_launch(_get("F"), maps), "outT")


FUSED = True


def kernel(**inputs):
    if FUSED:
        return run_fused(inputs)
    return run_unfused(inputs)
```

```python
import numpy as np
import ml_dtypes
from contextlib import ExitStack
import concourse.bass as bass
import concourse.mybir as mybir
from concourse.bass_utils import run_bass_kernel_spmd

F32 = mybir.dt.float32
BF16 = mybir.dt.bfloat16
AF = mybir.ActivationFunctionType
ALU = mybir.AluOpType
ENGS = ("pe", "act", "dve", "pool", "sp")
T = 1024
HALO = 32
EPS = 1e-6
NEG = -30000.0


class Op:
    __slots__ = ("idx", "eng", "fn", "deps", "dma", "need_inc", "sem", "val", "extra_waits", "cc")

    def __init__(self, idx, eng, fn, deps, dma):
        self.idx = idx
        self.eng = eng
        self.fn = fn
        self.deps = deps
        self.dma = dma
        self.need_inc = False
        self.sem = None
        self.val = 0
        self.extra_waits = []
        self.cc = False


class Prog:
    def __init__(self, nc, n_dma_sems=8):
        self.nc = nc
        self.ops = []
        self.last_w = {}
        self.readers = {}
        self.barrier_deps = []
        self.last_on_eng = {}
        self.dmas_since_barrier = []
        self.n_dma_sems = n_dma_sems

    muted = False

    def op(self, eng, fn, reads=(), writes=(), dma=False):
        if self.muted:
            return -1
        idx = len(self.ops)
        deps = set(self.barrier_deps)
        for k in reads:
            w = self.last_w.get(k)
            if w is not None:
                deps.add(w)
        for k in writes:
            w = self.last_w.get(k)
            if w is not None:
                deps.add(w)
            for r in self.readers.get(k, ()):
                deps.add(r)
        o = Op(idx, eng, fn, deps, dma)
        self.ops.append(o)
        for k in writes:
            self.last_w[k] = idx
            self.readers[k] = []
        for k in reads:
            if k not in writes:
                self.readers.setdefault(k, []).append(idx)
        self.last_on_eng[eng] = idx
        if dma:
            self.dmas_since_barrier.append(idx)
        return idx

    def barrier(self):
        deps = set(self.last_on_eng.values()) | set(self.dmas_since_barrier)
        deps = {d for d in deps if not self.ops[d].cc}
        self.barrier_deps = sorted(deps)
        self.dmas_since_barrier = []
        keep = {k: w for k, w in self.last_w.items() if self.ops[w].cc}
        self.last_w = keep
        self.readers = {k: [] for k in keep}

    def dma(self, q, out, in_, reads=(), writes=(), **kw):
        return self.op(q, lambda e: e.dma_start(out=out, in_=in_, **kw), reads, writes, dma=True)

    def collective(self, fn, reads=(), writes=()):
        i = self.op("pool", fn, reads, writes, dma=True)
        self.ops[i].cc = True
        return i

    def emit(self, stack):
        nc = self.nc
        ops = self.ops
        for o in ops:
            for d in o.deps:
                od = ops[d]
                if od.dma:
                    od.need_inc = True
                elif od.eng == "pe" and o.eng == "pe" and not o.dma:
                    continue
                else:
                    od.need_inc = True
        esem = {e: stack.enter_context(nc.semaphore("s_" + e)) for e in ENGS}
        dsem = {}
        for q in ("sp", "act", "pool"):
            dsem[q] = [stack.enter_context(nc.semaphore("d_%s%d" % (q, i))) for i in range(self.n_dma_sems)]
        ecount = {e: 0 for e in ENGS}
        dcount = {q: [0] * self.n_dma_sems for q in dsem}
        drr = {q: 0 for q in dsem}
        for o in ops:
            if o.cc:
                o.sem = stack.enter_context(nc.semaphore("cc%d" % o.idx))
                o.val = 1
            elif o.dma:
                q = o.eng
                i = drr[q]
                drr[q] = (i + 1) % self.n_dma_sems
                prev = dcount[q][i]
                if prev > 0:
                    o.extra_waits.append((dsem[q][i], prev))
                dcount[q][i] = prev + 16
                o.sem = dsem[q][i]
                o.val = prev + 16
            elif o.need_inc:
                ecount[o.eng] += 1
                o.sem = esem[o.eng]
                o.val = ecount[o.eng]
        engobj = {"pe": "tensor", "act": "scalar", "dve": "vector", "pool": "gpsimd", "sp": "sync"}
        block = stack.enter_context(nc.Block())

        def make(engname):
            def body(eng):
                known = {}
                for o in ops:
                    if o.eng != engname:
                        continue
                    waits = list(o.extra_waits)
                    for d in sorted(o.deps):
                        od = ops[d]
                        if od.sem is None:
                            continue
                        if (not od.dma) and od.eng == "pe" and engname == "pe" and not o.dma:
                            continue
                        waits.append((od.sem, od.val))
                    best = {}
                    for s, v in waits:
                        key = id(s)
                        if key not in best or best[key][1] < v:
                            best[key] = (s, v)
                    for key, (s, v) in best.items():
                        if known.get(key, 0) >= v:
                            continue
                        eng.wait_ge(s, v)
                        known[key] = v
                    ins = o.fn(eng)
                    if o.sem is not None and ins is not None:
                        ins.then_inc(o.sem, 1 if (o.cc or not o.dma) else 16)
            return body

        for engname in ENGS:
            getattr(block, engobj[engname])(make(engname))


PRM = {}
_o = 0
for _n, _w in (("ev_g", 16), ("conv_b", 8), ("ln_g", 8), ("ln_b", 8), ("q_g", 4), ("kv_g", 4),
               ("od_g", 16), ("fin_g", 16), ("hg", 1), ("l0", 16), ("l1", 16), ("conv_w", 8 * 31),
               ("sel", 4), ("vis", 3)):
    PRM[_n] = (_o, _w)
    _o += _w
NPRM = _o

N_IN0 = 42
KVR = 1024 + 64 + 1024
CH = 64
NCH = T // CH
SW = 136


def build(mode, l1=True):
    import os
    CUT = os.environ.get("MK_CUT", "")
    nc = bass.Bass("TRN2", target_bir_lowering=False)
    L0 = mode in ("A", "B", "F")
    P1 = (mode in ("B", "F") and l1) or mode == "P"
    P2 = mode in ("C", "F")
    NH = int(os.environ.get("MK_HEADS", "16"))

    def decl(name, shape, dt, role):
        if role == "in":
            return nc.dram_tensor(name, shape, dt, kind="ExternalInput").ap()
        if role == "out":
            return nc.dram_tensor(name, shape, dt, kind="ExternalOutput").ap()
        return nc.dram_tensor(name, shape, dt).ap()

    prm_d = decl("prm", [128, NPRM], F32, "in")
    cbf_d = decl("cbf", [128, 3, 128], BF16, "in")
    if L0:
        xT = decl("xT", [128, 16, HALO + T], F32, "in")
        rope_d = decl("rope", [64, 2, T], F32, "in")
        w_in0 = decl("w_in0", [6 if mode == "A" else N_IN0, 128, 2048], F32, "in")
        w_ukvk = decl("w_ukvk", [8, 128, 4 * 128], F32, "in")
        w_ukvv = decl("w_ukvv", [2, 128, 4 * 512], F32, "in")
        if mode != "F":
            kvb_a = decl("kvb", [KVR, T], BF16, "out" if mode == "A" else "int")
    if mode in ("B", "F"):
        w_uq = decl("w_uq", [8, 128, 4 * 256], F32, "in")
        w_out0 = decl("w_out0", [16, 128, 2048], F32, "in")
        if mode == "B":
            kvg_a = decl("kvg", [4 * KVR, T], BF16, "in")
    if mode != "A":
        x1d = decl("x1d", [128, 16, T], F32, {"B": "out", "C": "in", "F": "int", "P": "in"}[mode])
    if P1 or P2:
        w_in1 = decl("w_in1", [64, 128, 2048], F32, "in")
    if P1:
        if mode != "F":
            stb_a = decl("stb", [2048, SW], F32, "out")
    if P2:
        if mode == "C":
            stg_a = decl("stg", [4 * 2048, SW], F32, "in")
        w_out1 = decl("w_out1", [16, 128, 2048], F32, "in")
        x2d = decl("x2d", [128, 16, T], F32, "int")
        outT = decl("outT", [128, 16, T], F32, "out")

    RG = [[0, 1, 2, 3], [4, 5, 6, 7]]
    if mode == "F":
        kb = [decl("kb%d" % i, [256, T], BF16, "int") for i in range(4)]
        kg = [decl("kg%d" % i, [1024, T], BF16, "int") for i in range(4)]
        kpb = decl("kpb", [64, T], BF16, "int")
        kpgd = decl("kpgd", [256, T], BF16, "int")
        vb = [decl("vb%d" % i, [256, T], BF16, "int") for i in range(4)]
        vgd = [decl("vgd%d" % i, [1024, T], BF16, "int") for i in range(4)]
        sbp = [decl("sbp%d" % i, [512, SW], F32, "int") for i in range(4)]
        sgp = [decl("sgp%d" % i, [2048, SW], F32, "int") for i in range(4)]

    st = ExitStack()
    with st:
        def AG(in_ap, out_ap, rkey, wkey):
            p.collective(lambda e: e.collective_compute("AllGather", ALU.bypass, replica_groups=RG,
                                                        ins=[in_ap.opt()], outs=[out_ap.opt()]),
                         reads=[rkey], writes=[wkey])

        nsfx = [""]

        def sb(name, shape, dt=F32, stack=None):
            return (stack or st).enter_context(nc.sbuf_tensor("sb_" + name + nsfx[0], shape, dt))

        p = Prog(nc)
        ps = [st.enter_context(nc.psum_tensor("ps%d" % i, [128, 512], F32)) for i in range(7)]
        psb = st.enter_context(nc.psum_tensor("psb", [128, 8, 128], BF16))
        PK = [("ps", i) for i in range(7)]
        psv = ps[6][:, :].bitcast(BF16)

        def MM(out, lhsT, rhs, start, stop, reads, writes):
            p.op("pe", lambda e: e.matmul(out, lhsT=lhsT, rhs=rhs, start=start, stop=stop), reads, writes)

        def TR(out, in_, reads, writes):
            p.op("pe", lambda e: e.transpose(out, in_, ident), reads + ["cbf"], writes)

        def ACT(out, in_, func, reads, writes, scale=1.0, bias=None):
            if bias is None:
                p.op("act", lambda e: e.activation(out=out, in_=in_, func=func, scale=scale), reads, writes)
            else:
                p.op("act", lambda e: e.activation(out=out, in_=in_, func=func, scale=scale, bias=bias), reads, writes)

        def TT(eng, out, in0, in1, op, reads, writes):
            p.op(eng, lambda e: e.tensor_tensor(out=out, in0=in0, in1=in1, op=op), reads, writes)

        def TS(eng, out, in0, s1, s2, op0, op1, reads, writes):
            if s2 is None:
                p.op(eng, lambda e: e.tensor_scalar(out=out, in0=in0, scalar1=s1, scalar2=None, op0=op0), reads, writes)
            else:
                p.op(eng, lambda e: e.tensor_scalar(out=out, in0=in0, scalar1=s1, scalar2=s2, op0=op0, op1=op1), reads, writes)

        def STT(out, in0, scalar, in1, op0, op1, reads, writes):
            p.op("dve", lambda e: e.scalar_tensor_tensor(out=out, in0=in0, scalar=scalar, in1=in1, op0=op0, op1=op1),
                 reads, writes)

        def CP(eng, out, in_, reads, writes):
            if eng == "act":
                p.op("act", lambda e: e.activation(out=out, in_=in_, func=AF.Identity), reads, writes)
            else:
                p.op(eng, lambda e: e.tensor_copy(out=out, in_=in_), reads, writes)

        def RECIP(out, in_, reads, writes):
            p.op("dve", lambda e: e.reciprocal(out=out, in_=in_), reads, writes)

        def MEMSET(eng, ap, val, writes):
            p.op(eng, lambda e: e.memset(ap, val), (), writes)

        prm = sb("prm", [128, NPRM])
        cbf = sb("cbf", [128, 3, 128], BF16)
        epsc = sb("epsc", [128, 1])
        lbc = sb("lbc", [128, 16])
        omlb = sb("omlb", [128, 16])
        p.dma("sp", prm[:], prm_d, writes=["prm"])
        p.dma("sp", cbf[:], cbf_d, writes=["cbf"])
        MEMSET("dve", epsc[:], EPS, ["epsc"])
        ident = cbf[:, 0, :]
        ones = cbf[:, 1, :]
        triu = cbf[:, 2, :]

        def pc(name, j=0, n=1):
            o, w = PRM[name]
            return prm[:, o + j:o + j + n]

        TT("dve", lbc[:], pc("l1", 0, 16), pc("l0", 0, 16), ALU.subtract, ["prm"], ["lbc"])
        ACT(lbc[:], lbc[:], AF.Sigmoid, ["lbc"], ["lbc"])
        TS("dve", omlb[:], lbc[:], -1.0, 1.0, ALU.mult, ALU.add, ["lbc"], ["omlb"])

        wst = [sb("wst%d" % i, [128, 2048]) for i in range(2)]
        wbf = [sb("wbf%d" % i, [128, 2048], BF16) for i in range(2)]
        wctr = [0]

        def wload(src, n=2048):
            i = wctr[0] % 2
            wctr[0] += 1
            p.dma("sp", wst[i][:, 0:n], src, writes=[("wst", i)])
            CP("pool", wbf[i][:, 0:n], wst[i][:, 0:n], [("wst", i)], [("wbf", i)])
            return wbf[i], ("wbf", i)

        mix = sb("mix", [128, 16, T], BF16)
        rstd = sb("rstd", [128, T])
        sqb = sb("sqb", [128, 512], BF16)
        xs = [sb("xs%d" % i, [128, T]) for i in range(2)]

        def rms_stream(src_ap, gname, ncol, col0, sink):
            nh = [(a, min(512, ncol - a)) for a in range(0, ncol, 512)]
            for (a, n) in nh:
                for c in range(16):
                    xb = xs[c % 2]
                    p.dma("sp", xb[:, 0:n], src_ap[:, c, col0 + a:col0 + a + n], writes=[("xs", c % 2)])
                    ACT(sqb[:, 0:n], xb[:, 0:n], AF.Square, [("xs", c % 2)], ["sqb"])
                    MM(ps[0][:, 0:n], ones, sqb[:, 0:n], c == 0, c == 15, ["sqb", "cbf"], [PK[0]])
                ACT(rstd[:, a:a + n], ps[0][:, 0:n], AF.Sqrt, [PK[0], "epsc"], ["rstd"], scale=1.0 / 2048, bias=epsc[:])
                RECIP(rstd[:, a:a + n], rstd[:, a:a + n], ["rstd"], ["rstd"])
                for c in range(16):
                    xb = xs[c % 2]
                    p.dma("sp", xb[:, 0:n], src_ap[:, c, col0 + a:col0 + a + n], writes=[("xs", c % 2)])
                    sink(c, a, n, xb, ("xs", c % 2), gname)

        def lin_tile(wb, wkey, kc, m, rhs_fn, rkeys, a, n, pst, pkey):
            for c in range(kc):
                MM(pst[0:m, 0:n], wb[:, c * 128:c * 128 + m], rhs_fn(c, a, n), c == 0, c == kc - 1,
                   [wkey] + rkeys, [pkey])

        def out_proj(w_d, src_ap, col0, dst_ap):
            for m in range(16):
                wb, wk = wload(w_d[m])
                xb = xs[m % 2]
                p.dma("sp", xb[:, :], src_ap[:, m, col0:col0 + T], writes=[("xs", m % 2)])
                for half in range(2):
                    pst, pk = ps[1 + half], PK[1 + half]
                    for c in range(16):
                        MM(pst[:, :], wb[:, c * 128:(c + 1) * 128], mix[:, c, half * 512:(half + 1) * 512],
                           c == 0, c == 15, [wk] + [("mix", c)], [pk])
                    TT("dve", xb[:, half * 512:(half + 1) * 512], xb[:, half * 512:(half + 1) * 512], pst[:, :],
                       ALU.add, [("xs", m % 2), pk], [("xs", m % 2)])
                p.dma("sp", dst_ap[:, m, :], xb[:, :], reads=[("xs", m % 2)], writes=["x_out"])

        def layer0():
            with ExitStack() as s0:
                rope = sb("rope", [64, 2, T], F32, s0)
                p.dma("sp", rope[:], rope_d, writes=["rope"])
                kn = sb("kn", [128, 8, T], BF16, s0)
                vt = sb("vt", [128, 8, T], BF16, s0)
                kpe = sb("kpe", [64, T], BF16, s0)
                cq = sb("cq", [128, 4, T], F32, s0)
                with ExitStack() as s1:
                    hT = sb("hT", [128, 16, T], BF16, s1)
                    hh = sb("hh", [128, 16, HALO], BF16, s1)
                    rsth = sb("rsth", [128, HALO], F32, s1)

                    def sink_h(c, a, n, xb, xk, gname):
                        STT(hT[:, c, a:a + n], xb[:, 0:n], pc(gname, c), rstd[:, a:a + n], ALU.mult, ALU.mult,
                            [xk, "rstd", "prm"], ["hT"])
                    rms_stream(xT, "ev_g", T, HALO, sink_h)
                    if mode != "A":
                        for c in range(16):
                            xb = xs[c % 2]
                            p.dma("sp", xb[:, 0:HALO], xT[:, c, 0:HALO], writes=[("xs", c % 2)])
                            ACT(sqb[:, 0:HALO], xb[:, 0:HALO], AF.Square, [("xs", c % 2)], ["sqb"])
                            MM(ps[0][:, 0:HALO], ones, sqb[:, 0:HALO], c == 0, c == 15, ["sqb", "cbf"], [PK[0]])
                        ACT(rsth[:], ps[0][:, 0:HALO], AF.Sqrt, [PK[0], "epsc"], ["rsth"], scale=1.0 / 2048,
                            bias=epsc[:])
                        RECIP(rsth[:], rsth[:], ["rsth"], ["rsth"])
                        for c in range(16):
                            xb = xs[c % 2]
                            p.dma("sp", xb[:, 0:HALO], xT[:, c, 0:HALO], writes=[("xs", c % 2)])
                            STT(hh[:, c, :], xb[:, 0:HALO], pc("ev_g", c), rsth[:], ALU.mult, ALU.mult,
                                [("xs", c % 2), "rsth", "prm"], ["hh"])

                    def h_rhs(c, a, n):
                        return hT[:, c, a:a + n]

                    def in0_tile(ti, m, evac):
                        wb, wk = wload(w_in0[ti])
                        for half in range(2):
                            pst, pk = ps[1 + half], PK[1 + half]
                            lin_tile(wb, wk, 16, m, h_rhs, ["hT"], half * 512, 512, pst, pk)
                            evac(half, pst, pk)
                        return wb, wk

                    with ExitStack() as s2:
                        ckv = sb("ckv", [128, 4, T], F32, s2)
                        ckn = sb("ckn", [128, 4, T], BF16, s2)
                        kp32 = sb("kp32", [64, 2, T], F32, s2)
                        for j in range(4):
                            in0_tile(j, 128, lambda half, pst, pk, j=j: CP(
                                "act", ckv[:, j, half * 512:(half + 1) * 512], pst[:, :], [pk], ["ckv"]))
                        for j in range(2):
                            in0_tile(4 + j, 64, lambda half, pst, pk, j=j: CP(
                                "act", kp32[:, j, half * 512:(half + 1) * 512], pst[0:64, :], [pk], ["kp32"]))
                        for half in range(2):
                            sl = slice(half * 512, (half + 1) * 512)
                            for j in range(4):
                                ACT(sqb[:, :], ckv[:, j, sl], AF.Square, ["ckv"], ["sqb"])
                                MM(ps[0][:, :], ones, sqb[:, :], j == 0, j == 3, ["sqb", "cbf"], [PK[0]])
                            ACT(rstd[:, sl], ps[0][:, :], AF.Sqrt, [PK[0], "epsc"], ["rstd"], scale=1.0 / 512,
                                bias=epsc[:])
                            RECIP(rstd[:, sl], rstd[:, sl], ["rstd"], ["rstd"])
                            for j in range(4):
                                STT(ckn[:, j, sl], ckv[:, j, sl], pc("kv_g", j), rstd[:, sl], ALU.mult, ALU.mult,
                                    ["ckv", "rstd", "prm"], ["ckn"])
                        TT("dve", kp32[:, 0, :], kp32[:, 0, :], rope[:, 0, :], ALU.mult, ["kp32", "rope"], ["kp32"])
                        TT("dve", kp32[:, 1, :], kp32[:, 1, :], rope[:, 1, :], ALU.mult, ["kp32", "rope"], ["kp32"])
                        TT("dve", kpe[:, :], kp32[:, 0, :], kp32[:, 1, :], ALU.add, ["kp32"], ["kpe"])
                        if mode == "F":
                            p.dma("sp", kpb, kpe[:, :], reads=["kpe"], writes=["kpb"])
                            AG(kpb, kpgd, "kpb", "kpgd")
                        else:
                            p.dma("sp", kvb_a[1024:1088, :], kpe[:, :], reads=["kpe"], writes=["kvb"])
                        for h in range(8):
                            wb, wk = wload(w_ukvk[h], 512)
                            for half in range(2):
                                pst, pk = ps[1 + half], PK[1 + half]
                                for c in range(4):
                                    MM(pst[:, :], wb[:, c * 128:(c + 1) * 128], ckn[:, c, half * 512:(half + 1) * 512],
                                       c == 0, c == 3, [wk, "ckn"], [pk])
                                CP("act", kn[:, h, half * 512:(half + 1) * 512], pst[:, :], [pk], [("kn", h)])
                            if mode == "F":
                                p.dma("sp", kb[h // 2][(h % 2) * 128:(h % 2 + 1) * 128, :], kn[:, h, :],
                                      reads=[("kn", h)], writes=[("kb", h // 2)])
                                if h % 2 == 1:
                                    AG(kb[h // 2], kg[h // 2], ("kb", h // 2), ("kg", h // 2))
                            else:
                                p.dma("sp", kvb_a[h * 128:(h + 1) * 128, :], kn[:, h, :], reads=[("kn", h)],
                                      writes=["kvb"])
                        for vh in range(2):
                            wb, wk = wload(w_ukvv[vh])
                            for tt in range(8):
                                pst, pk = ps[1 + tt % 2], PK[1 + tt % 2]
                                for c in range(4):
                                    MM(pst[:, :], ckn[:, c, tt * 128:(tt + 1) * 128], wb[:, c * 512:(c + 1) * 512],
                                       c == 0, c == 3, [wk, "ckn"], [pk])
                                CP("act", vt[:, tt, vh * 512:(vh + 1) * 512], pst[:, :], [pk], ["vt"])
                        if mode == "F":
                            for i in range(4):
                                p.dma("sp", vb[i].rearrange("(t p) n -> p t n", p=128), vt[:, 2 * i:2 * i + 2, :],
                                      reads=["vt"], writes=[("vb", i)])
                                AG(vb[i], vgd[i], ("vb", i), ("vgd", i))
                        else:
                            p.dma("sp", kvb_a[1088:1088 + 1024, :].rearrange("(t p) n -> p t n", p=128), vt[:, :, :],
                                  reads=["vt"], writes=["kvb"])
                    p.barrier()
                    if mode == "A":
                        return
                    with ExitStack() as s2:
                        u = sb("u", [128, HALO + T], BF16, s2)
                        zs = sb("zs", [128, T], BF16, s2)
                        t32 = sb("t32", [128, T], F32, s2)
                        dg = sb("dg", [128, 31, 128], BF16, s2)
                        y32 = sb("y32", [128, 512], F32, s2)
                        ybf = sb("ybf", [128, 512], BF16, s2)
                        d32 = sb("d32", [128, 512], F32, s2)
                        r32 = sb("r32", [128, 512], F32, s2)
                        ones128 = sb("ones128", [128, 128], BF16, s2)
                        MEMSET("dve", ones128[:], 1.0 / 128, ["ones128"])
                        for j in range(8):
                            wb, wk = wload(w_in0[6 + 3 * j])
                            for half in range(2):
                                pst, pk = ps[1 + half], PK[1 + half]
                                lin_tile(wb, wk, 16, 128, h_rhs, ["hT"], half * 512, 512, pst, pk)
                                CP("act", t32[:, half * 512:(half + 1) * 512], pst[:, :], [pk], ["t32"])
                            for c in range(16):
                                MM(ps[3][:, 0:HALO], wb[:, c * 128:(c + 1) * 128], hh[:, c, :], c == 0, c == 15,
                                   [wk, "hh"], [PK[3]])
                            wb2, wk2 = wload(w_in0[7 + 3 * j])
                            for c in range(16):
                                MM(ps[4][:, 0:HALO], wb2[:, c * 128:(c + 1) * 128], hh[:, c, :], c == 0, c == 15,
                                   [wk2, "hh"], [PK[4]])
                            ACT(d32[:, 0:HALO], ps[4][:, 0:HALO], AF.Sigmoid, [PK[4]], ["d32"])
                            TT("dve", u[:, 0:HALO], d32[:, 0:HALO], ps[3][:, 0:HALO], ALU.mult, ["d32", PK[3]], ["u"])
                            for half in range(2):
                                pst, pk = ps[1 + half], PK[1 + half]
                                lin_tile(wb2, wk2, 16, 128, h_rhs, ["hT"], half * 512, 512, pst, pk)
                                ACT(d32[:, :], pst[:, :], AF.Sigmoid, [pk], ["d32"])
                                TT("dve", u[:, HALO + half * 512:HALO + (half + 1) * 512], d32[:, :],
                                   t32[:, half * 512:(half + 1) * 512], ALU.mult, ["d32", "t32"], ["u"])
                            wb3, wk3 = wload(w_in0[8 + 3 * j])
                            for half in range(2):
                                pst, pk = ps[1 + half], PK[1 + half]
                                lin_tile(wb3, wk3, 16, 128, h_rhs, ["hT"], half * 512, 512, pst, pk)
                                ACT(zs[:, half * 512:(half + 1) * 512], pst[:, :], AF.Silu, [pk], ["zs"])
                            o_w = PRM["conv_w"][0] + j * 31
                            for k in range(31):
                                TS("pool", dg[:, k, :], ident, prm[:, o_w + k:o_w + k + 1], None, ALU.mult, None,
                                   ["cbf", "prm"], [("dg", k)])
                            for half in range(2):
                                for k in range(31):
                                    a = HALO + half * 512 - 30 + k
                                    MM(ps[5][:, :], dg[:, k, :], u[:, a:a + 512], k == 0, k == 30, [("dg", k), "u"],
                                       [PK[5]])
                                ACT(y32[:], ps[5][:, :], AF.Identity, [PK[5], "prm"], ["y32"], bias=pc("conv_b", j))
                                CP("dve", ybf[:], y32[:], ["y32"], ["ybf"])
                                MM(ps[6][:, :], ones128[:], ybf[:], True, True, ["ones128", "ybf"], [PK[6]])
                                TT("dve", d32[:], y32[:], ps[6][:, :], ALU.subtract, ["y32", PK[6]], ["d32"])
                                ACT(ybf[:], d32[:], AF.Square, ["d32"], ["ybf"])
                                MM(ps[6][:, :], ones128[:], ybf[:], True, True, ["ones128", "ybf"], [PK[6]])
                                ACT(r32[:], ps[6][:, :], AF.Sqrt, [PK[6], "epsc"], ["r32"], bias=epsc[:])
                                RECIP(r32[:], r32[:], ["r32"], ["r32"])
                                TT("dve", d32[:], d32[:], r32[:], ALU.mult, ["d32", "r32"], ["d32"])
                                ACT(d32[:], d32[:], AF.Silu, ["d32", "prm"], ["d32"], scale=pc("ln_g", j),
                                    bias=pc("ln_b", j))
                                TT("dve", mix[:, j, half * 512:(half + 1) * 512], d32[:],
                                   zs[:, half * 512:(half + 1) * 512], ALU.mult, ["d32", "zs"], [("mix", j)])
                    p.barrier()
                    if CUT == "B":
                        p.muted = True
                    for j in range(8):
                        in0_tile(30 + j, 128, lambda half, pst, pk, j=j: ACT(
                            mix[:, 8 + j, half * 512:(half + 1) * 512], pst[:, :], AF.Silu, [pk], [("mix", 8 + j)]))
                    for j in range(4):
                        in0_tile(38 + j, 128, lambda half, pst, pk, j=j: CP(
                            "act", cq[:, j, half * 512:(half + 1) * 512], pst[:, :], [pk], ["cq"]))
                p.barrier()
                if CUT == "B2":
                    p.muted = True
                qn = sb("qn", [128, 8, T], BF16, s0)
                qpe = sb("qpe", [64, 8, T], BF16, s0)
                with ExitStack() as s2:
                    cqn = sb("cqn", [128, 4, T], BF16, s2)
                    qa = sb("qa", [64, 512], F32, s2)
                    qb = sb("qb", [64, 512], F32, s2)
                    for half in range(2):
                        sl = slice(half * 512, (half + 1) * 512)
                        for j in range(4):
                            ACT(sqb[:, :], cq[:, j, sl], AF.Square, ["cq"], ["sqb"])
                            MM(ps[0][:, :], ones, sqb[:, :], j == 0, j == 3, ["sqb", "cbf"], [PK[0]])
                        ACT(rstd[:, sl], ps[0][:, :], AF.Sqrt, [PK[0], "epsc"], ["rstd"], scale=1.0 / 512,
                            bias=epsc[:])
                        RECIP(rstd[:, sl], rstd[:, sl], ["rstd"], ["rstd"])
                        for j in range(4):
                            STT(cqn[:, j, sl], cq[:, j, sl], pc("q_g", j), rstd[:, sl], ALU.mult, ALU.mult,
                                ["cq", "rstd", "prm"], ["cqn"])
                    scale = 192.0 ** -0.5
                    for h in range(8):
                        wb, wk = wload(w_uq[h], 1024)
                        for half in range(2):
                            sl = slice(half * 512, (half + 1) * 512)
                            for c in range(4):
                                MM(ps[1][:, :], wb[:, c * 256:c * 256 + 128], cqn[:, c, sl], c == 0, c == 3,
                                   [wk, "cqn"], [PK[1]])
                            ACT(qn[:, h, sl], ps[1][:, :], AF.Identity, [PK[1]], [("qn", h)], scale=scale)
                            for c in range(4):
                                MM(ps[2][0:64, :], wb[:, c * 256 + 128:c * 256 + 192], cqn[:, c, sl], c == 0, c == 3,
                                   [wk, "cqn"], [PK[2]])
                            for c in range(4):
                                MM(ps[3][0:64, :], wb[:, c * 256 + 192:c * 256 + 256], cqn[:, c, sl], c == 0, c == 3,
                                   [wk, "cqn"], [PK[3]])
                            TT("dve", qa[:], ps[2][0:64, :], rope[:, 0, sl], ALU.mult, [PK[2], "rope"], ["qa"])
                            TT("dve", qb[:], ps[3][0:64, :], rope[:, 1, sl], ALU.mult, [PK[3], "rope"], ["qb"])
                            TT("dve", qa[:], qa[:], qb[:], ALU.add, ["qa", "qb"], ["qa"])
                            ACT(qpe[:, h, sl], qa[:], AF.Identity, ["qa"], [("qpe", h)], scale=scale)
                p.barrier()
                if CUT == "C":
                    p.muted = True
                with ExitStack() as s2:
                    kng = [sb("kng%d" % i, [128, 3 * T], BF16, s2) for i in range(2)]
                    vg = [sb("vg%d" % i, [128, 24, 128], BF16, s2) for i in range(2)]
                    kpg = sb("kpg", [64, 3 * T], BF16, s2)
                    onesv = sb("onesv", [128, 3, 128], BF16, s2)
                    pT = [sb("pT%d" % i, [128, 512], BF16, s2) for i in range(3)]
                    rl = sb("rl", [128, 512], F32, s2)
                    o32 = sb("o32", [128, 512], F32, s2)
                    for r in range(3):
                        if mode == "F":
                            p.dma("sp", kpg[:, r * T:(r + 1) * T], kpgd[r * 64:(r + 1) * 64, :],
                                  reads=["kpgd"], writes=["kpg"])
                        else:
                            p.dma("sp", kpg[:, r * T:(r + 1) * T], kvg_a[r * KVR + 1024:r * KVR + 1088, :],
                                  reads=["kvg"], writes=["kpg"])
                        TS("dve", onesv[:, r, :], ones, pc("vis", r), None, ALU.mult, None, ["cbf", "prm"], ["onesv"])
                    step = [0]
                    for h in range(8):
                        gi = h % 2
                        for r in range(3):
                            if mode == "F":
                                o_ = r * 256 + (h % 2) * 128
                                p.dma("sp", kng[gi][:, r * T:(r + 1) * T], kg[h // 2][o_:o_ + 128, :],
                                      reads=[("kg", h // 2)], writes=[("kng", gi)])
                                for i in range(4):
                                    p.dma("sp", vg[gi][:, r * 8 + 2 * i:r * 8 + 2 * i + 2, :],
                                          vgd[i][r * 256:(r + 1) * 256, h * 128:(h + 1) * 128].rearrange(
                                              "(t p) n -> p t n", p=128),
                                          reads=[("vgd", i)], writes=[("vg", gi)])
                            else:
                                p.dma("sp", kng[gi][:, r * T:(r + 1) * T],
                                      kvg_a[r * KVR + h * 128:r * KVR + (h + 1) * 128, :],
                                      reads=["kvg"], writes=[("kng", gi)])
                                p.dma("sp", vg[gi][:, r * 8:(r + 1) * 8, :],
                                      kvg_a[r * KVR + 1088:r * KVR + 1088 + 1024, h * 128:(h + 1) * 128].rearrange(
                                          "(t p) n -> p t n", p=128),
                                      reads=["kvg"], writes=[("vg", gi)])
                            TS("pool", vg[gi][:, r * 8:(r + 1) * 8, :], vg[gi][:, r * 8:(r + 1) * 8, :], pc("vis", r),
                               None, ALU.mult, None, [("vg", gi), "prm"], [("vg", gi)])
                        for qh in range(2):
                            q0 = qh * 512
                            tiles = []
                            nown = 4 if qh == 0 else 8
                            for kt in range(nown):
                                lo = max(0, kt * 128 - q0)
                                diag = (kt * 128 >= q0)
                                tiles.append(("own", kt, lo, diag))
                            for g in range(24):
                                tiles.append(("g", g, 0, False))
                            info = {}

                            def qk(ti):
                                kind, kt, lo, diag = tiles[ti]
                                n = 512 - lo
                                si = step[0] % 3
                                step[0] += 1
                                pss, pks = ps[2 + si], PK[2 + si]
                                if kind == "own":
                                    kT = kn[:, h, kt * 128:(kt + 1) * 128]
                                    kP = kpe[:, kt * 128:(kt + 1) * 128]
                                    vv = vt[:, kt, h * 128:(h + 1) * 128]
                                    ov = ones
                                    rk = [("kn", h), "kpe", "vt"]
                                else:
                                    kT = kng[gi][:, kt * 128:(kt + 1) * 128]
                                    kP = kpg[:, kt * 128:(kt + 1) * 128]
                                    vv = vg[gi][:, kt, :]
                                    ov = onesv[:, kt // 8, :]
                                    rk = [("kng", gi), "kpg", ("vg", gi), "onesv"]
                                MM(pss[:, 0:n], kT, qn[:, h, q0 + lo:q0 + 512], True, False, rk + [("qn", h)], [pks])
                                MM(pss[:, 0:n], kP, qpe[:, h, q0 + lo:q0 + 512], False, True, rk + [("qpe", h)], [pks])
                                info[ti] = (si, n, lo, diag, pss, pks, vv, ov, rk)

                            def rest(ti):
                                si, n, lo, diag, pss, pks, vv, ov, rk = info[ti]
                                ACT(pT[si][:, 0:n], pss[:, 0:n], AF.Exp, [pks], [("pT", si)])
                                if diag:
                                    TT("dve", pT[si][:, 0:128], pT[si][:, 0:128], triu, ALU.mult,
                                       [("pT", si), "cbf"], [("pT", si)])
                                first = (ti == 0)
                                last = (ti == len(tiles) - 1)
                                MM(ps[0][:, lo:512], vv, pT[si][:, 0:n], first, last, rk + [("pT", si)], [PK[0]])
                                MM(ps[1][:, lo:512], ov, pT[si][:, 0:n], first, last, rk + ["cbf", ("pT", si)], [PK[1]])

                            qk(0)
                            for ti in range(len(tiles)):
                                if ti + 1 < len(tiles):
                                    qk(ti + 1)
                                rest(ti)
                            CP("act", rl[:], ps[1][:, :], [PK[1]], ["rl"])
                            RECIP(rl[:], rl[:], ["rl"], ["rl"])
                            TT("dve", o32[:], ps[0][:, :], rl[:], ALU.mult, [PK[0], "rl"], ["o32"])
                            TT("dve", mix[:, 8 + h, q0:q0 + 512], o32[:], mix[:, 8 + h, q0:q0 + 512], ALU.mult,
                               ["o32", ("mix", 8 + h)], [("mix", 8 + h)])
            p.barrier()
            if CUT == "D":
                p.muted = True
            out_proj(w_out0, xT, HALO, x1d)
            p.muted = False
            p.barrier()

        def layer1(pass2):
            nsfx[0] = "_p2" if pass2 else "_p1"
            with ExitStack() as s0:
                h1 = sb("h1", [128, 16, T], BF16, s0)
                rmask = sb("rmask", [128, T], F32, s0)
                MEMSET("dve", rmask[:], 1.0, ["rmask"])
                MEMSET("dve", rmask[:].rearrange("p (c t) -> p c t", t=CH)[:, :, 0:1], 0.0, ["rmask"])

                def sink_h(c, a, n, xb, xk, gname):
                    STT(h1[:, c, a:a + n], xb[:, 0:n], pc(gname, c), rstd[:, a:a + n], ALU.mult, ALU.mult,
                        [xk, "rstd", "prm"], ["h1"])
                rms_stream(x1d, "od_g", T, 0, sink_h)

                def h_rhs(c, a, n):
                    return h1[:, c, a:a + n]

                sin_ = None
                if pass2:
                    sin_ = sb("sin", [128, 16, 128], F32, s0)
                    with ExitStack() as s1:
                        G = sb("G", [128, 3, 16, SW], F32, s1)
                        t2 = sb("t2", [128, 128], F32, s1)
                        t3 = sb("t3", [128, 128], F32, s1)
                        for r in range(3):
                            if mode == "F":
                                for i in range(4):
                                    p.dma("sp", G[:, r, 4 * i:4 * i + 4, :],
                                          sgp[i][r * 512:(r + 1) * 512, :].rearrange("(h p) w -> p h w", p=128),
                                          reads=[("sgp", i)], writes=["G"])
                            else:
                                p.dma("sp", G[:, r, :, :],
                                      stg_a[r * 2048:(r + 1) * 2048, :].rearrange("(h p) w -> p h w", p=128),
                                      reads=["stg"], writes=["G"])
                        for h in range(16):
                            S0, S1, S2 = G[:, 0, h, 0:128], G[:, 1, h, 0:128], G[:, 2, h, 0:128]
                            D1, D2 = G[:, 1, h, 128:129], G[:, 2, h, 128:129]
                            STT(t2[:], S0, D1, S1, ALU.mult, ALU.add, ["G"], ["t2"])
                            STT(t3[:], t2[:], D2, S2, ALU.mult, ALU.add, ["G", "t2"], ["t3"])
                            TS("dve", sin_[:, h, :], S0, pc("sel", 1), None, ALU.mult, None, ["G", "prm"], ["sin"])
                            STT(sin_[:, h, :], t2[:], pc("sel", 2), sin_[:, h, :], ALU.mult, ALU.add,
                                ["t2", "prm", "sin"], ["sin"])
                            STT(sin_[:, h, :], t3[:], pc("sel", 3), sin_[:, h, :], ALU.mult, ALU.add,
                                ["t3", "prm", "sin"], ["sin"])
                    p.barrier()

                sg = sb("sg", [128, T], F32, s0)
                ff = sb("ff", [128, T], F32, s0)
                lg = sb("lg", [128, T], F32, s0)
                bb = sb("bb", [128, T], F32, s0)
                khT = sb("khT", [128, T], BF16, s0)
                vT = sb("vT", [128, T], BF16, s0)
                ebc = sb("ebc", [128, NCH], F32, s0)
                kt_all = sb("kt_all", [CH, NCH, 128], BF16, s0)
                vt_all = sb("vt_all", [CH, NCH, 128], BF16, s0)
                psv3 = psv.rearrange("p (j k) -> p j k", k=128)
                ps3b = ps[3][:, :].bitcast(BF16).rearrange("p (j k) -> p j k", k=128)
                ps5b = ps[5][:, :].bitcast(BF16).rearrange("p (j k) -> p j k", k=128)
                banks_k = [(psb, "psb"), (psv3, PK[6])]
                banks_v = [(ps3b, PK[3]), (ps5b, PK[5])]
                S = sb("S", [128, 128], F32, s0)
                sst = sb("sst", [128, SW], F32, s0)
                MEMSET("dve", sst[:, 128:SW], 0.0, ["sst"])
                if pass2:
                    q32 = sb("q32", [128, T], F32, s0)
                    o32 = sb("o32l", [128, T], F32, s0)
                    gs = sb("gs", [128, T], BF16, s0)
                    qtT = sb("qtT", [128, T], BF16, s0)
                    ktT = sb("ktT", [128, T], BF16, s0)
                    qhT = sb("qhT", [128, T], BF16, s0)
                    negr = sb("negr", [128, NCH], F32, s0)
                    Sbf = [sb("Sbf%d" % i, [128, 128], BF16, s0) for i in range(2)]
                    Am_all = sb("Am_all", [CH, NCH, CH], BF16, s0)
                    rs1 = sb("rs1", [128, 512], F32, s0)

                def tile_in(ti, evac):
                    wb, wk = wload(w_in1[ti])
                    for half in range(2):
                        pst, pk = ps[1 + half], PK[1 + half]
                        lin_tile(wb, wk, 16, 128, h_rhs, ["h1"], half * 512, 512, pst, pk)
                        evac(half, pst, pk)

                for h in range(NH):
                    tile_in(4 * h + 1, lambda half, pst, pk: ACT(sg[:, half * 512:(half + 1) * 512], pst[:, :],
                                                                AF.Sigmoid, [pk], ["sg"]))
                    tile_in(4 * h + 2, lambda half, pst, pk: CP("act", vT[:, half * 512:(half + 1) * 512], pst[:, :],
                                                               [pk], ["vT"]))
                    if pass2:
                        tile_in(4 * h + 0, lambda half, pst, pk: CP("act", q32[:, half * 512:(half + 1) * 512],
                                                                   pst[:, :], [pk], ["q32"]))
                        tile_in(4 * h + 3, lambda half, pst, pk: ACT(gs[:, half * 512:(half + 1) * 512], pst[:, :],
                                                                    AF.Silu, [pk], ["gs"]))
                    TS("dve", ff[:], sg[:], omlb[:, h:h + 1], lbc[:, h:h + 1], ALU.mult, ALU.add,
                       ["sg", "omlb", "lbc"], ["ff"])
                    if CUT == "L1a":
                        p.muted = True
                    ACT(lg[:], ff[:], AF.Ln, ["ff"], ["lg"])
                    TS("dve", sg[:], ff[:], -1.0, 1.0, ALU.mult, ALU.add, ["ff"], ["sg"])
                    p.op("dve", lambda e: e.tensor_tensor_scan(out=bb[:], data0=rmask[:], data1=lg[:], initial=0.0,
                                                               op0=ALU.mult, op1=ALU.add),
                         ["rmask", "lg"], ["bb"])
                    bb3 = bb[:].rearrange("p (c t) -> p c t", t=CH)
                    if CUT == "L1b":
                        p.muted = True
                    ACT(ebc[:], bb3[:, :, CH - 1], AF.Exp, ["bb"], ["ebc"])
                    lg3 = lg[:].rearrange("p (c t) -> p c t", t=CH)
                    TT("dve", lg3[:, :, 0], lg3[:, :, 0], bb3[:, :, CH - 1], ALU.subtract, ["lg", "bb"], ["lg"])
                    p.op("dve", lambda e: e.tensor_tensor_scan(out=ff[:], data0=rmask[:], data1=lg[:], initial=0.0,
                                                               op0=ALU.mult, op1=ALU.add),
                         ["rmask", "lg"], ["ff"])
                    ACT(ff[:], ff[:], AF.Exp, ["ff"], ["ff"], scale=-1.0)
                    TT("dve", khT[:], sg[:], ff[:], ALU.mult, ["sg", "ff"], ["khT"])
                    if pass2:
                        TT("dve", negr[:], bb3[:, :, CH - 1], bb3[:, :, CH // 2 - 1], ALU.subtract, ["bb"], ["negr"])
                        TT("dve", lg3[:, :, 0], lg3[:, :, 0], negr[:], ALU.add, ["lg", "negr"], ["lg"])
                        p.op("dve", lambda e: e.tensor_tensor_scan(out=ff[:], data0=rmask[:], data1=lg[:],
                                                                   initial=0.0, op0=ALU.mult, op1=ALU.add),
                             ["rmask", "lg"], ["ff"])
                        ACT(lg[:], ff[:], AF.Exp, ["ff"], ["lg"])
                        TT("dve", qtT[:], q32[:], lg[:], ALU.mult, ["q32", "lg"], ["qtT"])
                        ACT(lg[:], ff[:], AF.Exp, ["ff"], ["lg"], scale=-1.0)
                        TT("dve", ktT[:], sg[:], lg[:], ALU.mult, ["sg", "lg"], ["ktT"])
                        ACT(ff[:], bb[:], AF.Exp, ["bb"], ["ff"])
                        TT("dve", qhT[:], q32[:], ff[:], ALU.mult, ["q32", "ff"], ["qhT"])
                        CP("dve", S[:], sin_[:, h, :], ["sin"], ["S"])
                        CP("act", Sbf[1][:], sin_[:, h, :], ["sin"], [("Sbf", 1)])
                    else:
                        MEMSET("dve", S[:], 0.0, ["S"])
                    if CUT == "L1c":
                        p.muted = True
                    for g in range(2):
                        bk, keyk = banks_k[g]
                        for j in range(8):
                            c = g * 8 + j
                            TR(bk[0:CH, j, :], khT[:, c * CH:(c + 1) * CH], ["khT"], [keyk])
                        CP("act", kt_all[:, g * 8:(g + 1) * 8, :], bk[0:CH, :, :], [keyk], [("kt", g)])
                        bv, keyv = banks_v[g]
                        for j in range(8):
                            c = g * 8 + j
                            TR(bv[0:CH, j, :], vT[:, c * CH:(c + 1) * CH], ["vT"], [keyv])
                        CP("dve", vt_all[:, g * 8:(g + 1) * 8, :], bv[0:CH, :, :], [keyv], [("vt", g)])
                    if pass2:
                        for g in range(2):
                            bankA, keyA = (ps[4], PK[4]) if g == 0 else (ps[3], PK[3])
                            for j in range(8):
                                c = g * 8 + j
                                cs = slice(c * CH, (c + 1) * CH)
                                MM(bankA[0:CH, j * CH:(j + 1) * CH], ktT[:, cs], qtT[:, cs], True, True,
                                   ["ktT", "qtT"], [keyA])
                            for j in range(8):
                                c = g * 8 + j
                                TT("dve", Am_all[:, c, :], bankA[0:CH, j * CH:(j + 1) * CH], triu[0:CH, 0:CH],
                                   ALU.mult, [keyA, "cbf"], [("Am", c)])
                    for c in range(NCH):
                        cs = slice(c * CH, (c + 1) * CH)
                        g, j = divmod(c, 8)
                        dsb, dsk = ps[1 + c % 2], PK[1 + c % 2]
                        if pass2:
                            ob, ok = (ps[0], PK[0]) if g == 0 else (ps[5], PK[5])
                            MM(ob[:, j * CH:(j + 1) * CH], vt_all[:, c, :], Am_all[:, c, :], True, False,
                               [("vt", g), ("Am", c)], [ok])
                        MM(dsb[:, 0:128], kt_all[:, c, :], vt_all[:, c, :], True, True, [("kt", g), ("vt", g)], [dsk])
                        if pass2:
                            MM(ob[:, j * CH:(j + 1) * CH], Sbf[(c + 1) % 2][:], qhT[:, cs], False, True,
                               [("Sbf", (c + 1) % 2), "qhT"], [ok])
                        STT(S[:], S[:], ebc[:, c:c + 1], dsb[:, 0:128], ALU.mult, ALU.add, ["S", "ebc", dsk], ["S"])
                        if pass2 and c < NCH - 1:
                            CP("act", Sbf[c % 2][:], S[:], ["S"], [("Sbf", c % 2)])
                    if pass2:
                        CP("act", o32[:, 0:512], ps[0][:, :], [PK[0]], ["o32l"])
                        CP("act", o32[:, 512:1024], ps[5][:, :], [PK[5]], ["o32l"])
                    if CUT == "L1d":
                        p.muted = True
                    if not pass2:
                        CP("dve", sst[:, 0:128], S[:], ["S"], ["sst"])
                        if os.environ.get("MK_NOTAIL"):
                            MEMSET("dve", sst[:, 128:129], 0.0, ["sst"])
                        else:
                            TS("dve", sst[:, 128:129], ebc[:, 0:1], 1.0, None, ALU.mult, None, ["ebc", "sst"], ["sst"])
                            for c in range(1, NCH):
                                TT("dve", sst[:, 128:129], sst[:, 128:129], ebc[:, c:c + 1], ALU.mult,
                                   ["ebc", "sst"], ["sst"])
                        if mode == "F":
                            p.dma("sp", sbp[h // 4][(h % 4) * 128:(h % 4 + 1) * 128, :], sst[:, :],
                                  reads=["sst"], writes=[("sbp", h // 4)])
                            if h % 4 == 3:
                                AG(sbp[h // 4], sgp[h // 4], ("sbp", h // 4), ("sgp", h // 4))
                        else:
                            p.dma("sp", stb_a[h * 128:(h + 1) * 128, :], sst[:, :], reads=["sst"],
                                  writes=["stb"])
                    else:
                        for half in range(2):
                            sl = slice(half * 512, (half + 1) * 512)
                            ACT(sqb[:, :], o32[:, sl], AF.Square, ["o32l"], ["sqb"])
                            MM(ps[0][:, :], ones, sqb[:, :], True, True, ["sqb", "cbf"], [PK[0]])
                            ACT(rs1[:], ps[0][:, :], AF.Sqrt, [PK[0], "epsc"], ["rs1"], scale=1.0 / 128, bias=epsc[:])
                            RECIP(rs1[:], rs1[:], ["rs1"], ["rs1"])
                            STT(rs1[:], o32[:, sl], pc("hg"), rs1[:], ALU.mult, ALU.mult, ["o32l", "prm", "rs1"], ["rs1"])
                            TT("dve", mix[:, h, sl], rs1[:], gs[:, sl], ALU.mult, ["rs1", "gs"], [("mix", h)])
            p.barrier()

        def finale():
            out_proj(w_out1, x1d, 0, x2d)
            p.barrier()

            def sink_o(c, a, n, xb, xk, gname):
                STT(xb[:, 0:n], xb[:, 0:n], pc(gname, c), rstd[:, a:a + n], ALU.mult, ALU.mult,
                    [xk, "rstd", "prm"], [xk])
                p.dma("sp", outT[:, c, a:a + n], xb[:, 0:n], reads=[xk], writes=["outT"])
            rms_stream(x2d, "fin_g", T, 0, sink_o)

        if L0:
            layer0()
        if P1:
            layer1(False)
        if P2:
            layer1(True)
            finale()
        p.muted = False
        p.barrier()
        p.op("sp", lambda e: None, reads=(), writes=["done"])
        p.emit(st)
    return nc


def _relay(tile):
    K, n = tile.shape
    return np.ascontiguousarray(tile.reshape(K // 128, 128, n).transpose(1, 0, 2).reshape(128, (K // 128) * n))


def _cols(v):
    n = v.shape[0] // 128
    return np.ascontiguousarray(v.reshape(n, 128).T)


def prepare(inputs):
    f = lambda k: np.asarray(inputs[k], dtype=np.float32)
    x = f("x")
    W = f("ev_w_in")[0]
    a_v, a_g, a_z = W[:, 0:1024], W[:, 1024:2048], W[:, 2048:3072]
    c_q, c_kv, k_pe, b_z = W[:, 3072:3584], W[:, 3584:4096], W[:, 4096:4160], W[:, 4160:5184]
    zpad = np.zeros((2048, 64), np.float32)
    sw = np.concatenate([np.arange(32, 64), np.arange(0, 32)])
    tiles = [c_kv[:, j * 128:(j + 1) * 128] for j in range(4)]
    tiles += [np.concatenate([k_pe, zpad], 1), np.concatenate([k_pe[:, sw], zpad], 1)]
    for j in range(8):
        sl = slice(j * 128, (j + 1) * 128)
        tiles += [a_v[:, sl], a_g[:, sl], a_z[:, sl]]
    tiles += [b_z[:, j * 128:(j + 1) * 128] for j in range(8)]
    tiles += [c_q[:, j * 128:(j + 1) * 128] for j in range(4)]
    w_in0 = np.stack([_relay(t) for t in tiles])
    uq = f("mla_w_uq")[0]
    w_uq = np.stack([_relay(np.concatenate([uq[:, h, 0:128], uq[:, h, 128:192], uq[:, h, 128:192][:, sw]], 1))
                     for h in range(8)])
    ukv = f("mla_w_ukv")[0]
    w_ukvk = np.stack([_relay(ukv[:, h, 0:128]) for h in range(8)])
    vfull = ukv[:, :, 128:256].reshape(512, 1024)
    w_ukvv = np.stack([_relay(vfull[:, 0:512]), _relay(vfull[:, 512:1024])])
    wo0 = f("ev_w_out")[0]
    w_out0 = np.stack([_relay(wo0[:, m * 128:(m + 1) * 128]) for m in range(16)])
    W1 = f("od_w_in")[0]
    t1 = []
    for h in range(16):
        for part in range(4):
            t1.append(W1[:, part * 2048 + h * 128: part * 2048 + (h + 1) * 128])
    w_in1 = np.stack([_relay(t) for t in t1])
    wo1 = f("od_w_out")[0]
    w_out1 = np.stack([_relay(wo1[:, m * 128:(m + 1) * 128]) for m in range(16)])
    prm = np.zeros((128, NPRM), np.float32)

    def put(name, arr):
        o, w = PRM[name]
        prm[:, o:o + w] = arr
    put("ev_g", _cols(f("ev_norm_g")[0]))
    put("conv_b", _cols(f("conv_b")[0]))
    put("ln_g", _cols(f("conv_ln_g")[0]))
    put("ln_b", _cols(f("conv_ln_b")[0]))
    put("q_g", _cols(f("mla_q_norm_g")[0]))
    put("kv_g", _cols(f("mla_kv_norm_g")[0]))
    put("od_g", _cols(f("od_norm_g")[0]))
    put("fin_g", _cols(f("final_norm_g")))
    put("hg", f("hgrn_norm_g")[0].reshape(128, 1))
    lg = f("hgrn_lb_logits")
    put("l0", _cols(lg[0]))
    put("l1", _cols(lg[1]))
    cw = f("conv_w")[0]
    put("conv_w", cw.reshape(31, 8, 128).transpose(2, 1, 0).reshape(128, 8 * 31))
    cbf = np.zeros((128, 3, 128), np.float32)
    cbf[:, 0, :] = np.eye(128)
    cbf[:, 1, :] = 1.0
    cbf[:, 2, :] = np.triu(np.ones((128, 128)))
    cbf = cbf.astype(ml_dtypes.bfloat16)
    inv_freq = (1.0 / (np.float32(10000.0) ** (np.arange(0, 64, 2, dtype=np.float32) / np.float32(64)))).astype(np.float32)
    shared = dict(cbf=cbf, w_in0=w_in0, w_uq=w_uq, w_ukvk=w_ukvk, w_ukvv=w_ukvv, w_out0=w_out0, w_in1=w_in1,
                  w_out1=w_out1)
    per = []
    for c in range(8):
        b, j = c // 4, c % 4
        s0 = j * T
        xs_ = np.zeros((HALO + T, 2048), np.float32)
        if j > 0:
            xs_[:, :] = x[b, s0 - HALO:s0 + T, :]
        else:
            xs_[HALO:, :] = x[b, 0:T, :]
        xT = np.ascontiguousarray(xs_.reshape(HALO + T, 16, 128).transpose(2, 1, 0))
        pos = np.arange(s0, s0 + T, dtype=np.float32)
        ang = pos[:, None] * inv_freq[None, :]
        cs, sn = np.cos(ang).astype(np.float32).T, np.sin(ang).astype(np.float32).T
        rope = np.stack([np.concatenate([cs, cs], 0), np.concatenate([-sn, sn], 0)], 1).astype(np.float32)
        pr = prm.copy()
        o, w = PRM["sel"]
        pr[:, o + j] = 1.0
        o, w = PRM["vis"]
        for r in range(3):
            pr[:, o + r] = 1.0 if r < j else 0.0
        per.append(dict(xT=xT, prm=pr, rope=np.ascontiguousarray(rope)))
    return shared, per


_NC = {}


def _get(mode, l1=True):
    k = (mode, l1)
    if k not in _NC:
        _NC[k] = build(mode, l1)
    return _NC[k]


def _launch(nc, maps):
    return run_bass_kernel_spmd(nc, maps, core_ids=list(range(8))).results


def _assemble(res, name):
    out = np.zeros((2, 4096, 2048), np.float32)
    for c in range(8):
        b, j = c // 4, c % 4
        o = np.asarray(res[c][name])
        out[b, j * T:(j + 1) * T, :] = o.transpose(2, 1, 0).reshape(T, 2048)
    return out


def run_unfused(inputs, upto="C"):
    shared, per = prepare(inputs)
    mA = [dict(prm=per[c]["prm"], cbf=shared["cbf"], xT=per[c]["xT"], rope=per[c]["rope"],
               w_in0=shared["w_in0"][0:6], w_ukvk=shared["w_ukvk"], w_ukvv=shared["w_ukvv"]) for c in range(8)]
    rA = _launch(_get("A"), mA)
    kvg = [np.concatenate([np.asarray(rA[(c // 4) * 4 + r]["kvb"]) for r in range(4)], 0) for c in range(8)]
    mB = [dict(prm=per[c]["prm"], cbf=shared["cbf"], xT=per[c]["xT"], rope=per[c]["rope"], kvg=kvg[c],
               w_in0=shared["w_in0"], w_ukvk=shared["w_ukvk"], w_ukvv=shared["w_ukvv"], w_uq=shared["w_uq"],
               w_out0=shared["w_out0"]) for c in range(8)]
    rB = _launch(_get("B", False), mB)
    if upto == "B0":
        return _assemble(rB, "x1d")
    x1 = [np.asarray(rB[c]["x1d"]) for c in range(8)]
    mP = [dict(prm=per[c]["prm"], cbf=shared["cbf"], x1d=x1[c], w_in1=shared["w_in1"]) for c in range(8)]
    rP = _launch(_get("P"), mP)
    stg = [np.concatenate([np.asarray(rP[(c // 4) * 4 + r]["stb"]) for r in range(4)], 0) for c in range(8)]
    mC = [dict(prm=per[c]["prm"], cbf=shared["cbf"], x1d=x1[c], stg=stg[c],
               w_in1=shared["w_in1"], w_out1=shared["w_out1"]) for c in range(8)]
    rC = _launch(_get("C"), mC)
    return _assemble(rC, "outT")


def run_fused(inputs):
    shared, per = prepare(inputs)
    maps = [dict(prm=per[c]["prm"], cbf=shared["cbf"], xT=per[c]["xT"], rope=per[c]["rope"], **{
        k: shared[k] for k in ("w_in0", "w_ukvk", "w_ukvv", "w_uq", "w_out0", "w_in1", "w_out1")}) for c in range(8)]
    return _assemble(_launch(_get("F"), maps), "outT")


FUSED = True


def kernel(**inputs):
    if FUSED:
        return run_fused(inputs)
    return run_unfused(inputs)
```

```python
import numpy as np
import ml_dtypes
from contextlib import ExitStack
import concourse.bass as bass
import concourse.mybir as mybir
from concourse.bass_utils import run_bass_kernel_spmd

F32 = mybir.dt.float32
BF16 = mybir.dt.bfloat16
AF = mybir.ActivationFunctionType
ALU = mybir.AluOpType
ENGS = ("pe", "act", "dve", "pool", "sp")
T = 1024
HALO = 32
EPS = 1e-6
NEG = -30000.0


class Op:
    __slots__ = ("idx", "eng", "fn", "deps", "dma", "need_inc", "sem", "val", "extra_waits", "cc")

    def __init__(self, idx, eng, fn, deps, dma):
        self.idx = idx
        self.eng = eng
        self.fn = fn
        self.deps = deps
        self.dma = dma
        self.need_inc = False
        self.sem = None
        self.val = 0
        self.extra_waits = []
        self.cc = False


class Prog:
    def __init__(self, nc, n_dma_sems=8):
        self.nc = nc
        self.ops = []
        self.last_w = {}
        self.readers = {}
        self.barrier_deps = []
        self.last_on_eng = {}
        self.dmas_since_barrier = []
        self.n_dma_sems = n_dma_sems

    muted = False

    def op(self, eng, fn, reads=(), writes=(), dma=False):
        if self.muted:
            return -1
        idx = len(self.ops)
        deps = set(self.barrier_deps)
        for k in reads:
            w = self.last_w.get(k)
            if w is not None:
                deps.add(w)
        for k in writes:
            w = self.last_w.get(k)
            if w is not None:
                deps.add(w)
            for r in self.readers.get(k, ()):
                deps.add(r)
        o = Op(idx, eng, fn, deps, dma)
        self.ops.append(o)
        for k in writes:
            self.last_w[k] = idx
            self.readers[k] = []
        for k in reads:
            if k not in writes:
                self.readers.setdefault(k, []).append(idx)
        self.last_on_eng[eng] = idx
        if dma:
            self.dmas_since_barrier.append(idx)
        return idx

    def barrier(self):
        deps = set(self.last_on_eng.values()) | set(self.dmas_since_barrier)
        deps = {d for d in deps if not self.ops[d].cc}
        self.barrier_deps = sorted(deps)
        self.dmas_since_barrier = []
        keep = {k: w for k, w in self.last_w.items() if self.ops[w].cc}
        self.last_w = keep
        self.readers = {k: [] for k in keep}

    def dma(self, q, out, in_, reads=(), writes=(), **kw):
        return self.op(q, lambda e: e.dma_start(out=out, in_=in_, **kw), reads, writes, dma=True)

    def collective(self, fn, reads=(), writes=()):
        i = self.op("pool", fn, reads, writes, dma=True)
        self.ops[i].cc = True
        return i

    def emit(self, stack):
        nc = self.nc
        ops = self.ops
        for o in ops:
            for d in o.deps:
                od = ops[d]
                if od.dma:
                    od.need_inc = True
                elif od.eng == "pe" and o.eng == "pe" and not o.dma:
                    continue
                else:
                    od.need_inc = True
        esem = {e: stack.enter_context(nc.semaphore("s_" + e)) for e in ENGS}
        dsem = {}
        for q in ("sp", "act", "pool"):
            dsem[q] = [stack.enter_context(nc.semaphore("d_%s%d" % (q, i))) for i in range(self.n_dma_sems)]
        ecount = {e: 0 for e in ENGS}
        dcount = {q: [0] * self.n_dma_sems for q in dsem}
        drr = {q: 0 for q in dsem}
        for o in ops:
            if o.cc:
                o.sem = stack.enter_context(nc.semaphore("cc%d" % o.idx))
                o.val = 1
            elif o.dma:
                q = o.eng
                i = drr[q]
                drr[q] = (i + 1) % self.n_dma_sems
                prev = dcount[q][i]
                if prev > 0:
                    o.extra_waits.append((dsem[q][i], prev))
                dcount[q][i] = prev + 16
                o.sem = dsem[q][i]
                o.val = prev + 16
            elif o.need_inc:
                ecount[o.eng] += 1
                o.sem = esem[o.eng]
                o.val = ecount[o.eng]
        engobj = {"pe": "tensor", "act": "scalar", "dve": "vector", "pool": "gpsimd", "sp": "sync"}
        block = stack.enter_context(nc.Block())

        def make(engname):
            def body(eng):
                known = {}
                for o in ops:
                    if o.eng != engname:
                        continue
                    waits = list(o.extra_waits)
                    for d in sorted(o.deps):
                        od = ops[d]
                        if od.sem is None:
                            continue
                        if (not od.dma) and od.eng == "pe" and engname == "pe" and not o.dma:
                            continue
                        waits.append((od.sem, od.val))
                    best = {}
                    for s, v in waits:
                        key = id(s)
                        if key not in best or best[key][1] < v:
                            best[key] = (s, v)
                    for key, (s, v) in best.items():
                        if known.get(key, 0) >= v:
                            continue
                        eng.wait_ge(s, v)
                        known[key] = v
                    ins = o.fn(eng)
                    if o.sem is not None and ins is not None:
                        ins.then_inc(o.sem, 1 if (o.cc or not o.dma) else 16)
            return body

        for engname in ENGS:
            getattr(block, engobj[engname])(make(engname))


PRM = {}
_o = 0
for _n, _w in (("ev_g", 16), ("conv_b", 8), ("ln_g", 8), ("ln_b", 8), ("q_g", 4), ("kv_g", 4),
               ("od_g", 16), ("fin_g", 16), ("hg", 1), ("l0", 16), ("l1", 16), ("conv_w", 8 * 31),
               ("sel", 4), ("vis", 3)):
    PRM[_n] = (_o, _w)
    _o += _w
NPRM = _o

N_IN0 = 42
KVR = 1024 + 64 + 1024
CH = 64
NCH = T // CH
SW = 136


def build(mode, l1=True):
    import os
    CUT = os.environ.get("MK_CUT", "")
    nc = bass.Bass("TRN2", target_bir_lowering=False)
    L0 = mode in ("A", "B", "F")
    P1 = (mode in ("B", "F") and l1) or mode == "P"
    P2 = mode in ("C", "F")
    NH = int(os.environ.get("MK_HEADS", "16"))

    def decl(name, shape, dt, role):
        if role == "in":
            return nc.dram_tensor(name, shape, dt, kind="ExternalInput").ap()
        if role == "out":
            return nc.dram_tensor(name, shape, dt, kind="ExternalOutput").ap()
        return nc.dram_tensor(name, shape, dt).ap()

    prm_d = decl("prm", [128, NPRM], F32, "in")
    cbf_d = decl("cbf", [128, 3, 128], BF16, "in")
    if L0:
        xT = decl("xT", [128, 16, HALO + T], F32, "in")
        rope_d = decl("rope", [64, 2, T], F32, "in")
        w_in0 = decl("w_in0", [6 if mode == "A" else N_IN0, 128, 2048], F32, "in")
        w_ukvk = decl("w_ukvk", [8, 128, 4 * 128], F32, "in")
        w_ukvv = decl("w_ukvv", [2, 128, 4 * 512], F32, "in")
        if mode != "F":
            kvb_a = decl("kvb", [KVR, T], BF16, "out" if mode == "A" else "int")
    if mode in ("B", "F"):
        w_uq = decl("w_uq", [8, 128, 4 * 256], F32, "in")
        w_out0 = decl("w_out0", [16, 128, 2048], F32, "in")
        if mode == "B":
            kvg_a = decl("kvg", [4 * KVR, T], BF16, "in")
    if mode != "A":
        x1d = decl("x1d", [128, 16, T], F32, {"B": "out", "C": "in", "F": "int", "P": "in"}[mode])
    if P1 or P2:
        w_in1 = decl("w_in1", [64, 128, 2048], F32, "in")
    if P1:
        if mode != "F":
            stb_a = decl("stb", [2048, SW], F32, "out")
    if P2:
        if mode == "C":
            stg_a = decl("stg", [4 * 2048, SW], F32, "in")
        w_out1 = decl("w_out1", [16, 128, 2048], F32, "in")
        x2d = decl("x2d", [128, 16, T], F32, "int")
        outT = decl("outT", [128, 16, T], F32, "out")

    RG = [[0, 1, 2, 3], [4, 5, 6, 7]]
    if mode == "F":
        kb = [decl("kb%d" % i, [256, T], BF16, "int") for i in range(4)]
        kg = [decl("kg%d" % i, [1024, T], BF16, "int") for i in range(4)]
        kpb = decl("kpb", [64, T], BF16, "int")
        kpgd = decl("kpgd", [256, T], BF16, "int")
        vb = [decl("vb%d" % i, [256, T], BF16, "int") for i in range(4)]
        vgd = [decl("vgd%d" % i, [1024, T], BF16, "int") for i in range(4)]
        sbp = [decl("sbp%d" % i, [512, SW], F32, "int") for i in range(4)]
        sgp = [decl("sgp%d" % i, [2048, SW], F32, "int") for i in range(4)]

    st = ExitStack()
    with st:
        def AG(in_ap, out_ap, rkey, wkey):
            p.collective(lambda e: e.collective_compute("AllGather", ALU.bypass, replica_groups=RG,
                                                        ins=[in_ap.opt()], outs=[out_ap.opt()]),
                         reads=[rkey], writes=[wkey])

        nsfx = [""]

        def sb(name, shape, dt=F32, stack=None):
            return (stack or st).enter_context(nc.sbuf_tensor("sb_" + name + nsfx[0], shape, dt))

        p = Prog(nc)
        ps = [st.enter_context(nc.psum_tensor("ps%d" % i, [128, 512], F32)) for i in range(7)]
        psb = st.enter_context(nc.psum_tensor("psb", [128, 8, 128], BF16))
        PK = [("ps", i) for i in range(7)]
        psv = ps[6][:, :].bitcast(BF16)

        def MM(out, lhsT, rhs, start, stop, reads, writes):
            p.op("pe", lambda e: e.matmul(out, lhsT=lhsT, rhs=rhs, start=start, stop=stop), reads, writes)

        def TR(out, in_, reads, writes):
            p.op("pe", lambda e: e.transpose(out, in_, ident), reads + ["cbf"], writes)

        def ACT(out, in_, func, reads, writes, scale=1.0, bias=None):
            if bias is None:
                p.op("act", lambda e: e.activation(out=out, in_=in_, func=func, scale=scale), reads, writes)
            else:
                p.op("act", lambda e: e.activation(out=out, in_=in_, func=func, scale=scale, bias=bias), reads, writes)

        def TT(eng, out, in0, in1, op, reads, writes):
            p.op(eng, lambda e: e.tensor_tensor(out=out, in0=in0, in1=in1, op=op), reads, writes)

        def TS(eng, out, in0, s1, s2, op0, op1, reads, writes):
            if s2 is None:
                p.op(eng, lambda e: e.tensor_scalar(out=out, in0=in0, scalar1=s1, scalar2=None, op0=op0), reads, writes)
            else:
                p.op(eng, lambda e: e.tensor_scalar(out=out, in0=in0, scalar1=s1, scalar2=s2, op0=op0, op1=op1), reads, writes)

        def STT(out, in0, scalar, in1, op0, op1, reads, writes):
            p.op("dve", lambda e: e.scalar_tensor_tensor(out=out, in0=in0, scalar=scalar, in1=in1, op0=op0, op1=op1),
                 reads, writes)

        def CP(eng, out, in_, reads, writes):
            if eng == "act":
                p.op("act", lambda e: e.activation(out=out, in_=in_, func=AF.Identity), reads, writes)
            else:
                p.op(eng, lambda e: e.tensor_copy(out=out, in_=in_), reads, writes)

        def RECIP(out, in_, reads, writes):
            p.op("dve", lambda e: e.reciprocal(out=out, in_=in_), reads, writes)

        def MEMSET(eng, ap, val, writes):
            p.op(eng, lambda e: e.memset(ap, val), (), writes)

        prm = sb("prm", [128, NPRM])
        cbf = sb("cbf", [128, 3, 128], BF16)
        epsc = sb("epsc", [128, 1])
        lbc = sb("lbc", [128, 16])
        omlb = sb("omlb", [128, 16])
        p.dma("sp", prm[:], prm_d, writes=["prm"])
        p.dma("sp", cbf[:], cbf_d, writes=["cbf"])
        MEMSET("dve", epsc[:], EPS, ["epsc"])
        ident = cbf[:, 0, :]
        ones = cbf[:, 1, :]
        triu = cbf[:, 2, :]

        def pc(name, j=0, n=1):
            o, w = PRM[name]
            return prm[:, o + j:o + j + n]

        TT("dve", lbc[:], pc("l1", 0, 16), pc("l0", 0, 16), ALU.subtract, ["prm"], ["lbc"])
        ACT(lbc[:], lbc[:], AF.Sigmoid, ["lbc"], ["lbc"])
        TS("dve", omlb[:], lbc[:], -1.0, 1.0, ALU.mult, ALU.add, ["lbc"], ["omlb"])

        wst = [sb("wst%d" % i, [128, 2048]) for i in range(2)]
        wbf = [sb("wbf%d" % i, [128, 2048], BF16) for i in range(2)]
        wctr = [0]

        def wload(src, n=2048):
            i = wctr[0] % 2
            wctr[0] += 1
            p.dma("sp", wst[i][:, 0:n], src, writes=[("wst", i)])
            CP("pool", wbf[i][:, 0:n], wst[i][:, 0:n], [("wst", i)], [("wbf", i)])
            return wbf[i], ("wbf", i)

        mix = sb("mix", [128, 16, T], BF16)
        rstd = sb("rstd", [128, T])
        sqb = sb("sqb", [128, 512], BF16)
        xs = [sb("xs%d" % i, [128, T]) for i in range(2)]

        def rms_stream(src_ap, gname, ncol, col0, sink):
            nh = [(a, min(512, ncol - a)) for a in range(0, ncol, 512)]
            for (a, n) in nh:
                for c in range(16):
                    xb = xs[c % 2]
                    p.dma("sp", xb[:, 0:n], src_ap[:, c, col0 + a:col0 + a + n], writes=[("xs", c % 2)])
                    ACT(sqb[:, 0:n], xb[:, 0:n], AF.Square, [("xs", c % 2)], ["sqb"])
                    MM(ps[0][:, 0:n], ones, sqb[:, 0:n], c == 0, c == 15, ["sqb", "cbf"], [PK[0]])
                ACT(rstd[:, a:a + n], ps[0][:, 0:n], AF.Sqrt, [PK[0], "epsc"], ["rstd"], scale=1.0 / 2048, bias=epsc[:])
                RECIP(rstd[:, a:a + n], rstd[:, a:a + n], ["rstd"], ["rstd"])
                for c in range(16):
                    xb = xs[c % 2]
                    p.dma("sp", xb[:, 0:n], src_ap[:, c, col0 + a:col0 + a + n], writes=[("xs", c % 2)])
                    sink(c, a, n, xb, ("xs", c % 2), gname)

        def lin_tile(wb, wkey, kc, m, rhs_fn, rkeys, a, n, pst, pkey):
            for c in range(kc):
                MM(pst[0:m, 0:n], wb[:, c * 128:c * 128 + m], rhs_fn(c, a, n), c == 0, c == kc - 1,
                   [wkey] + [(rk, c) for rk in rkeys], [pkey])

        def out_proj(w_d, src_ap, col0, dst_ap):
            for m in range(16):
                wb, wk = wload(w_d[m])
                xb = xs[m % 2]
                p.dma("sp", xb[:, :], src_ap[:, m, col0:col0 + T], writes=[("xs", m % 2)])
                for half in range(2):
                    pst, pk = ps[1 + half], PK[1 + half]
                    for c in range(16):
                        MM(pst[:, :], wb[:, c * 128:(c + 1) * 128], mix[:, c, half * 512:(half + 1) * 512],
                           c == 0, c == 15, [wk] + [("mix", c)], [pk])
                    TT("dve", xb[:, half * 512:(half + 1) * 512], xb[:, half * 512:(half + 1) * 512], pst[:, :],
                       ALU.add, [("xs", m % 2), pk], [("xs", m % 2)])
                p.dma("sp", dst_ap[:, m, :], xb[:, :], reads=[("xs", m % 2)], writes=["x_out"])

        def layer0():
            with ExitStack() as s0:
                rope = sb("rope", [64, 2, T], F32, s0)
                p.dma("sp", rope[:], rope_d, writes=["rope"])
                kn = sb("kn", [128, 8, T], BF16, s0)
                vt = sb("vt", [128, 8, T], BF16, s0)
                kpe = sb("kpe", [64, T], BF16, s0)
                cq = sb("cq", [128, 4, T], F32, s0)
                with ExitStack() as s1:
                    hT = sb("hT", [128, 16, T], BF16, s1)
                    hh = sb("hh", [128, 16, HALO], BF16, s1)
                    rsth = sb("rsth", [128, HALO], F32, s1)

                    def sink_h(c, a, n, xb, xk, gname):
                        STT(hT[:, c, a:a + n], xb[:, 0:n], pc(gname, c), rstd[:, a:a + n], ALU.mult, ALU.mult,
                            [xk, "rstd", "prm"], [("hT", c)])
                    rms_stream(xT, "ev_g", T, HALO, sink_h)
                    if mode != "A":
                        for c in range(16):
                            xb = xs[c % 2]
                            p.dma("sp", xb[:, 0:HALO], xT[:, c, 0:HALO], writes=[("xs", c % 2)])
                            ACT(sqb[:, 0:HALO], xb[:, 0:HALO], AF.Square, [("xs", c % 2)], ["sqb"])
                            MM(ps[0][:, 0:HALO], ones, sqb[:, 0:HALO], c == 0, c == 15, ["sqb", "cbf"], [PK[0]])
                        ACT(rsth[:], ps[0][:, 0:HALO], AF.Sqrt, [PK[0], "epsc"], ["rsth"], scale=1.0 / 2048,
                            bias=epsc[:])
                        RECIP(rsth[:], rsth[:], ["rsth"], ["rsth"])
                        for c in range(16):
                            xb = xs[c % 2]
                            p.dma("sp", xb[:, 0:HALO], xT[:, c, 0:HALO], writes=[("xs", c % 2)])
                            STT(hh[:, c, :], xb[:, 0:HALO], pc("ev_g", c), rsth[:], ALU.mult, ALU.mult,
                                [("xs", c % 2), "rsth", "prm"], ["hh"])

                    def h_rhs(c, a, n):
                        return hT[:, c, a:a + n]

                    def in0_tile(ti, m, evac):
                        wb, wk = wload(w_in0[ti])
                        for half in range(2):
                            pst, pk = ps[1 + half], PK[1 + half]
                            lin_tile(wb, wk, 16, m, h_rhs, ["hT"], half * 512, 512, pst, pk)
                            evac(half, pst, pk)
                        return wb, wk

                    with ExitStack() as s2:
                        ckv = sb("ckv", [128, 4, T], F32, s2)
                        ckn = sb("ckn", [128, 4, T], BF16, s2)
                        kp32 = sb("kp32", [64, 2, T], F32, s2)
                        for j in range(4):
                            in0_tile(j, 128, lambda half, pst, pk, j=j: CP(
                                "act", ckv[:, j, half * 512:(half + 1) * 512], pst[:, :], [pk], ["ckv"]))
                        for j in range(2):
                            in0_tile(4 + j, 64, lambda half, pst, pk, j=j: CP(
                                "act", kp32[:, j, half * 512:(half + 1) * 512], pst[0:64, :], [pk], ["kp32"]))
                        for half in range(2):
                            sl = slice(half * 512, (half + 1) * 512)
                            for j in range(4):
                                ACT(sqb[:, :], ckv[:, j, sl], AF.Square, ["ckv"], ["sqb"])
                                MM(ps[0][:, :], ones, sqb[:, :], j == 0, j == 3, ["sqb", "cbf"], [PK[0]])
                            ACT(rstd[:, sl], ps[0][:, :], AF.Sqrt, [PK[0], "epsc"], ["rstd"], scale=1.0 / 512,
                                bias=epsc[:])
                            RECIP(rstd[:, sl], rstd[:, sl], ["rstd"], ["rstd"])
                            for j in range(4):
                                STT(ckn[:, j, sl], ckv[:, j, sl], pc("kv_g", j), rstd[:, sl], ALU.mult, ALU.mult,
                                    ["ckv", "rstd", "prm"], ["ckn"])
                        TT("dve", kp32[:, 0, :], kp32[:, 0, :], rope[:, 0, :], ALU.mult, ["kp32", "rope"], ["kp32"])
                        TT("dve", kp32[:, 1, :], kp32[:, 1, :], rope[:, 1, :], ALU.mult, ["kp32", "rope"], ["kp32"])
                        TT("dve", kpe[:, :], kp32[:, 0, :], kp32[:, 1, :], ALU.add, ["kp32"], ["kpe"])
                        if mode == "F":
                            p.dma("sp", kpb, kpe[:, :], reads=["kpe"], writes=["kpb"])
                            AG(kpb, kpgd, "kpb", "kpgd")
                        else:
                            p.dma("sp", kvb_a[1024:1088, :], kpe[:, :], reads=["kpe"], writes=["kvb"])
                        for h in range(8):
                            wb, wk = wload(w_ukvk[h], 512)
                            for half in range(2):
                                pst, pk = ps[1 + half], PK[1 + half]
                                for c in range(4):
                                    MM(pst[:, :], wb[:, c * 128:(c + 1) * 128], ckn[:, c, half * 512:(half + 1) * 512],
                                       c == 0, c == 3, [wk, "ckn"], [pk])
                                CP("act", kn[:, h, half * 512:(half + 1) * 512], pst[:, :], [pk], [("kn", h)])
                            if mode == "F":
                                p.dma("sp", kb[h // 2][(h % 2) * 128:(h % 2 + 1) * 128, :], kn[:, h, :],
                                      reads=[("kn", h)], writes=[("kb", h // 2)])
                                if h % 2 == 1:
                                    AG(kb[h // 2], kg[h // 2], ("kb", h // 2), ("kg", h // 2))
                            else:
                                p.dma("sp", kvb_a[h * 128:(h + 1) * 128, :], kn[:, h, :], reads=[("kn", h)],
                                      writes=["kvb"])
                        for vh in range(2):
                            wb, wk = wload(w_ukvv[vh])
                            for tt in range(8):
                                pst, pk = ps[1 + tt % 2], PK[1 + tt % 2]
                                for c in range(4):
                                    MM(pst[:, :], ckn[:, c, tt * 128:(tt + 1) * 128], wb[:, c * 512:(c + 1) * 512],
                                       c == 0, c == 3, [wk, "ckn"], [pk])
                                CP("act", vt[:, tt, vh * 512:(vh + 1) * 512], pst[:, :], [pk], ["vt"])
                        if mode == "F":
                            for i in range(4):
                                p.dma("sp", vb[i].rearrange("(t p) n -> p t n", p=128), vt[:, 2 * i:2 * i + 2, :],
                                      reads=["vt"], writes=[("vb", i)])
                                AG(vb[i], vgd[i], ("vb", i), ("vgd", i))
                        else:
                            p.dma("sp", kvb_a[1088:1088 + 1024, :].rearrange("(t p) n -> p t n", p=128), vt[:, :, :],
                                  reads=["vt"], writes=["kvb"])
                    p.barrier()
                    if mode == "A":
                        return
                    with ExitStack() as s2:
                        u = sb("u", [128, HALO + T], BF16, s2)
                        zs = sb("zs", [128, T], BF16, s2)
                        t32 = sb("t32", [128, T], F32, s2)
                        dg = sb("dg", [128, 31, 128], BF16, s2)
                        y32 = sb("y32", [128, 512], F32, s2)
                        ybf = sb("ybf", [128, 512], BF16, s2)
                        d32 = sb("d32", [128, 512], F32, s2)
                        r32 = sb("r32", [128, 512], F32, s2)
                        ones128 = sb("ones128", [128, 128], BF16, s2)
                        MEMSET("dve", ones128[:], 1.0 / 128, ["ones128"])
                        for j in range(8):
                            wb, wk = wload(w_in0[6 + 3 * j])
                            for half in range(2):
                                pst, pk = ps[1 + half], PK[1 + half]
                                lin_tile(wb, wk, 16, 128, h_rhs, ["hT"], half * 512, 512, pst, pk)
                                CP("act", t32[:, half * 512:(half + 1) * 512], pst[:, :], [pk], ["t32"])
                            for c in range(16):
                                MM(ps[3][:, 0:HALO], wb[:, c * 128:(c + 1) * 128], hh[:, c, :], c == 0, c == 15,
                                   [wk, "hh"], [PK[3]])
                            wb2, wk2 = wload(w_in0[7 + 3 * j])
                            for c in range(16):
                                MM(ps[4][:, 0:HALO], wb2[:, c * 128:(c + 1) * 128], hh[:, c, :], c == 0, c == 15,
                                   [wk2, "hh"], [PK[4]])
                            ACT(d32[:, 0:HALO], ps[4][:, 0:HALO], AF.Sigmoid, [PK[4]], ["d32"])
                            TT("dve", u[:, 0:HALO], d32[:, 0:HALO], ps[3][:, 0:HALO], ALU.mult, ["d32", PK[3]], ["u"])
                            for half in range(2):
                                pst, pk = ps[1 + half], PK[1 + half]
                                lin_tile(wb2, wk2, 16, 128, h_rhs, ["hT"], half * 512, 512, pst, pk)
                                ACT(d32[:, :], pst[:, :], AF.Sigmoid, [pk], ["d32"])
                                TT("dve", u[:, HALO + half * 512:HALO + (half + 1) * 512], d32[:, :],
                                   t32[:, half * 512:(half + 1) * 512], ALU.mult, ["d32", "t32"], ["u"])
                            wb3, wk3 = wload(w_in0[8 + 3 * j])
                            for half in range(2):
                                pst, pk = ps[1 + half], PK[1 + half]
                                lin_tile(wb3, wk3, 16, 128, h_rhs, ["hT"], half * 512, 512, pst, pk)
                                ACT(zs[:, half * 512:(half + 1) * 512], pst[:, :], AF.Silu, [pk], ["zs"])
                            o_w = PRM["conv_w"][0] + j * 31
                            for k in range(31):
                                TS("pool", dg[:, k, :], ident, prm[:, o_w + k:o_w + k + 1], None, ALU.mult, None,
                                   ["cbf", "prm"], [("dg", k)])
                            for half in range(2):
                                for k in range(31):
                                    a = HALO + half * 512 - 30 + k
                                    MM(ps[5][:, :], dg[:, k, :], u[:, a:a + 512], k == 0, k == 30, [("dg", k), "u"],
                                       [PK[5]])
                                ACT(y32[:], ps[5][:, :], AF.Identity, [PK[5], "prm"], ["y32"], bias=pc("conv_b", j))
                                CP("dve", ybf[:], y32[:], ["y32"], ["ybf"])
                                MM(ps[6][:, :], ones128[:], ybf[:], True, True, ["ones128", "ybf"], [PK[6]])
                                TT("dve", d32[:], y32[:], ps[6][:, :], ALU.subtract, ["y32", PK[6]], ["d32"])
                                ACT(ybf[:], d32[:], AF.Square, ["d32"], ["ybf"])
                                MM(ps[6][:, :], ones128[:], ybf[:], True, True, ["ones128", "ybf"], [PK[6]])
                                ACT(r32[:], ps[6][:, :], AF.Sqrt, [PK[6], "epsc"], ["r32"], bias=epsc[:])
                                RECIP(r32[:], r32[:], ["r32"], ["r32"])
                                TT("dve", d32[:], d32[:], r32[:], ALU.mult, ["d32", "r32"], ["d32"])
                                ACT(d32[:], d32[:], AF.Silu, ["d32", "prm"], ["d32"], scale=pc("ln_g", j),
                                    bias=pc("ln_b", j))
                                TT("dve", mix[:, j, half * 512:(half + 1) * 512], d32[:],
                                   zs[:, half * 512:(half + 1) * 512], ALU.mult, ["d32", "zs"], [("mix", j)])
                    p.barrier()
                    if CUT == "B":
                        p.muted = True
                    for j in range(8):
                        in0_tile(30 + j, 128, lambda half, pst, pk, j=j: ACT(
                            mix[:, 8 + j, half * 512:(half + 1) * 512], pst[:, :], AF.Silu, [pk], [("mix", 8 + j)]))
                    for j in range(4):
                        in0_tile(38 + j, 128, lambda half, pst, pk, j=j: CP(
                            "act", cq[:, j, half * 512:(half + 1) * 512], pst[:, :], [pk], ["cq"]))
                p.barrier()
                if CUT == "B2":
                    p.muted = True
                qn = sb("qn", [128, 8, T], BF16, s0)
                qpe = sb("qpe", [64, 8, T], BF16, s0)
                with ExitStack() as s2:
                    cqn = sb("cqn", [128, 4, T], BF16, s2)
                    qa = sb("qa", [64, 512], F32, s2)
                    qb = sb("qb", [64, 512], F32, s2)
                    for half in range(2):
                        sl = slice(half * 512, (half + 1) * 512)
                        for j in range(4):
                            ACT(sqb[:, :], cq[:, j, sl], AF.Square, ["cq"], ["sqb"])
                            MM(ps[0][:, :], ones, sqb[:, :], j == 0, j == 3, ["sqb", "cbf"], [PK[0]])
                        ACT(rstd[:, sl], ps[0][:, :], AF.Sqrt, [PK[0], "epsc"], ["rstd"], scale=1.0 / 512,
                            bias=epsc[:])
                        RECIP(rstd[:, sl], rstd[:, sl], ["rstd"], ["rstd"])
                        for j in range(4):
                            STT(cqn[:, j, sl], cq[:, j, sl], pc("q_g", j), rstd[:, sl], ALU.mult, ALU.mult,
                                ["cq", "rstd", "prm"], ["cqn"])
                    scale = 192.0 ** -0.5
                    for h in range(8):
                        wb, wk = wload(w_uq[h], 1024)
                        for half in range(2):
                            sl = slice(half * 512, (half + 1) * 512)
                            for c in range(4):
                                MM(ps[1][:, :], wb[:, c * 256:c * 256 + 128], cqn[:, c, sl], c == 0, c == 3,
                                   [wk, "cqn"], [PK[1]])
                            ACT(qn[:, h, sl], ps[1][:, :], AF.Identity, [PK[1]], [("qn", h)], scale=scale)
                            for c in range(4):
                                MM(ps[2][0:64, :], wb[:, c * 256 + 128:c * 256 + 192], cqn[:, c, sl], c == 0, c == 3,
                                   [wk, "cqn"], [PK[2]])
                            for c in range(4):
                                MM(ps[3][0:64, :], wb[:, c * 256 + 192:c * 256 + 256], cqn[:, c, sl], c == 0, c == 3,
                                   [wk, "cqn"], [PK[3]])
                            TT("dve", qa[:], ps[2][0:64, :], rope[:, 0, sl], ALU.mult, [PK[2], "rope"], ["qa"])
                            TT("dve", qb[:], ps[3][0:64, :], rope[:, 1, sl], ALU.mult, [PK[3], "rope"], ["qb"])
                            TT("dve", qa[:], qa[:], qb[:], ALU.add, ["qa", "qb"], ["qa"])
                            ACT(qpe[:, h, sl], qa[:], AF.Identity, ["qa"], [("qpe", h)], scale=scale)
                p.barrier()
                if CUT == "C":
                    p.muted = True
                with ExitStack() as s2:
                    kng = [sb("kng%d" % i, [128, 3 * T], BF16, s2) for i in range(2)]
                    vg = [sb("vg%d" % i, [128, 24, 128], BF16, s2) for i in range(2)]
                    kpg = sb("kpg", [64, 3 * T], BF16, s2)
                    onesv = sb("onesv", [128, 3, 128], BF16, s2)
                    pT = [sb("pT%d" % i, [128, 512], BF16, s2) for i in range(3)]
                    rl = sb("rl", [128, 512], F32, s2)
                    o32 = sb("o32", [128, 512], F32, s2)
                    for r in range(3):
                        if mode == "F":
                            p.dma("sp", kpg[:, r * T:(r + 1) * T], kpgd[r * 64:(r + 1) * 64, :],
                                  reads=["kpgd"], writes=["kpg"])
                        else:
                            p.dma("sp", kpg[:, r * T:(r + 1) * T], kvg_a[r * KVR + 1024:r * KVR + 1088, :],
                                  reads=["kvg"], writes=["kpg"])
                        TS("dve", onesv[:, r, :], ones, pc("vis", r), None, ALU.mult, None, ["cbf", "prm"], ["onesv"])
                    step = [0]
                    for h in range(8):
                        gi = h % 2
                        for r in range(3):
                            if mode == "F":
                                o_ = r * 256 + (h % 2) * 128
                                p.dma("sp", kng[gi][:, r * T:(r + 1) * T], kg[h // 2][o_:o_ + 128, :],
                                      reads=[("kg", h // 2)], writes=[("kng", gi)])
                                for i in range(4):
                                    p.dma("sp", vg[gi][:, r * 8 + 2 * i:r * 8 + 2 * i + 2, :],
                                          vgd[i][r * 256:(r + 1) * 256, h * 128:(h + 1) * 128].rearrange(
                                              "(t p) n -> p t n", p=128),
                                          reads=[("vgd", i)], writes=[("vg", gi)])
                            else:
                                p.dma("sp", kng[gi][:, r * T:(r + 1) * T],
                                      kvg_a[r * KVR + h * 128:r * KVR + (h + 1) * 128, :],
                                      reads=["kvg"], writes=[("kng", gi)])
                                p.dma("sp", vg[gi][:, r * 8:(r + 1) * 8, :],
                                      kvg_a[r * KVR + 1088:r * KVR + 1088 + 1024, h * 128:(h + 1) * 128].rearrange(
                                          "(t p) n -> p t n", p=128),
                                      reads=["kvg"], writes=[("vg", gi)])
                            TS("pool", vg[gi][:, r * 8:(r + 1) * 8, :], vg[gi][:, r * 8:(r + 1) * 8, :], pc("vis", r),
                               None, ALU.mult, None, [("vg", gi), "prm"], [("vg", gi)])
                        for qh in range(2):
                            q0 = qh * 512
                            tiles = []
                            nown = 4 if qh == 0 else 8
                            for kt in range(nown):
                                lo = max(0, kt * 128 - q0)
                                diag = (kt * 128 >= q0)
                                tiles.append(("own", kt, lo, diag))
                            for g in range(24):
                                tiles.append(("g", g, 0, False))
                            info = {}

                            def qk(ti):
                                kind, kt, lo, diag = tiles[ti]
                                n = 512 - lo
                                si = step[0] % 3
                                step[0] += 1
                                pss, pks = ps[2 + si], PK[2 + si]
                                if kind == "own":
                                    kT = kn[:, h, kt * 128:(kt + 1) * 128]
                                    kP = kpe[:, kt * 128:(kt + 1) * 128]
                                    vv = vt[:, kt, h * 128:(h + 1) * 128]
                                    ov = ones
                                    rk = [("kn", h), "kpe", "vt"]
                                else:
                                    kT = kng[gi][:, kt * 128:(kt + 1) * 128]
                                    kP = kpg[:, kt * 128:(kt + 1) * 128]
                                    vv = vg[gi][:, kt, :]
                                    ov = onesv[:, kt // 8, :]
                                    rk = [("kng", gi), "kpg", ("vg", gi), "onesv"]
                                MM(pss[:, 0:n], kT, qn[:, h, q0 + lo:q0 + 512], True, False, rk + [("qn", h)], [pks])
                                MM(pss[:, 0:n], kP, qpe[:, h, q0 + lo:q0 + 512], False, True, rk + [("qpe", h)], [pks])
                                info[ti] = (si, n, lo, diag, pss, pks, vv, ov, rk)

                            def rest(ti):
                                si, n, lo, diag, pss, pks, vv, ov, rk = info[ti]
                                ACT(pT[si][:, 0:n], pss[:, 0:n], AF.Exp, [pks], [("pT", si)])
                                if diag:
                                    TT("dve", pT[si][:, 0:128], pT[si][:, 0:128], triu, ALU.mult,
                                       [("pT", si), "cbf"], [("pT", si)])
                                first = (ti == 0)
                                last = (ti == len(tiles) - 1)
                                MM(ps[0][:, lo:512], vv, pT[si][:, 0:n], first, last, rk + [("pT", si)], [PK[0]])
                                MM(ps[1][:, lo:512], ov, pT[si][:, 0:n], first, last, rk + ["cbf", ("pT", si)], [PK[1]])

                            qk(0)
                            for ti in range(len(tiles)):
                                if ti + 1 < len(tiles):
                                    qk(ti + 1)
                                rest(ti)
                            CP("act", rl[:], ps[1][:, :], [PK[1]], ["rl"])
                            RECIP(rl[:], rl[:], ["rl"], ["rl"])
                            TT("dve", o32[:], ps[0][:, :], rl[:], ALU.mult, [PK[0], "rl"], ["o32"])
                            TT("dve", mix[:, 8 + h, q0:q0 + 512], o32[:], mix[:, 8 + h, q0:q0 + 512], ALU.mult,
                               ["o32", ("mix", 8 + h)], [("mix", 8 + h)])
            p.barrier()
            if CUT == "D":
                p.muted = True
            out_proj(w_out0, xT, HALO, x1d)
            p.muted = False
            p.barrier()

        def layer1(pass2):
            nsfx[0] = "_p2" if pass2 else "_p1"
            with ExitStack() as s0:
                h1 = sb("h1", [128, 16, T], BF16, s0)
                rmask = sb("rmask", [128, T], F32, s0)
                MEMSET("dve", rmask[:], 1.0, ["rmask"])
                MEMSET("dve", rmask[:].rearrange("p (c t) -> p c t", t=CH)[:, :, 0:1], 0.0, ["rmask"])

                def sink_h(c, a, n, xb, xk, gname):
                    STT(h1[:, c, a:a + n], xb[:, 0:n], pc(gname, c), rstd[:, a:a + n], ALU.mult, ALU.mult,
                        [xk, "rstd", "prm"], [("h1", c)])
                rms_stream(x1d, "od_g", T, 0, sink_h)

                def h_rhs(c, a, n):
                    return h1[:, c, a:a + n]

                sin_ = None
                if pass2:
                    sin_ = sb("sin", [128, 16, 128], F32, s0)
                    with ExitStack() as s1:
                        G = sb("G", [128, 3, 16, SW], F32, s1)
                        t2 = sb("t2", [128, 128], F32, s1)
                        t3 = sb("t3", [128, 128], F32, s1)
                        for r in range(3):
                            if mode == "F":
                                for i in range(4):
                                    p.dma("sp", G[:, r, 4 * i:4 * i + 4, :],
                                          sgp[i][r * 512:(r + 1) * 512, :].rearrange("(h p) w -> p h w", p=128),
                                          reads=[("sgp", i)], writes=["G"])
                            else:
                                p.dma("sp", G[:, r, :, :],
                                      stg_a[r * 2048:(r + 1) * 2048, :].rearrange("(h p) w -> p h w", p=128),
                                      reads=["stg"], writes=["G"])
                        for h in range(16):
                            S0, S1, S2 = G[:, 0, h, 0:128], G[:, 1, h, 0:128], G[:, 2, h, 0:128]
                            D1, D2 = G[:, 1, h, 128:129], G[:, 2, h, 128:129]
                            STT(t2[:], S0, D1, S1, ALU.mult, ALU.add, ["G"], ["t2"])
                            STT(t3[:], t2[:], D2, S2, ALU.mult, ALU.add, ["G", "t2"], ["t3"])
                            TS("dve", sin_[:, h, :], S0, pc("sel", 1), None, ALU.mult, None, ["G", "prm"], ["sin"])
                            STT(sin_[:, h, :], t2[:], pc("sel", 2), sin_[:, h, :], ALU.mult, ALU.add,
                                ["t2", "prm", "sin"], ["sin"])
                            STT(sin_[:, h, :], t3[:], pc("sel", 3), sin_[:, h, :], ALU.mult, ALU.add,
                                ["t3", "prm", "sin"], ["sin"])
                    p.barrier()

                sg = sb("sg", [128, T], F32, s0)
                ff = sb("ff", [128, T], F32, s0)
                lg = sb("lg", [128, T], F32, s0)
                bb = sb("bb", [128, T], F32, s0)
                khT = sb("khT", [128, T], BF16, s0)
                vT = sb("vT", [128, T], BF16, s0)
                ebc = sb("ebc", [128, NCH], F32, s0)
                kt_all = sb("kt_all", [CH, NCH, 128], BF16, s0)
                vt_all = sb("vt_all", [CH, NCH, 128], BF16, s0)
                psv3 = psv.rearrange("p (j k) -> p j k", k=128)
                ps3b = ps[3][:, :].bitcast(BF16).rearrange("p (j k) -> p j k", k=128)
                ps5b = ps[5][:, :].bitcast(BF16).rearrange("p (j k) -> p j k", k=128)
                banks_k = [(psb, "psb"), (psv3, PK[6])]
                banks_v = [(ps3b, PK[3]), (ps5b, PK[5])]
                S = sb("S", [128, 128], F32, s0)
                sst = sb("sst", [128, SW], F32, s0)
                MEMSET("dve", sst[:, 128:SW], 0.0, ["sst"])
                if pass2:
                    q32 = sb("q32", [128, T], F32, s0)
                    o32 = sb("o32l", [128, T], F32, s0)
                    gs = sb("gs", [128, T], BF16, s0)
                    qtT = sb("qtT", [128, T], BF16, s0)
                    ktT = sb("ktT", [128, T], BF16, s0)
                    qhT = sb("qhT", [128, T], BF16, s0)
                    negr = sb("negr", [128, NCH], F32, s0)
                    Sbf = [sb("Sbf%d" % i, [128, 128], BF16, s0) for i in range(2)]
                    Am_all = sb("Am_all", [CH, NCH, CH], BF16, s0)
                    rs1 = sb("rs1", [128, 512], F32, s0)

                def tile_in(ti, evac):
                    wb, wk = wload(w_in1[ti])
                    for half in range(2):
                        pst, pk = ps[1 + half], PK[1 + half]
                        lin_tile(wb, wk, 16, 128, h_rhs, ["h1"], half * 512, 512, pst, pk)
                        evac(half, pst, pk)

                for h in range(NH):
                    tile_in(4 * h + 1, lambda half, pst, pk: ACT(sg[:, half * 512:(half + 1) * 512], pst[:, :],
                                                                AF.Sigmoid, [pk], ["sg"]))
                    TS("dve", ff[:], sg[:], omlb[:, h:h + 1], lbc[:, h:h + 1], ALU.mult, ALU.add,
                       ["sg", "omlb", "lbc"], ["ff"])
                    if CUT == "L1a":
                        p.muted = True
                    ACT(lg[:], ff[:], AF.Ln, ["ff"], ["lg"])
                    TS("dve", sg[:], ff[:], -1.0, 1.0, ALU.mult, ALU.add, ["ff"], ["sg"])
                    p.op("dve", lambda e: e.tensor_tensor_scan(out=bb[:], data0=rmask[:], data1=lg[:], initial=0.0,
                                                               op0=ALU.mult, op1=ALU.add),
                         ["rmask", "lg"], ["bb"])
                    bb3 = bb[:].rearrange("p (c t) -> p c t", t=CH)
                    if CUT == "L1b":
                        p.muted = True
                    ACT(ebc[:], bb3[:, :, CH - 1], AF.Exp, ["bb"], ["ebc"])
                    lg3 = lg[:].rearrange("p (c t) -> p c t", t=CH)
                    TT("dve", lg3[:, :, 0], lg3[:, :, 0], bb3[:, :, CH - 1], ALU.subtract, ["lg", "bb"], ["lg"])
                    p.op("dve", lambda e: e.tensor_tensor_scan(out=ff[:], data0=rmask[:], data1=lg[:], initial=0.0,
                                                               op0=ALU.mult, op1=ALU.add),
                         ["rmask", "lg"], ["ff"])
                    ACT(ff[:], ff[:], AF.Exp, ["ff"], ["ff"], scale=-1.0)
                    TT("dve", khT[:], sg[:], ff[:], ALU.mult, ["sg", "ff"], ["khT"])
                    tile_in(4 * h + 2, lambda half, pst, pk: CP("act", vT[:, half * 512:(half + 1) * 512], pst[:, :],
                                                               [pk], ["vT"]))
                    if pass2:
                        tile_in(4 * h + 0, lambda half, pst, pk: CP("act", q32[:, half * 512:(half + 1) * 512],
                                                                   pst[:, :], [pk], ["q32"]))
                        tile_in(4 * h + 3, lambda half, pst, pk: ACT(gs[:, half * 512:(half + 1) * 512], pst[:, :],
                                                                    AF.Silu, [pk], ["gs"]))
                        TT("dve", negr[:], bb3[:, :, CH - 1], bb3[:, :, CH // 2 - 1], ALU.subtract, ["bb"], ["negr"])
                        TT("dve", lg3[:, :, 0], lg3[:, :, 0], negr[:], ALU.add, ["lg", "negr"], ["lg"])
                        p.op("dve", lambda e: e.tensor_tensor_scan(out=ff[:], data0=rmask[:], data1=lg[:],
                                                                   initial=0.0, op0=ALU.mult, op1=ALU.add),
                             ["rmask", "lg"], ["ff"])
                        ACT(lg[:], ff[:], AF.Exp, ["ff"], ["lg"])
                        TT("dve", qtT[:], q32[:], lg[:], ALU.mult, ["q32", "lg"], ["qtT"])
                        ACT(lg[:], ff[:], AF.Exp, ["ff"], ["lg"], scale=-1.0)
                        TT("dve", ktT[:], sg[:], lg[:], ALU.mult, ["sg", "lg"], ["ktT"])
                        ACT(ff[:], bb[:], AF.Exp, ["bb"], ["ff"])
                        TT("dve", qhT[:], q32[:], ff[:], ALU.mult, ["q32", "ff"], ["qhT"])
                        CP("dve", S[:], sin_[:, h, :], ["sin"], ["S"])
                        CP("act", Sbf[1][:], sin_[:, h, :], ["sin"], [("Sbf", 1)])
                    else:
                        MEMSET("dve", S[:], 0.0, ["S"])
                    if CUT == "L1c":
                        p.muted = True
                    for g in range(2):
                        bk, keyk = banks_k[g]
                        for j in range(8):
                            c = g * 8 + j
                            TR(bk[0:CH, j, :], khT[:, c * CH:(c + 1) * CH], ["khT"], [keyk])
                        CP("act", kt_all[:, g * 8:(g + 1) * 8, :], bk[0:CH, :, :], [keyk], [("kt", g)])
                        bv, keyv = banks_v[g]
                        for j in range(8):
                            c = g * 8 + j
                            TR(bv[0:CH, j, :], vT[:, c * CH:(c + 1) * CH], ["vT"], [keyv])
                        CP("dve", vt_all[:, g * 8:(g + 1) * 8, :], bv[0:CH, :, :], [keyv], [("vt", g)])
                    if pass2:
                        for g in range(2):
                            bankA, keyA = (ps[4], PK[4]) if g == 0 else (ps[3], PK[3])
                            for j in range(8):
                                c = g * 8 + j
                                cs = slice(c * CH, (c + 1) * CH)
                                MM(bankA[0:CH, j * CH:(j + 1) * CH], ktT[:, cs], qtT[:, cs], True, True,
                                   ["ktT", "qtT"], [keyA])
                            for j in range(8):
                                c = g * 8 + j
                                TT("dve", Am_all[:, c, :], bankA[0:CH, j * CH:(j + 1) * CH], triu[0:CH, 0:CH],
                                   ALU.mult, [keyA, "cbf"], [("Am", c)])
                    for c in range(NCH):
                        cs = slice(c * CH, (c + 1) * CH)
                        g, j = divmod(c, 8)
                        dsb, dsk = ps[1 + c % 2], PK[1 + c % 2]
                        if pass2:
                            ob, ok = (ps[0], PK[0]) if g == 0 else (ps[5], PK[5])
                            MM(ob[:, j * CH:(j + 1) * CH], vt_all[:, c, :], Am_all[:, c, :], True, False,
                               [("vt", g), ("Am", c)], [ok])
                        MM(dsb[:, 0:128], kt_all[:, c, :], vt_all[:, c, :], True, True, [("kt", g), ("vt", g)], [dsk])
                        if pass2:
                            MM(ob[:, j * CH:(j + 1) * CH], Sbf[(c + 1) % 2][:], qhT[:, cs], False, True,
                               [("Sbf", (c + 1) % 2), "qhT"], [ok])
                        STT(S[:], S[:], ebc[:, c:c + 1], dsb[:, 0:128], ALU.mult, ALU.add, ["S", "ebc", dsk], ["S"])
                        if pass2 and c < NCH - 1:
                            CP("act", Sbf[c % 2][:], S[:], ["S"], [("Sbf", c % 2)])
                    if pass2:
                        CP("act", o32[:, 0:512], ps[0][:, :], [PK[0]], ["o32l"])
                        CP("act", o32[:, 512:1024], ps[5][:, :], [PK[5]], ["o32l"])
                    if CUT == "L1d":
                        p.muted = True
                    if not pass2:
                        CP("dve", sst[:, 0:128], S[:], ["S"], ["sst"])
                        if os.environ.get("MK_NOTAIL"):
                            MEMSET("dve", sst[:, 128:129], 0.0, ["sst"])
                        else:
                            TS("dve", sst[:, 128:129], ebc[:, 0:1], 1.0, None, ALU.mult, None, ["ebc", "sst"], ["sst"])
                            for c in range(1, NCH):
                                TT("dve", sst[:, 128:129], sst[:, 128:129], ebc[:, c:c + 1], ALU.mult,
                                   ["ebc", "sst"], ["sst"])
                        if mode == "F":
                            p.dma("sp", sbp[h // 4][(h % 4) * 128:(h % 4 + 1) * 128, :], sst[:, :],
                                  reads=["sst"], writes=[("sbp", h // 4)])
                            if h % 4 == 3:
                                AG(sbp[h // 4], sgp[h // 4], ("sbp", h // 4), ("sgp", h // 4))
                        else:
                            p.dma("sp", stb_a[h * 128:(h + 1) * 128, :], sst[:, :], reads=["sst"],
                                  writes=["stb"])
                    else:
                        for half in range(2):
                            sl = slice(half * 512, (half + 1) * 512)
                            ACT(sqb[:, :], o32[:, sl], AF.Square, ["o32l"], ["sqb"])
                            MM(ps[0][:, :], ones, sqb[:, :], True, True, ["sqb", "cbf"], [PK[0]])
                            ACT(rs1[:], ps[0][:, :], AF.Sqrt, [PK[0], "epsc"], ["rs1"], scale=1.0 / 128, bias=epsc[:])
                            RECIP(rs1[:], rs1[:], ["rs1"], ["rs1"])
                            STT(rs1[:], o32[:, sl], pc("hg"), rs1[:], ALU.mult, ALU.mult, ["o32l", "prm", "rs1"], ["rs1"])
                            TT("dve", mix[:, h, sl], rs1[:], gs[:, sl], ALU.mult, ["rs1", "gs"], [("mix", h)])
            p.barrier()

        def finale():
            out_proj(w_out1, x1d, 0, x2d)
            p.barrier()

            def sink_o(c, a, n, xb, xk, gname):
                STT(xb[:, 0:n], xb[:, 0:n], pc(gname, c), rstd[:, a:a + n], ALU.mult, ALU.mult,
                    [xk, "rstd", "prm"], [xk])
                p.dma("sp", outT[:, c, a:a + n], xb[:, 0:n], reads=[xk], writes=["outT"])
            rms_stream(x2d, "fin_g", T, 0, sink_o)

        if L0:
            layer0()
        if P1:
            layer1(False)
        if P2:
            layer1(True)
            finale()
        p.muted = False
        p.barrier()
        p.op("sp", lambda e: None, reads=(), writes=["done"])
        p.emit(st)
    return nc


def _relay(tile):
    K, n = tile.shape
    return np.ascontiguousarray(tile.reshape(K // 128, 128, n).transpose(1, 0, 2).reshape(128, (K // 128) * n))


def _cols(v):
    n = v.shape[0] // 128
    return np.ascontiguousarray(v.reshape(n, 128).T)


def prepare(inputs):
    f = lambda k: np.asarray(inputs[k], dtype=np.float32)
    x = f("x")
    W = f("ev_w_in")[0]
    a_v, a_g, a_z = W[:, 0:1024], W[:, 1024:2048], W[:, 2048:3072]
    c_q, c_kv, k_pe, b_z = W[:, 3072:3584], W[:, 3584:4096], W[:, 4096:4160], W[:, 4160:5184]
    zpad = np.zeros((2048, 64), np.float32)
    sw = np.concatenate([np.arange(32, 64), np.arange(0, 32)])
    tiles = [c_kv[:, j * 128:(j + 1) * 128] for j in range(4)]
    tiles += [np.concatenate([k_pe, zpad], 1), np.concatenate([k_pe[:, sw], zpad], 1)]
    for j in range(8):
        sl = slice(j * 128, (j + 1) * 128)
        tiles += [a_v[:, sl], a_g[:, sl], a_z[:, sl]]
    tiles += [b_z[:, j * 128:(j + 1) * 128] for j in range(8)]
    tiles += [c_q[:, j * 128:(j + 1) * 128] for j in range(4)]
    w_in0 = np.stack([_relay(t) for t in tiles])
    uq = f("mla_w_uq")[0]
    w_uq = np.stack([_relay(np.concatenate([uq[:, h, 0:128], uq[:, h, 128:192], uq[:, h, 128:192][:, sw]], 1))
                     for h in range(8)])
    ukv = f("mla_w_ukv")[0]
    w_ukvk = np.stack([_relay(ukv[:, h, 0:128]) for h in range(8)])
    vfull = ukv[:, :, 128:256].reshape(512, 1024)
    w_ukvv = np.stack([_relay(vfull[:, 0:512]), _relay(vfull[:, 512:1024])])
    wo0 = f("ev_w_out")[0]
    w_out0 = np.stack([_relay(wo0[:, m * 128:(m + 1) * 128]) for m in range(16)])
    W1 = f("od_w_in")[0]
    t1 = []
    for h in range(16):
        for part in range(4):
            t1.append(W1[:, part * 2048 + h * 128: part * 2048 + (h + 1) * 128])
    w_in1 = np.stack([_relay(t) for t in t1])
    wo1 = f("od_w_out")[0]
    w_out1 = np.stack([_relay(wo1[:, m * 128:(m + 1) * 128]) for m in range(16)])
    prm = np.zeros((128, NPRM), np.float32)

    def put(name, arr):
        o, w = PRM[name]
        prm[:, o:o + w] = arr
    put("ev_g", _cols(f("ev_norm_g")[0]))
    put("conv_b", _cols(f("conv_b")[0]))
    put("ln_g", _cols(f("conv_ln_g")[0]))
    put("ln_b", _cols(f("conv_ln_b")[0]))
    put("q_g", _cols(f("mla_q_norm_g")[0]))
    put("kv_g", _cols(f("mla_kv_norm_g")[0]))
    put("od_g", _cols(f("od_norm_g")[0]))
    put("fin_g", _cols(f("final_norm_g")))
    put("hg", f("hgrn_norm_g")[0].reshape(128, 1))
    lg = f("hgrn_lb_logits")
    put("l0", _cols(lg[0]))
    put("l1", _cols(lg[1]))
    cw = f("conv_w")[0]
    put("conv_w", cw.reshape(31, 8, 128).transpose(2, 1, 0).reshape(128, 8 * 31))
    cbf = np.zeros((128, 3, 128), np.float32)
    cbf[:, 0, :] = np.eye(128)
    cbf[:, 1, :] = 1.0
    cbf[:, 2, :] = np.triu(np.ones((128, 128)))
    cbf = cbf.astype(ml_dtypes.bfloat16)
    inv_freq = (1.0 / (np.float32(10000.0) ** (np.arange(0, 64, 2, dtype=np.float32) / np.float32(64)))).astype(np.float32)
    shared = dict(cbf=cbf, w_in0=w_in0, w_uq=w_uq, w_ukvk=w_ukvk, w_ukvv=w_ukvv, w_out0=w_out0, w_in1=w_in1,
                  w_out1=w_out1)
    per = []
    for c in range(8):
        b, j = c // 4, c % 4
        s0 = j * T
        xs_ = np.zeros((HALO + T, 2048), np.float32)
        if j > 0:
            xs_[:, :] = x[b, s0 - HALO:s0 + T, :]
        else:
            xs_[HALO:, :] = x[b, 0:T, :]
        xT = np.ascontiguousarray(xs_.reshape(HALO + T, 16, 128).transpose(2, 1, 0))
        pos = np.arange(s0, s0 + T, dtype=np.float32)
        ang = pos[:, None] * inv_freq[None, :]
        cs, sn = np.cos(ang).astype(np.float32).T, np.sin(ang).astype(np.float32).T
        rope = np.stack([np.concatenate([cs, cs], 0), np.concatenate([-sn, sn], 0)], 1).astype(np.float32)
        pr = prm.copy()
        o, w = PRM["sel"]
        pr[:, o + j] = 1.0
        o, w = PRM["vis"]
        for r in range(3):
            pr[:, o + r] = 1.0 if r < j else 0.0
        per.append(dict(xT=xT, prm=pr, rope=np.ascontiguousarray(rope)))
    return shared, per


_NC = {}


def _get(mode, l1=True):
    k = (mode, l1)
    if k not in _NC:
        _NC[k] = build(mode, l1)
    return _NC[k]


def _launch(nc, maps):
    return run_bass_kernel_spmd(nc, maps, core_ids=list(range(8))).results


def _assemble(res, name):
    out = np.zeros((2, 4096, 2048), np.float32)
    for c in range(8):
        b, j = c // 4, c % 4
        o = np.asarray(res[c][name])
        out[b, j * T:(j + 1) * T, :] = o.transpose(2, 1, 0).reshape(T, 2048)
    return out


def run_unfused(inputs, upto="C"):
    shared, per = prepare(inputs)
    mA = [dict(prm=per[c]["prm"], cbf=shared["cbf"], xT=per[c]["xT"], rope=per[c]["rope"],
               w_in0=shared["w_in0"][0:6], w_ukvk=shared["w_ukvk"], w_ukvv=shared["w_ukvv"]) for c in range(8)]
    rA = _launch(_get("A"), mA)
    kvg = [np.concatenate([np.asarray(rA[(c // 4) * 4 + r]["kvb"]) for r in range(4)], 0) for c in range(8)]
    mB = [dict(prm=per[c]["prm"], cbf=shared["cbf"], xT=per[c]["xT"], rope=per[c]["rope"], kvg=kvg[c],
               w_in0=shared["w_in0"], w_ukvk=shared["w_ukvk"], w_ukvv=shared["w_ukvv"], w_uq=shared["w_uq"],
               w_out0=shared["w_out0"]) for c in range(8)]
    rB = _launch(_get("B", False), mB)
    if upto == "B0":
        return _assemble(rB, "x1d")
    x1 = [np.asarray(rB[c]["x1d"]) for c in range(8)]
    mP = [dict(prm=per[c]["prm"], cbf=shared["cbf"], x1d=x1[c], w_in1=shared["w_in1"]) for c in range(8)]
    rP = _launch(_get("P"), mP)
    stg = [np.concatenate([np.asarray(rP[(c // 4) * 4 + r]["stb"]) for r in range(4)], 0) for c in range(8)]
    mC = [dict(prm=per[c]["prm"], cbf=shared["cbf"], x1d=x1[c], stg=stg[c],
               w_in1=shared["w_in1"], w_out1=shared["w_out1"]) for c in range(8)]
    rC = _launch(_get("C"), mC)
    return _assemble(rC, "outT")


def run_fused(inputs):
    shared, per = prepare(inputs)
    maps = [dict(prm=per[c]["prm"], cbf=shared["cbf"], xT=per[c]["xT"], rope=per[c]["rope"], **{
        k: shared[k] for k in ("w_in0", "w_ukvk", "w_ukvv", "w_uq", "w_out0", "w_in1", "w_out1")}) for c in range(8)]
    return _assemble(_launch(_get("F"), maps), "outT")


FUSED = True


def kernel(**inputs):
    if FUSED:
        return run_fused(inputs)
    return run_unfused(inputs)
```
